# Optimizing a Trainium2 kernel written in Bass

```python
import jax, jax.numpy as jnp
from jax import lax
import numpy as np

D_MODEL = 1024
BATCH = 32
SEQ = 2048
DEPTH = 2

PLE_DIM = 256
BLOCK = 128
EPS = 1e-6
NEG = -1e30

SWA_HEADS = 8
SWA_KV_HEADS = 2
SWA_HEAD_DIM = 64
SWA_WINDOW = 128
SWA_WIDTH = SWA_HEADS * SWA_HEAD_DIM
SWA_KV_WIDTH = SWA_KV_HEADS * SWA_HEAD_DIM

MLA_HEADS = 8
MLA_NOPE = 64
MLA_ROPE = 32
MLA_V = 64
MLA_Q_LORA = 256
MLA_KV_LORA = 128
MLA_WIDTH = MLA_HEADS * MLA_V
MLA_QK = MLA_NOPE + MLA_ROPE
ROPE_THETA = 10000.0

IN_SIZES = (SWA_WIDTH, SWA_KV_WIDTH, SWA_KV_WIDTH, SWA_WIDTH,
            MLA_Q_LORA, MLA_KV_LORA, MLA_ROPE, MLA_WIDTH,
            D_MODEL, D_MODEL)
IN_WIDTH = sum(IN_SIZES)

kernel_name = "hybrid_swa_sink_mla_gated_merge"


def rms_norm(x, g):
    xf = x.astype(jnp.float32)
    y = xf * lax.rsqrt(jnp.mean(xf * xf, axis=-1, keepdims=True) + EPS)
    return (y * g.astype(jnp.float32)).astype(x.dtype)


def split_columns(z, sizes):
    idx = []
    acc = 0
    for sz in sizes[:-1]:
        acc += sz
        idx.append(acc)
    return jnp.split(z, idx, axis=-1)


def alibi_slopes(n):
    return jnp.exp2(-8.0 * (jnp.arange(n, dtype=jnp.float32) + 1.0) / n)


def apply_rope(x, pos):
    r = x.shape[-1]
    inv = ROPE_THETA ** (-jnp.arange(0, r, 2, dtype=jnp.float32) / r)
    ang = pos.astype(jnp.float32)[..., None] * inv
    cos = jnp.cos(ang)[:, :, None, :]
    sin = jnp.sin(ang)[:, :, None, :]
    xf = x.astype(jnp.float32)
    x1, x2 = xf[..., : r // 2], xf[..., r // 2:]
    out = jnp.concatenate([x1 * cos - x2 * sin, x2 * cos + x1 * sin], axis=-1)
    return out.astype(x.dtype)


def swa_sink_attention(q, k, v, sink, pos):
    b, s, h, dh = q.shape
    kvh = k.shape[2]
    g = h // kvh
    nb = s // BLOCK
    qb = q.reshape(b, nb, BLOCK, kvh, g, dh)

    def band(t):
        tail = t.shape[2:]
        pad = jnp.zeros((b, BLOCK) + tail, t.dtype)
        prev = jnp.concatenate([pad, t[:, :-BLOCK]], axis=1).reshape((b, nb, BLOCK) + tail)
        cur = t.reshape((b, nb, BLOCK) + tail)
        return jnp.concatenate([prev, cur], axis=2)

    kb, vb, pk = band(k), band(v), band(pos)
    pq = pos.reshape(b, nb, BLOCK)
    scores = jnp.einsum('bnqkgd,bnskd->bnkgqs', qb, kb,
                        preferred_element_type=jnp.float32) * (dh ** -0.5)
    dist = (pq[:, :, :, None] - pk[:, :, None, :]).astype(jnp.float32)
    slopes = alibi_slopes(h).reshape(kvh, g)
    scores = scores - slopes[None, None, :, :, None, None] * dist[:, :, None, None, :, :]
    n_i = jnp.arange(nb)[:, None, None]
    q_i = jnp.arange(BLOCK)[None, :, None]
    k_j = jnp.arange(2 * BLOCK)[None, None, :]
    t_abs = n_i * BLOCK + q_i
    s_abs = n_i * BLOCK - BLOCK + k_j
    valid = (s_abs >= 0) & (s_abs <= t_abs) & (t_abs - s_abs < SWA_WINDOW)
    scores = jnp.where(valid[None, :, None, None, :, :], scores, NEG)
    sink_b = sink.astype(jnp.float32).reshape(kvh, g)[None, None, :, :, None]
    m = jnp.maximum(jnp.max(scores, axis=-1), sink_b)
    e = jnp.exp(scores - m[..., None])
    denom = jnp.sum(e, axis=-1) + jnp.exp(sink_b - m)
    probs = e / denom[..., None]
    out = jnp.einsum('bnkgqs,bnskd->bnqkgd', probs.astype(v.dtype), vb)
    return out.reshape(b, s, h * dh)


def mla_causal_attention(q, k, v):
    b, s, h, dq = q.shape
    nb = s // BLOCK
    qb = q.reshape(b, nb, BLOCK, h, dq).transpose(1, 0, 2, 3, 4)
    kpos = jnp.arange(s)
    scale = dq ** -0.5

    def one_block(args):
        qblk, n = args
        sc = jnp.einsum('bqhd,bshd->bhqs', qblk, k,
                        preferred_element_type=jnp.float32) * scale
        qpos = n * BLOCK + jnp.arange(BLOCK)
        sc = jnp.where(kpos[None, :] <= qpos[:, None], sc, NEG)
        pr = jax.nn.softmax(sc, axis=-1)
        return jnp.einsum('bhqs,bshd->bqhd', pr.astype(v.dtype), v)

    out = lax.map(one_block, (qb, jnp.arange(nb)))
    return out.transpose(1, 0, 2, 3, 4).reshape(b, s, h * v.shape[-1])


def setup_inputs(seed: int = 0) -> dict:
    key = jax.random.key(seed)
    ks = jax.random.split(key, 20)
    f32 = jnp.float32

    def nrm(k, shape, fan_in):
        return jax.random.normal(k, shape, f32) * (fan_in ** -0.5)

    def gain(k, shape):
        return 1.0 + 0.02 * jax.random.normal(k, shape, f32)

    x = jax.random.normal(ks[0], (BATCH, SEQ, D_MODEL), f32)
    p = jax.random.normal(ks[1], (DEPTH, BATCH, SEQ, PLE_DIM), f32)
    positions = jnp.broadcast_to(jnp.arange(SEQ, dtype=jnp.int32)[None, :], (BATCH, SEQ))
    return {
        "x": x,
        "p": p,
        "positions": positions,
        "g_mix": gain(ks[2], (DEPTH, D_MODEL)),
        "w_in": nrm(ks[3], (DEPTH, D_MODEL, IN_WIDTH), D_MODEL),
        "sink": 0.5 * jax.random.normal(ks[4], (DEPTH, SWA_HEADS), f32),
        "g_q": gain(ks[5], (DEPTH, MLA_Q_LORA)),
        "w_uq": nrm(ks[6], (DEPTH, MLA_Q_LORA, MLA_HEADS * MLA_QK), MLA_Q_LORA),
        "g_kv": gain(ks[7], (DEPTH, MLA_KV_LORA)),
        "w_ukv": nrm(ks[8], (DEPTH, MLA_KV_LORA, MLA_HEADS * (MLA_NOPE + MLA_V)), MLA_KV_LORA),
        "w_br_a": nrm(ks[9], (DEPTH, SWA_WIDTH, D_MODEL), SWA_WIDTH),
        "w_br_b": nrm(ks[10], (DEPTH, MLA_WIDTH, D_MODEL), MLA_WIDTH),
        "w_out": nrm(ks[11], (DEPTH, D_MODEL, D_MODEL), D_MODEL),
        "g_ple": gain(ks[12], (DEPTH, D_MODEL)),
        "w_ple_gate": nrm(ks[13], (DEPTH, D_MODEL, D_MODEL), D_MODEL),
        "w_ple_proj": nrm(ks[14], (DEPTH, PLE_DIM, D_MODEL), PLE_DIM),
        "g_final": gain(ks[15], (D_MODEL,)),
    }


def reference(x, p, positions, g_mix, w_in, sink, g_q, w_uq, g_kv, w_ukv,
              w_br_a, w_br_b, w_out, g_ple, w_ple_gate, w_ple_proj, g_final):
    b, s, _ = x.shape
    for i in range(DEPTH):
        h = rms_norm(x, g_mix[i])
        z = h @ w_in[i]
        (a_q, a_k, a_v, a_gate, b_qd, b_kvd, b_kr, b_gate,
         m_a, m_b) = split_columns(z, IN_SIZES)

        qa = a_q.reshape(b, s, SWA_HEADS, SWA_HEAD_DIM)
        ka = a_k.reshape(b, s, SWA_KV_HEADS, SWA_HEAD_DIM)
        va = a_v.reshape(b, s, SWA_KV_HEADS, SWA_HEAD_DIM)
        o_a = swa_sink_attention(qa, ka, va, sink[i], positions) * jax.nn.silu(a_gate)

        qb = (rms_norm(b_qd, g_q[i]) @ w_uq[i]).reshape(b, s, MLA_HEADS, MLA_QK)
        q_nope, q_rope = qb[..., :MLA_NOPE], qb[..., MLA_NOPE:]
        q_rope = apply_rope(q_rope, positions)
        kv = (rms_norm(b_kvd, g_kv[i]) @ w_ukv[i]).reshape(b, s, MLA_HEADS, MLA_NOPE + MLA_V)
        k_nope, vb = kv[..., :MLA_NOPE], kv[..., MLA_NOPE:]
        k_rope = apply_rope(b_kr[:, :, None, :], positions)
        q_full = jnp.concatenate([q_nope, q_rope], axis=-1)
        k_full = jnp.concatenate(
            [k_nope, jnp.broadcast_to(k_rope, (b, s, MLA_HEADS, MLA_ROPE))], axis=-1)
        o_b = mla_causal_attention(q_full, k_full, vb) * jax.nn.silu(b_gate)

        y = jax.nn.sigmoid(m_a) * (o_a @ w_br_a[i]) + jax.nn.sigmoid(m_b) * (o_b @ w_br_b[i])
        x = x + y @ w_out[i]

        pg = jax.nn.sigmoid(rms_norm(x, g_ple[i]) @ w_ple_gate[i])
        x = x + pg * (p[i].astype(x.dtype) @ w_ple_proj[i])
    return rms_norm(x, g_final)
```

```python
import contextlib
import math
import numpy as np
import concourse.bass as bass
import concourse.mybir as mybir
from concourse.bass_utils import run_bass_kernel_spmd

F32 = mybir.dt.float32
BF16 = mybir.dt.bfloat16
I32 = mybir.dt.int32
AF = mybir.ActivationFunctionType
ALU = mybir.AluOpType

NCORES = 8
SPC = 4
S = 2048
D = 1024
NB = 16
NCH = 4
CH = 512
KC = 8
DEPTH = 2
EPS = 1e-6
BIGM = float(2 ** 20)
NEGM = -30000.0
SLOPES = [2.0 ** (-(i + 1)) for i in range(8)]
MLA_SCALE = 96.0 ** -0.5

ENGS = ("pe", "act", "dve", "pool", "sp")


class Op:
    __slots__ = ("eng", "fn", "deps", "sig", "epoch", "is_dma", "dma_key", "idx", "dma_cnt")

    def __init__(self, eng, fn, epoch, is_dma, dma_key):
        self.eng = eng
        self.fn = fn
        self.deps = []
        self.sig = None
        self.epoch = epoch
        self.is_dma = is_dma
        self.dma_key = dma_key
        self.dma_cnt = None


class Prog:
    def __init__(self, nc):
        self.nc = nc
        self.ops = []
        self.last_w = {}
        self.readers = {}
        self.epoch = 0
        self.dma_counts = {}

    def op(self, eng, fn, reads=(), writes=(), dma_key=None):
        is_dma = dma_key is not None
        o = Op(eng, fn, self.epoch, is_dma, dma_key)
        o.idx = len(self.ops)
        deps = {}
        for r in reads:
            w = self.last_w.get(r)
            if w is not None:
                deps[w.idx] = (w, "raw")
        for r in writes:
            w = self.last_w.get(r)
            if w is not None:
                deps.setdefault(w.idx, (w, "waw"))
            rd = self.readers.get(r)
            if rd:
                for x in rd[0].values():
                    deps.setdefault(x.idx, (x, "war"))
                for x in rd[1]:
                    deps.setdefault(x.idx, (x, "war"))
        for w, kind in deps.values():
            if w.eng == eng and not w.is_dma and not is_dma:
                if eng == "pe":
                    continue
                if kind == "war":
                    continue
            o.deps.append(w)
        for r in writes:
            self.last_w[r] = o
            self.readers[r] = ({}, [])
        for r in reads:
            rd = self.readers.get(r)
            if rd is None:
                rd = ({}, [])
                self.readers[r] = rd
            if is_dma:
                rd[1].append(o)
            else:
                rd[0][eng] = o
        if is_dma:
            c = self.dma_counts.get(dma_key, 0) + 1
            self.dma_counts[dma_key] = c
            o.dma_cnt = c
        self.ops.append(o)
        return o

    def emit(self, final_wait_ops=()):
        nc = self.nc
        need = set()
        for o in self.ops:
            for d in o.deps:
                need.add(d.idx)
        counters = {}
        semkeys = set()
        for o in self.ops:
            if o.is_dma:
                semkeys.add(("dma", o.dma_key))
                continue
            if o.idx in need:
                k = (o.eng, o.epoch)
                counters[k] = counters.get(k, 0) + 1
                o.sig = counters[k]
                semkeys.add(k)
        with contextlib.ExitStack() as st:
            sems = {}
            for i, k in enumerate(sorted(semkeys, key=str)):
                sems[k] = st.enter_context(nc.semaphore("s%d" % i))
            block = st.enter_context(nc.Block())
            per_eng = {e: [o for o in self.ops if o.eng == e] for e in ENGS}

            def target(d):
                if d.is_dma:
                    return ("dma", d.dma_key), 16 * d.dma_cnt
                return (d.eng, d.epoch), d.sig

            def run(engobj, ename):
                waited = {}
                for o in per_eng[ename]:
                    req = {}
                    for d in o.deps:
                        k, v = target(d)
                        if req.get(k, 0) < v:
                            req[k] = v
                    for k, v in req.items():
                        if waited.get(k, 0) >= v:
                            continue
                        engobj.wait_ge(sems[k], v)
                        waited[k] = v
                    ins = o.fn(engobj)
                    if o.is_dma:
                        ins.then_inc(sems[("dma", o.dma_key)], 16)
                    elif o.sig is not None:
                        ins.then_inc(sems[(o.eng, o.epoch)], 1)
                if ename == "sp":
                    for o in final_wait_ops:
                        k, v = target(o)
                        engobj.wait_ge(sems[k], v)

            @block.tensor
            def _(e):
                run(e, "pe")

            @block.scalar
            def _(e):
                run(e, "act")

            @block.vector
            def _(e):
                run(e, "dve")

            @block.gpsimd
            def _(e):
                run(e, "pool")

            @block.sync
            def _(e):
                run(e, "sp")


def unit_table():
    units = []

    def add(name, size, subs):
        units.append((name, size, subs))

    for g in range(2):
        add("A%da" % g, 3072, {"qa": 0, "ka": 2048})
        add("A%db" % g, 2560, {"ga": 0, "va": 2048})
    add("B1a", 2048, {"qd": 0})
    add("B1b", 1536, {"kvd": 0, "kr": 1024})
    add("B2a", 2048, {"gb": 0})
    add("B2b", 2048, {"gb": 0})
    for pr in range(4):
        add("MP%d" % pr, 1024, {"uq": 0, "uk": 768, "uv": 896})
    for gy in range(8):
        add("Y%d" % gy, 3072, {"ma": 0, "mb": 1024, "wa": 2048, "wb": 2560})
    for u in range(4):
        add("O%d" % u, 2048, {"wo": 0})
    for u in range(4):
        add("G%d" % u, 2560, {"pg": 0})
    offs = {}
    o = 0
    for name, size, subs in units:
        offs[name] = (o, size, subs)
        o += size
    return units, offs, o


UNITS, UOFF, WTOTAL = unit_table()
_YO = ["Y%d" % i for i in range(8)] + ["O%d" % i for i in range(4)]
_G = ["G%d" % i for i in range(4)]
MAIN_ORDER = ["A0a", "A0b"] * 4 + ["A1a", "A1b"] * 4 + ["B1a", "B1b", "B2a", "B2b"] * 4 + _YO * 4 + _G * 4
MP_ORDER = ["MP%d" % i for i in range(4)]
CONV_GROUPS = [("A0a", "A1b"), ("B1a", "MP3"), ("Y0", "Y3"), ("Y4", "Y7"), ("O0", "O3"), ("G0", "G3")]
WSLOT = 3072
NWSLOT = 3
MPSLOT = 1024


def prep_weights(inp):
    out = np.zeros((DEPTH, 128, WTOTAL), np.float32)

    def put(arr, off, W, cols, kcn):
        M = len(cols)
        for kc in range(kcn):
            arr[:, off + kc * M: off + (kc + 1) * M] = W[kc * 128:(kc + 1) * 128][:, cols]

    r = np.arange
    for l in range(DEPTH):
        a = out[l]
        win = np.asarray(inp["w_in"][l], np.float32)
        wuq = np.asarray(inp["w_uq"][l], np.float32)
        wukv = np.asarray(inp["w_ukv"][l], np.float32)
        wa = np.asarray(inp["w_br_a"][l], np.float32)
        wb = np.asarray(inp["w_br_b"][l], np.float32)
        wo = np.asarray(inp["w_out"][l], np.float32)
        wpg = np.asarray(inp["w_ple_gate"][l], np.float32)
        wpp = np.asarray(inp["w_ple_proj"][l], np.float32)
        for g in range(2):
            o, _, sub = UOFF["A%da" % g]
            for pr in range(2):
                put(a, o + sub["qa"] + pr * 1024, win, r(0, 128) + (4 * g + 2 * pr) * 64, 8)
            kcols = np.concatenate([r(0, 64), r(0, 64)]) + 512 + g * 64
            put(a, o + sub["ka"], win, kcols, 8)
            o, _, sub = UOFF["A%db" % g]
            for pr in range(2):
                put(a, o + sub["ga"] + pr * 1024, win, r(0, 128) + 768 + (4 * g + 2 * pr) * 64, 8)
            put(a, o + sub["va"], win, r(0, 64) + 640 + g * 64, 8)
        o, _, sub = UOFF["B1a"]
        for gq in range(2):
            put(a, o + gq * 1024, win, r(0, 128) + 1280 + gq * 128, 8)
        o, _, sub = UOFF["B1b"]
        put(a, o + sub["kvd"], win, r(0, 128) + 1536, 8)
        krc = np.concatenate([r(0, 32), r(16, 32), r(0, 16)]) + 1664
        put(a, o + sub["kr"], win, krc, 8)
        for i, nm in enumerate(["B2a", "B2b"]):
            o, _, sub = UOFF[nm]
            for t in range(2):
                put(a, o + t * 1024, win, r(0, 128) + 1696 + (2 * i + t) * 128, 8)
        for pr in range(4):
            o, _, sub = UOFF["MP%d" % pr]
            for kc in range(2):
                for hh in range(2):
                    h = 2 * pr + hh
                    base = o + sub["uq"] + ((kc * 2 + hh) * 2) * 96
                    a[:, base: base + 96] = wuq[kc * 128:(kc + 1) * 128, h * 96: h * 96 + 96]
                    base2 = base + 96
                    a[:, base2 + 64: base2 + 80] = wuq[kc * 128:(kc + 1) * 128, h * 96 + 80: h * 96 + 96]
                    a[:, base2 + 80: base2 + 96] = wuq[kc * 128:(kc + 1) * 128, h * 96 + 64: h * 96 + 80]
            for hh in range(2):
                h = 2 * pr + hh
                a[:, o + sub["uk"] + hh * 64: o + sub["uk"] + hh * 64 + 64] = wukv[:, h * 128: h * 128 + 64]
                a[:, o + sub["uv"] + hh * 64: o + sub["uv"] + hh * 64 + 64] = wukv[:, h * 128 + 64: h * 128 + 128]
        for gy in range(8):
            o, _, sub = UOFF["Y%d" % gy]
            put(a, o + sub["ma"], win, r(0, 128) + 2208 + gy * 128, 8)
            put(a, o + sub["mb"], win, r(0, 128) + 3232 + gy * 128, 8)
            put(a, o + sub["wa"], wa, r(0, 128) + gy * 128, 4)
            put(a, o + sub["wb"], wb, r(0, 128) + gy * 128, 4)
        for u in range(4):
            o, _, sub = UOFF["O%d" % u]
            for gl in range(2):
                put(a, o + gl * 1024, wo, r(0, 128) + (2 * u + gl) * 128, 8)
            o, _, sub = UOFF["G%d" % u]
            for gl in range(2):
                put(a, o + gl * 1280, wpg, r(0, 128) + (2 * u + gl) * 128, 8)
                put(a, o + gl * 1280 + 1024, wpp, r(0, 128) + (2 * u + gl) * 128, 2)
    return out


def host_consts():
    c = {}
    c["identf"] = np.eye(128, dtype=np.float32)
    s = np.arange(128)[:, None]
    q = np.arange(128)[None, :]
    c["maskneg"] = np.where(s <= q, 0.0, NEGM).astype(np.float32)
    mb = np.zeros((128, 2, 128), np.float32)
    mb[:, 0, :] = np.where(s > q, 0.0, BIGM)
    mb[:, 1, :] = np.where(s <= q, 0.0, BIGM)
    c["maskbig"] = mb
    sid = np.zeros((128, 8, 128), np.float32)
    for h in range(8):
        sid[:, h, :] = -8.0 * SLOPES[h] * np.eye(128, dtype=np.float32)
    c["sid"] = sid
    p = np.arange(128)
    inv = (10000.0 ** (-(np.arange(0, 32, 2, dtype=np.float32)) / 32.0)).astype(np.float32)
    sgn = np.where((p % 32) < 16, -1.0, 1.0).astype(np.float32)
    col = np.zeros((128, 8), np.float32)
    col[:, 0] = inv[p % 16]
    col[:, 1] = 2.0 * math.pi * sgn
    col[:, 4] = 2.0 * math.pi
    col[:, 5] = (inv[p % 16].astype(np.float64) / (2.0 * math.pi)).astype(np.float32)
    c["ropecol"] = col
    return c


A_X = 0
A_CS = 65536
A_SN = 69632
A_DM = 73728
A_HT = 81920
A_OA = 98304
A_OB = 114688
A_RR = 131072
RR_BYTES = 26624
A_KRF = A_RR + RR_BYTES
A_PTB = A_KRF + 4096
A_SCR = A_PTB + 6144
A_WSL = A_SCR + 16384
A_MPS = A_WSL + NWSLOT * WSLOT * 2
A_END = A_MPS + 2 * MPSLOT * 2
R_QTA = 0
R_KTA = 8192
R_VA = 12288
R_PTA = 18432
R_BIAS = 22528
R_KTB = 0
R_VB = 8192
R_QDN = 14336
R_KVDN = 22528
R_YT = 0
R_PTT = 16384
R_PSTG = 24576
R_XS = 0
R_POSI = 8192
R_POSF = 16384
R_XN = 8192
S_RS = 0
S_DN = 4096
S_2 = 8192


class _Cut(Exception):
    pass


def build_program(nseq=SPC, nlayers=DEPTH, cut_at=None, debug=False):
    nc = bass.Bass("TRN2", target_bir_lowering=False)

    def cut(n):
        if cut_at is not None and cut_at == n:
            raise _Cut()
    P = Prog(nc)

    def din(name, shape, dt=F32):
        return nc.dram_tensor(name, list(shape), dt, kind="ExternalInput")

    x_d = din("x", [SPC, S, D])
    p_d = din("p", [DEPTH, SPC, S, 256])
    pos_d = din("pos", [SPC, S], I32)
    wsrc_d = din("wsrc", [DEPTH, 128, WTOTAL + 16])
    gcol_d = din("gcol", [128, 64])
    sink_d = din("sink", [1, 16])
    identf_d = din("identf", [128, 128])
    maskneg_d = din("maskneg", [128, 128])
    maskbig_d = din("maskbig", [128, 2, 128])
    sid_d = din("sid", [128, 8, 128])
    ropecol_d = din("ropecol", [128, 8])
    out_d = nc.dram_tensor("out", [SPC, S, D], F32, kind="ExternalOutput")
    wscr_d = nc.dram_tensor("wscr", [DEPTH, 128, WTOTAL], BF16, kind="Internal")

    with contextlib.ExitStack() as st:
        def sb(name, shape, dt):
            return st.enter_context(nc.sbuf_tensor("sb_" + name, list(shape), dt))

        identf = sb("identf", [128, 128], F32)
        identb = sb("identb", [128, 128], BF16)
        onesb = sb("onesb", [128, 128], BF16)
        maskneg = sb("maskneg", [128, 128], BF16)
        maskbig = sb("maskbig", [128, 2, 128], F32)
        sidb = sb("sidb", [128, 8, 128], BF16)
        ropecol = sb("ropecol", [128, 8], F32)
        gcol = sb("gcol", [128, 64], F32)
        es = sb("es", [128, 16], F32)
        dummy = sb("dummy", [128, 8], F32)
        pki = sb("pki", [128, 16], I32)
        pkf = sb("pkf", [128, 16], F32)
        ARENA = sb("arena", [128, A_END // 2], BF16)
        A16 = ARENA
        A32 = ARENA[:, :].bitcast(F32).tensor
        AI32 = ARENA[:, :].bitcast(I32).tensor
        F16n = A_END // 2
        F32n = A_END // 4
        PS = st.enter_context(nc.psum_tensor("psall", [128, 4096], F32))

        class _Bank:
            def __init__(self, i):
                self.i = i

            def __getitem__(self, key):
                ps_, cs_ = key
                assert ps_ == slice(None)
                c0 = 0 if cs_.start is None else cs_.start
                c1 = 512 if cs_.stop is None else cs_.stop
                return bass.AP(PS, self.i * 512 + c0, [[4096, 128], [1, c1 - c0]])

        banks = [_Bank(i) for i in range(8)]

        def a16(byte_base, p0, pn, el_off, *dims):
            assert byte_base % 2 == 0
            return bass.AP(A16, p0 * F16n + byte_base // 2 + el_off, [[F16n, pn]] + [list(d_) for d_ in dims])

        def a32(byte_base, p0, pn, el_off, *dims):
            assert byte_base % 4 == 0
            return bass.AP(A32, p0 * F32n + byte_base // 4 + el_off, [[F32n, pn]] + [list(d_) for d_ in dims])

        def ai32(byte_base, p0, pn, el_off, *dims):
            return bass.AP(AI32, p0 * F32n + byte_base // 4 + el_off, [[F32n, pn]] + [list(d_) for d_ in dims])

        def tap(t, p0, pn, off, *dims):
            n = 1
            for d_ in list(t.shape)[1:]:
                n *= int(d_)
            return bass.AP(t, p0 * n + off, [[n, pn]] + [list(d_) for d_ in dims])

        def bkp(b, p0, pn, off, *dims):
            return bass.AP(PS, p0 * 4096 + b * 512 + off, [[4096, pn]] + [list(d_) for d_ in dims])

        def rk(byte_off, nbytes):
            lo = (A_RR + byte_off) // 1024
            hi = (A_RR + byte_off + nbytes - 1) // 1024
            return [("ar", i) for i in range(lo, hi + 1)]

        def sk(byte_off, nbytes):
            lo = (A_SCR + byte_off) // 1024
            hi = (A_SCR + byte_off + nbytes - 1) // 1024
            return [("ar", i) for i in range(lo, hi + 1)]

        def BK(i):
            return ("bank", i)

        def XK(kc, c):
            return ("X", kc, c)

        HTSEL = [0]

        def HK(kc, c):
            return ("HT", HTSEL[0], kc)

        def X_ap(kc, tok0, n):
            return a32(A_X, 0, 128, kc * S + tok0, [1, n])

        def HT_ap(kc, tok0, n):
            return a16(A_HT, 0, 128, HTSEL[0] * 4096 + kc * 512 + (tok0 % 512), [1, n])

        def OA_ap(t, tok0, n):
            return a16(A_OA, 0, 128, t * S + tok0, [1, n])

        def OB_ap(t, tok0, n):
            return a16(A_OB, 0, 128, t * S + tok0, [1, n])

        bank_rr = [0]

        def nbank(pool=None):
            pool = list(range(8)) if pool is None else list(pool)
            b = pool[bank_rr[0] % len(pool)]
            bank_rr[0] += 1
            return b

        evac_rr = [0]

        def evac_copy(out_ap, in_ap, reads, writes, eng=None):
            if eng is None:
                eng = "act" if (evac_rr[0] % 2 == 0) else "dve"
                evac_rr[0] += 1
            if eng == "act":
                P.op("act", lambda e: e.copy(out_ap, in_ap), reads=reads, writes=writes)
            else:
                P.op("dve", lambda e: e.tensor_copy(out_ap, in_ap), reads=reads, writes=writes)

        dbg_count = [0]

        def dbg(name, src_ap, shape, dt, reads):
            if not debug:
                return
            t = nc.dram_tensor("dbg_" + name, list(shape), dt, kind="ExternalOutput")
            P.op("sp", lambda e: e.dma_start(out=t.ap(), in_=src_ap), reads=reads, dma_key=("dbg", name))

        def mm(out_ap, lhsT, rhs, start, stop, reads, writes):
            P.op("pe", lambda e: e.matmul(out_ap, lhsT, rhs, start=start, stop=stop), reads=reads, writes=writes)

        class Stream:
            def __init__(self, base, nslot, slotsz, order, tag):
                self.base = base
                self.nslot = nslot
                self.slotsz = slotsz
                self.tag = tag
                self.seq = []
                self.pos = 0
                self.loc = {}
                self.order = order
                self.cnt = 0

            def plan(self, nseq_, nl):
                for s_ in range(nseq_):
                    for l in range(nl):
                        for nm in self.order:
                            self.seq.append((l, nm))

            def _emit_load(self, idx):
                l, nm = self.seq[idx]
                slot = idx % self.nslot
                off, size, _ = UOFF[nm]
                dst = a16(self.base, 0, 128, slot * self.slotsz, [1, size])
                src = bass.AP(wscr_d, l * 128 * WTOTAL + off, [[WTOTAL, 128], [1, size]])
                P.op("sp", lambda e: e.dma_start(out=dst, in_=src), reads=[("wscr", l, nm)],
                     writes=[(self.tag, slot)], dma_key=(self.tag, slot))
                self.loc[idx] = slot

            def get(self, l, nm, lookahead=2):
                idx = self.cnt
                assert self.seq[idx] == (l, nm), (self.seq[idx], l, nm)
                self.cnt += 1
                upto = min(len(self.seq), idx + 1 + lookahead, idx + self.nslot)
                while self.pos < upto:
                    self._emit_load(self.pos)
                    self.pos += 1
                slot = self.loc[idx]
                return slot, (self.tag, slot)

        WS = Stream(A_WSL, NWSLOT, WSLOT, MAIN_ORDER, "ws")
        MS = Stream(A_MPS, 2, MPSLOT, MP_ORDER, "mp")
        WS.plan(nseq, nlayers)
        MS.plan(nseq, nlayers)

        def wsl(slot, off, n):
            return a16(A_WSL, 0, 128, slot * WSLOT + off, [1, n])

        def mps(slot, off, n):
            return a16(A_MPS, 0, 128, slot * MPSLOT + off, [1, n])

        P.epoch = 0
        P.op("sp", lambda e: e.dma_start(out=identf[:], in_=identf_d.ap()), writes=["identf"], dma_key="c0")
        P.op("pool", lambda e: e.dma_start(out=identb[:], in_=identf_d.ap()), writes=["identb"], dma_key="c1")
        P.op("pool", lambda e: e.dma_start(out=maskneg[:], in_=maskneg_d.ap()), writes=["maskneg"], dma_key="c2")
        P.op("sp", lambda e: e.dma_start(out=maskbig[:], in_=maskbig_d.ap()), writes=["maskbig"], dma_key="c3")
        P.op("pool", lambda e: e.dma_start(out=sidb[:], in_=sid_d.ap()), writes=["sidb"], dma_key="c4")
        P.op("sp", lambda e: e.dma_start(out=ropecol[:], in_=ropecol_d.ap()), writes=["ropecol"], dma_key="c5")
        P.op("sp", lambda e: e.dma_start(out=gcol[:], in_=gcol_d.ap()), writes=["gcol"], dma_key="c6")
        P.op("sp", lambda e: e.dma_start(out=es[:], in_=sink_d.ap().partition_broadcast(128)), writes=["es"], dma_key="c8")
        P.op("pool", lambda e: e.memset(onesb[:], 1.0), writes=["onesb"])
        P.op("act", lambda e: e.activation(out=es[:], in_=es[:], func=AF.Exp), reads=["es"], writes=["es"])
        def emit_conv(l):
            for gi, (ua, ub) in enumerate(CONV_GROUPS):
                o0 = UOFF[ua][0]
                o1 = UOFF[ub][0] + UOFF[ub][1]
                names = [nm for nm, _, _ in UNITS if o0 <= UOFF[nm][0] < o1]
                pos_ = o0
                while pos_ < o1:
                    n = min(4096, o1 - pos_)
                    src = bass.AP(wsrc_d, l * 128 * (WTOTAL + 16) + pos_, [[WTOTAL + 16, 128], [1, n]])
                    dst = bass.AP(wscr_d, l * 128 * WTOTAL + pos_, [[WTOTAL, 128], [1, n]])
                    last = (pos_ + n >= o1)
                    P.op("pool", lambda e, src=src, dst=dst: e.dma_start(out=dst, in_=src),
                         writes=[("wscr", l, nm) for nm in names] if last else [],
                         dma_key=("cv", l, gi))
                    pos_ += n

        emit_conv(0)
        if nlayers > 1 and nseq == 0:
            emit_conv(1)
        out_ops = []
        if debug:
            tw = nc.dram_tensor("dbg_wscr", [128, WTOTAL], BF16, kind="ExternalOutput")
            P.op("sp", lambda e: e.dma_start(out=tw.ap(), in_=wscr_d.ap()[0]),
                 reads=[("wscr", 0, nm) for nm, _, _ in UNITS], dma_key=("dbg", "wscr"))

        def rmsnorm_chunk(c, gbase, dst_fn, dst_keys_fn):
            b = nbank()
            sqk = sk(S_2, 8192)
            sq_all = a16(A_SCR + S_2, 0, 128, 0, [CH, KC], [1, CH])
            P.op("act", lambda e: e.activation(out=sq_all, in_=a32(A_X, 0, 128, c * CH, [S, KC], [1, CH]), func=AF.Square),
                 reads=[XK(k, c) for k in range(KC)], writes=sqk)
            for kc in range(KC):
                mm(banks[b][:, :], onesb[:, :], a16(A_SCR + S_2, 0, 128, kc * CH, [1, CH]), kc == 0, kc == KC - 1,
                   reads=sqk + ["onesb"], writes=[BK(b)])
            rs = a32(A_SCR + S_RS, 0, 128, (c % 2) * CH, [1, CH])
            rkey = sk(S_RS + (c % 2) * 2048, 2048)
            P.op("act", lambda e: e.activation(out=rs, in_=banks[b][:, :], func=AF.Ln, bias=EPS, scale=1.0 / D),
                 reads=[BK(b)], writes=rkey)
            P.op("act", lambda e: e.activation(out=rs, in_=rs, func=AF.Exp, scale=-0.5), reads=rkey, writes=rkey)
            for kc in range(KC):
                dst = dst_fn(kc)
                P.op("dve", lambda e, kc=kc, dst=dst: e.scalar_tensor_tensor(
                    out=dst, in0=X_ap(kc, c * CH, CH), scalar=gcol[:, gbase + kc: gbase + kc + 1],
                    in1=rs, op0=ALU.mult, op1=ALU.mult),
                    reads=[XK(kc, c), "gcol"] + rkey, writes=dst_keys_fn(kc))

        def norm_to_HT(gbase, cq, buf=0):
            prev = HTSEL[0]
            HTSEL[0] = buf
            for c in (cq,):
                rmsnorm_chunk(c, gbase, lambda kc, c=c, d_=None: None, None) if False else None
                dsts = {kc: HT_ap(kc, c * CH, CH) for kc in range(KC)}
                keys = {kc: [HK(kc, c)] for kc in range(KC)}
                rmsnorm_chunk(c, gbase, lambda kc, dsts=dsts: dsts[kc], lambda kc, keys=keys: keys[kc])
            HTSEL[0] = prev

        def seq_setup(s):
            posi_k = rk(R_POSI, 8192)
            posf_k = rk(R_POSF, 8192)
            POSI = ai32(A_RR + R_POSI, 0, 128, 0, [1, S])
            POSF = a32(A_RR + R_POSF, 0, 128, 0, [1, S])
            P.op("sp", lambda e: e.dma_start(out=POSI, in_=pos_d.ap()[s:s + 1, :].partition_broadcast(128)),
                 writes=posi_k, dma_key="pos")
            for n in range(NB):
                src = bass.AP(pos_d, s * S + n * 128, [[1, 128], [1, 1]])
                P.op("sp", lambda e, n=n, src=src: e.dma_start(out=pki[:, n:n + 1], in_=src),
                     writes=["pki"], dma_key="pk")
            P.op("dve", lambda e: e.tensor_copy(POSF, POSI), reads=posi_k, writes=posf_k)
            P.op("dve", lambda e: e.tensor_copy(pkf[:], pki[:]), reads=["pki"], writes=["pkf"])
            r_k = [("OB", t_, c_) for t_ in range(2) for c_ in range(NCH)]
            ri_k = [("OB", t_, c_) for t_ in (2, 3) for c_ in range(NCH)]
            rf_k = sk(S_2, 8192)
            RV = a32(A_OB, 0, 128, 0, [1, S])
            RI = ai32(A_OB + 8192, 0, 128, 0, [1, S])
            RF = a32(A_SCR + S_2, 0, 128, 0, [1, S])
            for shift, dst_base, scol in ((0.0, A_SN, 1), (0.25, A_CS, 4)):
                P.op("dve", lambda e, shift=shift: e.tensor_scalar(out=RV, in0=POSF, scalar1=ropecol[:, 5:6], scalar2=shift,
                                                                   op0=ALU.mult, op1=ALU.add), reads=posf_k + ["ropecol"], writes=r_k)
                P.op("dve", lambda e: e.tensor_copy(RI, RV), reads=r_k, writes=ri_k)
                P.op("dve", lambda e: e.tensor_copy(RF, RI), reads=ri_k, writes=rf_k)
                P.op("dve", lambda e: e.tensor_tensor(out=RV, in0=RV, in1=RF, op=ALU.subtract), reads=r_k + rf_k, writes=r_k)
                P.op("dve", lambda e: e.tensor_scalar(out=RF, in0=RV, scalar1=0.5, scalar2=None, op0=ALU.is_ge), reads=r_k, writes=rf_k)
                P.op("dve", lambda e: e.tensor_tensor(out=RV, in0=RV, in1=RF, op=ALU.subtract), reads=r_k + rf_k, writes=r_k)
                P.op("dve", lambda e: e.tensor_scalar(out=RF, in0=RV, scalar1=-0.5, scalar2=None, op0=ALU.is_lt), reads=r_k, writes=rf_k)
                P.op("dve", lambda e: e.tensor_tensor(out=RV, in0=RV, in1=RF, op=ALU.add), reads=r_k + rf_k, writes=r_k)
                P.op("act", lambda e, dst_base=dst_base, scol=scol: e.activation(
                    out=a16(dst_base, 0, 128, 0, [1, S]), in_=RV, func=AF.Sin, scale=ropecol[:, scol:scol + 1]),
                    reads=r_k + ["ropecol"], writes=["TAB"])
            for blk in range(NB):
                buf = blk % 2
                xk = rk(R_XS + buf * 4096, 4096)
                P.op("sp", lambda e, blk=blk, buf=buf: e.dma_start(
                    out=a32(A_RR + R_XS, 0, 128, buf * 1024, [1, 1024]), in_=x_d.ap()[s, blk * 128:(blk + 1) * 128, :]),
                    writes=xk, dma_key=("xs", buf))
                for half in range(2):
                    b = nbank()
                    for j in range(4):
                        kc = half * 4 + j
                        P.op("pe", lambda e, b=b, j=j, kc=kc, buf=buf: e.transpose(
                            banks[b][:, j * 128:(j + 1) * 128], a32(A_RR + R_XS, 0, 128, buf * 1024 + kc * 128, [1, 128]),
                            identf[:, :]), reads=xk + ["identf"], writes=[BK(b)])
                    dst = a32(A_X, 0, 128, half * 4 * S + blk * 128, [S, 4], [1, 128])
                    src = bkp(b, 0, 128, 0, [128, 4], [1, 128])
                    evac_copy(dst, src, reads=[BK(b)], writes=[XK(half * 4 + j, blk // 4) for j in range(4)])
            for n in range(NB):
                for w in range(2):
                    j = n - 1 + w
                    if j < 0:
                        continue
                    tb = (n * 2 + w) % 2
                    tmp = a32(A_SCR + S_DN, 0, 128, tb * CH, [1, 128])
                    tk = sk(S_DN + tb * 2048, 512)
                    P.op("pool", lambda e, n=n, j=j, tmp=tmp: e.tensor_scalar(
                        out=tmp, in0=a32(A_RR + R_POSF, 0, 128, n * 128, [1, 128]), scalar1=pkf[:, j:j + 1], scalar2=None,
                        op0=ALU.subtract), reads=posf_k + ["pkf"], writes=tk)
                    P.op("pool", lambda e, n=n, w=w, tmp=tmp: e.tensor_tensor(
                        out=a16(A_DM, 0, 128, (n * 2 + w) * 128, [1, 128]), in0=tmp, in1=maskbig[:, w, :], op=ALU.add),
                        reads=tk + ["maskbig"], writes=[("DM", n)])

        dn_rr = [0]

        def normalise(be, bo, ncols, gate_fn, out_fn, gate_keys, out_keys, sink_cols=None):
            buf = dn_rr[0] % 2
            dn_rr[0] += 1
            dnk = sk(S_DN + buf * 2048, 2048)

            def dn(p0, pn):
                return a32(A_SCR + S_DN, p0, pn, buf * CH, [1, ncols])
            if sink_cols is None:
                P.op("act", lambda e: e.activation(out=dn(0, 64), in_=bkp(be, 64, 64, 0, [1, ncols]), func=AF.Ln),
                     reads=[BK(be)], writes=dnk)
                P.op("act", lambda e: e.activation(out=dn(64, 64), in_=bkp(bo, 0, 64, 0, [1, ncols]), func=AF.Ln),
                     reads=[BK(bo)], writes=dnk)
            else:
                for k in range(ncols // 128):
                    he, ho = sink_cols[0][k], sink_cols[1][k]
                    P.op("act", lambda e, k=k, he=he: e.activation(
                        out=a32(A_SCR + S_DN, 0, 64, buf * CH + k * 128, [1, 128]), in_=bkp(be, 64, 64, k * 128, [1, 128]),
                        func=AF.Ln, bias=tap(es, 64, 64, he, [1, 1])), reads=[BK(be), "es"], writes=dnk)
                    P.op("act", lambda e, k=k, ho=ho: e.activation(
                        out=a32(A_SCR + S_DN, 64, 64, buf * CH + k * 128, [1, 128]), in_=bkp(bo, 0, 64, k * 128, [1, 128]),
                        func=AF.Ln, bias=tap(es, 0, 64, ho, [1, 1])), reads=[BK(bo), "es"], writes=dnk)
            P.op("act", lambda e: e.activation(out=dn(0, 128), in_=dn(0, 128), func=AF.Exp, scale=-1.0), reads=dnk, writes=dnk)
            P.op("pool", lambda e: e.tensor_tensor(out=dn(0, 128), in0=dn(0, 128), in1=gate_fn(0, 128), op=ALU.mult),
                 reads=dnk + gate_keys, writes=dnk)
            P.op("dve", lambda e: e.tensor_tensor(out=out_fn(0, 64), in0=bkp(be, 0, 64, 0, [1, ncols]), in1=dn(0, 64), op=ALU.mult),
                 reads=[BK(be)] + dnk, writes=out_keys)
            P.op("dve", lambda e: e.tensor_tensor(out=out_fn(64, 64), in0=bkp(bo, 64, 64, 0, [1, ncols]), in1=dn(64, 64), op=ALU.mult),
                 reads=[BK(bo)] + dnk, writes=out_keys)

        def layer(s, l, have_first):
            gmix0 = l * 8
            gple0 = 16 + l * 8
            cut(2)
            def proj_A(g, cq):
                hcs = (cq,)
                sa, ka = WS.get(l, "A%da" % g)
                for pr in range(2):
                    for c in hcs:
                        b = nbank()
                        for kc in range(KC):
                            mm(banks[b][:, :], wsl(sa, (pr * 8 + kc) * 128, 128), HT_ap(kc, c * CH, CH),
                               kc == 0, kc == KC - 1, reads=[ka, HK(kc, c)], writes=[BK(b)])
                        evac_copy(a16(A_RR + R_QTA, 0, 128, pr * S + c * CH, [1, CH]), banks[b][:, :],
                                  reads=[BK(b)], writes=rk(R_QTA + (pr * S + c * CH) * 2, 1024))
                for c in hcs:
                    b = nbank()
                    for kc in range(KC):
                        mm(banks[b][:, :], wsl(sa, 2048 + kc * 128, 128), HT_ap(kc, c * CH, CH),
                           kc == 0, kc == KC - 1, reads=[ka, HK(kc, c)], writes=[BK(b)])
                    evac_copy(a16(A_RR + R_KTA, 0, 128, c * CH, [1, CH]), banks[b][:, :],
                              reads=[BK(b)], writes=rk(R_KTA + c * CH * 2, 1024))
                sbb, kb = WS.get(l, "A%db" % g)
                for pr in range(2):
                    for c in hcs:
                        b = nbank()
                        for kc in range(KC):
                            mm(banks[b][:, :], wsl(sbb, (pr * 8 + kc) * 128, 128), HT_ap(kc, c * CH, CH),
                               kc == 0, kc == KC - 1, reads=[kb, HK(kc, c)], writes=[BK(b)])
                        t = 2 * g + pr
                        P.op("act", lambda e, b=b, t=t, c=c: e.activation(out=OA_ap(t, c * CH, CH), in_=banks[b][:, :], func=AF.Silu),
                             reads=[BK(b)], writes=[("OA", t, c)])
                if cq == 0:
                    vak = rk(R_VA, NB * 192 * 2)
                    P.op("pool", lambda e: e.memset(a16(A_RR + R_VA, 0, 128, 0, [192, NB], [1, 64]), 1.0), writes=vak)
                    P.op("pool", lambda e: e.memset(a16(A_RR + R_VA, 0, 128, 128, [192, NB], [1, 64]), 1.0), writes=vak)
                for q4 in hcs:
                    b = nbank()
                    for j in range(4):
                        blk = q4 * 4 + j
                        for kc in range(KC):
                            mm(bkp(b, 0, 128, j * 64, [1, 64]), HT_ap(kc, blk * 128, 128),
                               wsl(sbb, 2048 + kc * 64, 64), kc == 0, kc == KC - 1,
                               reads=[kb, HK(kc, q4)], writes=[BK(b)])
                    evac_copy(a16(A_RR + R_VA, 0, 128, q4 * 4 * 192 + 64, [192, 4], [1, 64]),
                              bkp(b, 0, 128, 0, [64, 4], [1, 64]), reads=[BK(b)],
                              writes=rk(R_VA + q4 * 4 * 192 * 2, 4 * 192 * 2))

            def attention_A(g):
                cut(3)

                def swa_scores(n, g=g):
                    buf = n % 2
                    sb0 = 0 if n % 2 == 0 else 2
                    ws_valid = [w for w in range(2) if n - 1 + w >= 0]
                    first = [True, True]
                    for w in ws_valid:
                        j = n - 1 + w
                        for hl in range(4):
                            par = hl % 2
                            jj = hl // 2
                            r0 = par * 64
                            mm(bkp(sb0 + par, 0, 128, (w * 2 + jj) * 128, [1, 128]),
                               a16(A_RR + R_KTA, r0, 64, j * 128, [1, 128]),
                               a16(A_RR + R_QTA, r0, 64, jj * S + n * 128, [1, 128]),
                               first[par], False,
                               reads=rk(R_KTA + j * 256, 256) + rk(R_QTA + (jj * S + n * 128) * 2, 256), writes=[BK(sb0 + par)])
                            first[par] = False
                    for w in ws_valid:
                        for hl in range(4):
                            par = hl % 2
                            jj = hl // 2
                            mm(bkp(sb0 + par, 0, 128, (w * 2 + jj) * 128, [1, 128]), tap(sidb, 0, 128, (4 * g + hl) * 128, [1, 128]),
                               a16(A_DM, 0, 128, (n * 2 + w) * 128, [1, 128]),
                               False, (w == ws_valid[-1] and jj == 1), reads=["sidb", ("DM", n)], writes=[BK(sb0 + par)])
                    w0 = ws_valid[0]
                    nw = len(ws_valid)
                    P.op("act", lambda e: e.activation(
                        out=a16(A_RR + R_PTA + buf * 2048, 0, 128, w0 * 256, [512, 2], [1, nw * 256]),
                        in_=bkp(sb0, 0, 128, w0 * 256, [512, 2], [1, nw * 256]), func=AF.Exp, scale=0.125),
                        reads=[BK(sb0), BK(sb0 + 1)], writes=rk(R_PTA + buf * 2048, 2048))

                def swa_pv(n, g=g):
                    buf = n % 2
                    ws_valid = [w for w in range(2) if n - 1 + w >= 0]
                    be = 4 + (n % 2) * 2
                    bo = be + 1
                    for par, bacc in ((0, be), (1, bo)):
                        for w in ws_valid:
                            j = n - 1 + w
                            lhs = a16(A_RR + R_VA, 0, 128, j * 192 + (64 if par == 0 else 0), [1, 128])
                            rhs = a16(A_RR + R_PTA + buf * 2048, 0, 128, par * 512 + w * 256, [1, 256])
                            mm(bkp(bacc, 0, 128, 0, [1, 256]), lhs, rhs, w == ws_valid[0], w == ws_valid[-1],
                               reads=rk(R_VA + j * 384, 384) + rk(R_PTA + buf * 2048, 2048), writes=[BK(bacc)])
                    c = n // 4
                    oak = [("OA", 2 * g, c), ("OA", 2 * g + 1, c)]
                    normalise(be, bo, 256,
                              lambda p0, pn, n=n, g=g: a16(A_OA, p0, pn, 2 * g * S + n * 128, [S, 2], [1, 128]),
                              lambda p0, pn, n=n, g=g: a16(A_OA, p0, pn, 2 * g * S + n * 128, [S, 2], [1, 128]),
                              oak, oak,
                              sink_cols=([l * 8 + 4 * g + 0, l * 8 + 4 * g + 2], [l * 8 + 4 * g + 1, l * 8 + 4 * g + 3]))

                swa_scores(0)
                for n in range(NB):
                    if n + 1 < NB:
                        swa_scores(n + 1)
                    swa_pv(n)

            def proj_B(cq):
                hcs = (cq,)
                s1a, k1a = WS.get(l, "B1a")
                s1b, k1b = WS.get(l, "B1b", lookahead=1)
                gq0 = 40 + l * 2
                gkv0 = 44 + l
                for c in hcs:
                    qb = []
                    for gq in range(2):
                        b = nbank()
                        qb.append(b)
                        for kc in range(KC):
                            mm(banks[b][:, :], wsl(s1a, (gq * 8 + kc) * 128, 128), HT_ap(kc, c * CH, CH),
                               kc == 0, kc == KC - 1, reads=[k1a, HK(kc, c)], writes=[BK(b)])
                        P.op("act", lambda e, b=b, gq=gq: e.activation(out=a16(A_SCR + S_2, 0, 128, gq * CH, [1, CH]),
                                                                       in_=banks[b][:, :], func=AF.Square),
                             reads=[BK(b)], writes=sk(S_2 + gq * 1024, 1024))
                    bkv = nbank()
                    for kc in range(KC):
                        mm(banks[bkv][:, :], wsl(s1b, kc * 128, 128), HT_ap(kc, c * CH, CH),
                           kc == 0, kc == KC - 1, reads=[k1b, HK(kc, c)], writes=[BK(bkv)])
                    P.op("act", lambda e, b=bkv: e.activation(out=a16(A_SCR + S_2, 0, 128, 2 * CH, [1, CH]),
                                                              in_=banks[b][:, :], func=AF.Square),
                         reads=[BK(bkv)], writes=sk(S_2 + 2048, 1024))
                    bs1 = nbank()
                    for gq in range(2):
                        mm(banks[bs1][:, :], onesb[:, :], a16(A_SCR + S_2, 0, 128, gq * CH, [1, CH]), gq == 0, gq == 1,
                           reads=sk(S_2 + gq * 1024, 1024) + ["onesb"], writes=[BK(bs1)])
                    bs2 = nbank()
                    mm(banks[bs2][:, :], onesb[:, :], a16(A_SCR + S_2, 0, 128, 2 * CH, [1, CH]), True, True,
                       reads=sk(S_2 + 2048, 1024) + ["onesb"], writes=[BK(bs2)])
                    rsq = a32(A_SCR + S_2 + 4096, 0, 128, 0, [1, CH])
                    rsk = a32(A_SCR + S_2 + 6144, 0, 128, 0, [1, CH])
                    rsqk = sk(S_2 + 4096, 2048)
                    rskk = sk(S_2 + 6144, 2048)
                    P.op("act", lambda e, b=bs1: e.activation(out=rsq, in_=banks[b][:, :], func=AF.Ln, bias=EPS, scale=1.0 / 256),
                         reads=[BK(bs1)], writes=rsqk)
                    P.op("act", lambda e: e.activation(out=rsq, in_=rsq, func=AF.Exp, scale=-0.5), reads=rsqk, writes=rsqk)
                    P.op("act", lambda e, b=bs2: e.activation(out=rsk, in_=banks[b][:, :], func=AF.Ln, bias=EPS, scale=1.0 / 128),
                         reads=[BK(bs2)], writes=rskk)
                    P.op("act", lambda e: e.activation(out=rsk, in_=rsk, func=AF.Exp, scale=-0.5), reads=rskk, writes=rskk)
                    for gq in range(2):
                        P.op("dve", lambda e, gq=gq, b=qb[gq], c=c: e.scalar_tensor_tensor(
                            out=a16(A_RR + R_QDN, 0, 128, gq * S + c * CH, [1, CH]), in0=banks[b][:, :],
                            scalar=gcol[:, gq0 + gq: gq0 + gq + 1], in1=rsq, op0=ALU.mult, op1=ALU.mult),
                            reads=[BK(qb[gq]), "gcol"] + rsqk, writes=rk(R_QDN + (gq * S + c * CH) * 2, 1024))
                    P.op("dve", lambda e, b=bkv, c=c: e.scalar_tensor_tensor(
                        out=a16(A_RR + R_KVDN, 0, 128, c * CH, [1, CH]), in0=banks[b][:, :],
                        scalar=gcol[:, gkv0: gkv0 + 1], in1=rsk, op0=ALU.mult, op1=ALU.mult),
                        reads=[BK(bkv), "gcol"] + rskk, writes=rk(R_KVDN + c * CH * 2, 1024))
                    bka = nbank()
                    bkb = nbank()
                    for var, b in ((0, bka), (1, bkb)):
                        for kc in range(KC):
                            mm(bkp(b, 0, 32, 0, [1, CH]), wsl(s1b, 1024 + kc * 64 + var * 32, 32),
                               HT_ap(kc, c * CH, CH), kc == 0, kc == KC - 1,
                               reads=[k1b, HK(kc, c)], writes=[BK(b)])
                    t1 = a32(A_SCR + S_DN, 0, 32, 0, [1, CH])
                    t2 = a32(A_SCR + S_DN, 0, 32, CH, [1, CH])
                    t1k = sk(S_DN, 2048)
                    t2k = sk(S_DN + 2048, 2048)
                    P.op("dve", lambda e, b=bka, c=c: e.tensor_tensor(out=t1, in0=bkp(b, 0, 32, 0, [1, CH]),
                                                                     in1=a16(A_CS, 0, 32, c * CH, [1, CH]), op=ALU.mult),
                         reads=[BK(bka), "TAB"], writes=t1k)
                    P.op("dve", lambda e, b=bkb, c=c: e.tensor_tensor(out=t2, in0=bkp(b, 0, 32, 0, [1, CH]),
                                                                     in1=a16(A_SN, 0, 32, c * CH, [1, CH]), op=ALU.mult),
                         reads=[BK(bkb), "TAB"], writes=t2k)
                    P.op("pool", lambda e, c=c: e.tensor_tensor(out=a16(A_KRF, 0, 32, c * CH, [1, CH]), in0=t1, in1=t2, op=ALU.add),
                         reads=t1k + t2k, writes=[("KRF", c)])
                for i, nm in enumerate(["B2a", "B2b"]):
                    s2, k2 = WS.get(l, nm)
                    for tt in range(2):
                        t = 2 * i + tt
                        for c in hcs:
                            b = nbank()
                            for kc in range(KC):
                                mm(banks[b][:, :], wsl(s2, (tt * 8 + kc) * 128, 128), HT_ap(kc, c * CH, CH),
                                   kc == 0, kc == KC - 1, reads=[k2, HK(kc, c)], writes=[BK(b)])
                            P.op("act", lambda e, b=b, t=t, c=c: e.activation(out=OB_ap(t, c * CH, CH), in_=banks[b][:, :], func=AF.Silu),
                                 reads=[BK(b)], writes=[("OB", t, c)])

            def mla_pairs():
                if s == 0 and l == 0 and nlayers > 1:
                    emit_conv(1)
                P.op("pool", lambda e: e.memset(a16(A_RR + R_VB, 0, 128, 64, [192, NB], [1, 64]), 1.0),
                     writes=rk(R_VB, NB * 192 * 2))
                if s == 0 and l == 0:
                    dbg("QDN", a16(A_RR + R_QDN, 0, 128, 0, [S, 2], [1, S]), [128, 2, S], BF16, rk(R_QDN, 8192))
                    dbg("KVDN", a16(A_RR + R_KVDN, 0, 128, 0, [1, S]), [128, S], BF16, rk(R_KVDN, 4096))
                    dbg("KRF", a16(A_KRF, 0, 128, 0, [1, S]), [128, S], BF16, [("KRF", c_) for c_ in range(NCH)])
                    dbg("GB", a16(A_OB, 0, 128, 0, [S, 4], [1, S]), [128, 4, S], BF16, [("OB", t_, c_) for t_ in range(4) for c_ in range(NCH)])
                cut(5)
                for pair in range(4):
                    sm, km = MS.get(l, "MP%d" % pair, lookahead=1)
                    for c in range(NCH):
                        for hh in range(2):
                            b = nbank(range(4))
                            mm(bkp(b, 0, 64, 0, [1, CH]), mps(sm, 768 + hh * 64, 64), a16(A_RR + R_KVDN, 0, 128, c * CH, [1, CH]),
                               True, True, reads=[km] + rk(R_KVDN + c * CH * 2, 1024), writes=[BK(b)])
                            evac_copy(a16(A_RR + R_KTB, 0, 64, hh * S + c * CH, [1, CH]), bkp(b, 0, 64, 0, [1, CH]),
                                      reads=[BK(b)], writes=rk(R_KTB + (hh * S + c * CH) * 2, 1024))
                    P.op("sp", lambda e: e.dma_start(out=a16(A_RR + R_KTB, 64, 32, 0, [S, 2], [1, S]),
                                                     in_=a16(A_KRF, 0, 32, 0, [0, 2], [1, S])),
                         reads=[("KRF", c) for c in range(NCH)], writes=rk(R_KTB, 8192), dma_key="krep")
                    for q4 in range(4):
                        b = nbank(range(4))
                        for j in range(4):
                            blk = q4 * 4 + j
                            mm(bkp(b, 0, 128, j * 128, [1, 128]), a16(A_RR + R_KVDN, 0, 128, blk * 128, [1, 128]),
                               mps(sm, 896, 128), True, True, reads=[km] + rk(R_KVDN + blk * 256, 256), writes=[BK(b)])
                        evac_copy(a16(A_RR + R_VB, 0, 128, q4 * 4 * 192, [192, 4], [128, 2], [1, 64]),
                                  bkp(b, 0, 128, 0, [128, 4], [64, 2], [1, 64]),
                                  reads=[BK(b)], writes=rk(R_VB + q4 * 4 * 384, 4 * 384))

                    def emit_q(c, pair=pair, sm=sm, km=km):
                        qbuf = c % 2
                        for hh in range(2):
                            bp = nbank(range(4))
                            bs_ = nbank(range(4))
                            for var, b in ((0, bp), (1, bs_)):
                                for kc in range(2):
                                    mm(bkp(b, 0, 96, 0, [1, CH]), mps(sm, ((kc * 2 + hh) * 2 + var) * 96, 96),
                                       a16(A_RR + R_QDN, 0, 128, kc * S + c * CH, [1, CH]), kc == 0, kc == 1,
                                       reads=[km] + rk(R_QDN + (kc * S + c * CH) * 2, 1024), writes=[BK(b)])
                            qslot = qbuf * 2 + hh
                            qk = [("HT", 1, qslot)]
                            P.op("act", lambda e, bp=bp, qslot=qslot: e.copy(a16(A_HT, 0, 64, 4096 + qslot * CH, [1, CH]), bkp(bp, 0, 64, 0, [1, CH])),
                                 reads=[BK(bp)], writes=qk)
                            t1 = a32(A_SCR + S_2, 64, 32, 0, [1, CH])
                            t2 = a32(A_SCR + S_2, 64, 32, CH, [1, CH])
                            t1k = sk(S_2, 2048)
                            t2k = sk(S_2 + 2048, 2048)
                            P.op("dve", lambda e, bp=bp: e.tensor_tensor(out=t1, in0=bkp(bp, 64, 32, 0, [1, CH]),
                                                                        in1=a16(A_CS, 64, 32, c * CH, [1, CH]), op=ALU.mult),
                                 reads=[BK(bp), "TAB"], writes=t1k)
                            P.op("dve", lambda e, bs_=bs_: e.tensor_tensor(out=t2, in0=bkp(bs_, 64, 32, 0, [1, CH]),
                                                                          in1=a16(A_SN, 64, 32, c * CH, [1, CH]), op=ALU.mult),
                                 reads=[BK(bs_), "TAB"], writes=t2k)
                            P.op("pool", lambda e, qslot=qslot: e.tensor_tensor(out=a16(A_HT, 64, 32, 4096 + qslot * CH, [1, CH]), in0=t1, in1=t2, op=ALU.add),
                                 reads=t1k + t2k, writes=qk)

                    emit_q(0)
                    pt_rr = [0]
                    for c in range(NCH):
                        qbuf = c % 2
                        te = 4 + 2 * (c % 2)
                        to = 5 + 2 * (c % 2)
                        steps = [(j, hh) for j in range(4 * c + 4) for hh in range(2)]
                        state = {}

                        def emit_s(idx, c=c, qbuf=qbuf, steps=steps, state=state):
                            j, hh = steps[idx]
                            i = j - 4 * c
                            col0 = 128 * i if i > 0 else 0
                            diag = i >= 0
                            ncol = CH - col0
                            b = nbank(range(4))
                            pb = pt_rr[0] % 6
                            pt_rr[0] += 1
                            state[idx] = (b, pb, col0, ncol)
                            qslot = qbuf * 2 + hh
                            mm(bkp(b, 0, 128, col0, [1, ncol]), a16(A_RR + R_KTB, 0, 96, hh * S + j * 128, [1, 128]),
                               a16(A_HT, 0, 96, 4096 + qslot * CH + col0, [1, ncol]), True, not diag,
                               reads=rk(R_KTB + (hh * S + j * 128) * 2, 256) + [("HT", 1, qslot)], writes=[BK(b)])
                            if diag:
                                mm(bkp(b, 0, 128, col0, [1, 128]), identb[:, :], maskneg[:, :], False, True,
                                   reads=["identb", "maskneg"], writes=[BK(b)])
                            P.op("act", lambda e: e.activation(out=a16(A_PTB, 0, 128, pb * CH + col0, [1, ncol]),
                                                               in_=bkp(b, 0, 128, col0, [1, ncol]), func=AF.Exp, scale=MLA_SCALE),
                                 reads=[BK(b)], writes=[("PTB", pb)])

                        def emit_pv(idx, c=c, te=te, to=to, steps=steps, state=state):
                            j, hh = steps[idx]
                            b, pb, col0, ncol = state[idx]
                            acc = te if hh == 0 else to
                            lhs = a16(A_RR + R_VB, 0, 128, j * 192 + (0 if hh == 0 else 64), [1, 128])
                            mm(bkp(acc, 0, 128, col0, [1, ncol]), lhs, a16(A_PTB, 0, 128, pb * CH + col0, [1, ncol]),
                               j == 0, j == 4 * c + 3, reads=rk(R_VB + j * 384, 384) + [("PTB", pb)], writes=[BK(acc)])

                        LA = 2
                        for k in range(min(LA, len(steps))):
                            emit_s(k)
                        for idx in range(len(steps)):
                            if idx + LA < len(steps):
                                emit_s(idx + LA)
                            emit_pv(idx)
                            if idx == len(steps) // 2 and c + 1 < NCH:
                                emit_q(c + 1)
                        obk = [("OB", pair, c)]
                        normalise(te, to, CH,
                                  lambda p0, pn, c=c, pair=pair: a16(A_OB, p0, pn, pair * S + c * CH, [1, CH]),
                                  lambda p0, pn, c=c, pair=pair: a16(A_OB, p0, pn, pair * S + c * CH, [1, CH]),
                                  obk, obk)
                if s == 0 and l == 0:
                    dbg("OB", a16(A_OB, 0, 128, 0, [S, 4], [1, S]), [128, 4, S], BF16, [("OB", t_, c_) for t_ in range(4) for c_ in range(NCH)])

            def prep_ptt():
                for blk in range(NB):
                    buf = blk % 2
                    pk_ = rk(R_PSTG + buf * 1024, 1024)
                    P.op("sp", lambda e, blk=blk, buf=buf: e.dma_start(
                        out=a32(A_RR + R_PSTG, 0, 128, buf * 256, [1, 256]), in_=p_d.ap()[l, s, blk * 128:(blk + 1) * 128, :]),
                        writes=pk_, dma_key=("ps", buf))
                    b = nbank()
                    for kc in range(2):
                        P.op("pe", lambda e, b=b, kc=kc, buf=buf: e.transpose(
                            banks[b][:, kc * 128:(kc + 1) * 128], a32(A_RR + R_PSTG, 0, 128, buf * 256 + kc * 128, [1, 128]),
                            identf[:, :]), reads=pk_ + ["identf"], writes=[BK(b)])
                    evac_copy(a16(A_RR + R_PTT, 0, 128, blk * 128, [S, 2], [1, 128]),
                              bkp(b, 0, 128, 0, [128, 2], [1, 128]), reads=[BK(b)],
                              writes=rk(R_PTT + blk * 256, 256) + rk(R_PTT + (S + blk * 128) * 2, 256))

            def proj_YO(cq):
                hcs = (cq,)
                for gy in range(8):
                    sy, ky = WS.get(l, "Y%d" % gy)
                    for c in hcs:
                        bma, bmb, bya, byb = nbank(), nbank(), nbank(), nbank()
                        for kc in range(KC):
                            mm(banks[bma][:, :], wsl(sy, kc * 128, 128), HT_ap(kc, c * CH, CH),
                               kc == 0, kc == KC - 1, reads=[ky, HK(kc, c)], writes=[BK(bma)])
                        for kc in range(KC):
                            mm(banks[bmb][:, :], wsl(sy, 1024 + kc * 128, 128), HT_ap(kc, c * CH, CH),
                               kc == 0, kc == KC - 1, reads=[ky, HK(kc, c)], writes=[BK(bmb)])
                        for kc in range(4):
                            mm(banks[bya][:, :], wsl(sy, 2048 + kc * 128, 128), OA_ap(kc, c * CH, CH),
                               kc == 0, kc == 3, reads=[ky, ("OA", kc, c)], writes=[BK(bya)])
                        for kc in range(4):
                            mm(banks[byb][:, :], wsl(sy, 2560 + kc * 128, 128), OB_ap(kc, c * CH, CH),
                               kc == 0, kc == 3, reads=[ky, ("OB", kc, c)], writes=[BK(byb)])
                        pb = (gy * NCH + c) % 2
                        sa_ = a32(A_SCR + S_2, 0, 128, pb * 1024, [1, CH])
                        sb_ = a32(A_SCR + S_2, 0, 128, pb * 1024 + CH, [1, CH])
                        sak = sk(S_2 + pb * 4096, 2048)
                        sbk = sk(S_2 + pb * 4096 + 2048, 2048)
                        P.op("act", lambda e, b=bma, sa_=sa_: e.activation(out=sa_, in_=banks[b][:, :], func=AF.Sigmoid),
                             reads=[BK(bma)], writes=sak)
                        P.op("act", lambda e, b=bmb, sb_=sb_: e.activation(out=sb_, in_=banks[b][:, :], func=AF.Sigmoid),
                             reads=[BK(bmb)], writes=sbk)
                        P.op("dve", lambda e, b=bya, sa_=sa_: e.tensor_tensor(out=sa_, in0=banks[b][:, :], in1=sa_, op=ALU.mult),
                             reads=[BK(bya)] + sak, writes=sak)
                        P.op("dve", lambda e, b=byb, sb_=sb_: e.tensor_tensor(out=sb_, in0=banks[b][:, :], in1=sb_, op=ALU.mult),
                             reads=[BK(byb)] + sbk, writes=sbk)
                        P.op("dve", lambda e, gy=gy, c=c, sa_=sa_, sb_=sb_: e.tensor_tensor(
                            out=a16(A_RR + R_YT, 0, 128, gy * 512, [1, CH]), in0=sa_, in1=sb_, op=ALU.add),
                            reads=sak + sbk, writes=rk(R_YT + (gy * 512) * 2, 1024))
                for u in range(4):
                    so, ko = WS.get(l, "O%d" % u)
                    for gl in range(2):
                        gx = 2 * u + gl
                        for c in hcs:
                            b = nbank()
                            for kc in range(KC):
                                mm(banks[b][:, :], wsl(so, (gl * 8 + kc) * 128, 128), a16(A_RR + R_YT, 0, 128, kc * 512, [1, CH]),
                                   kc == 0, kc == KC - 1, reads=[ko] + rk(R_YT + (kc * 512) * 2, 1024), writes=[BK(b)])
                            xa = X_ap(gx, c * CH, CH)
                            P.op("dve", lambda e, b=b, xa=xa: e.tensor_tensor(out=xa, in0=xa, in1=banks[b][:, :], op=ALU.add),
                                 reads=[BK(b), XK(gx, c)], writes=[XK(gx, c)])

            def proj_G(cq):
                hcs = (cq,)
                for u in range(4):
                    sg, kg = WS.get(l, "G%d" % u)
                    for gl in range(2):
                        gx = 2 * u + gl
                        for c in hcs:
                            bg, bp_ = nbank(), nbank()
                            for kc in range(KC):
                                mm(banks[bg][:, :], wsl(sg, gl * 1280 + kc * 128, 128), HT_ap(kc, c * CH, CH),
                                   kc == 0, kc == KC - 1, reads=[kg, HK(kc, c)], writes=[BK(bg)])
                            for kc in range(2):
                                mm(banks[bp_][:, :], wsl(sg, gl * 1280 + 1024 + kc * 128, 128),
                                   a16(A_RR + R_PTT, 0, 128, kc * S + c * CH, [1, CH]),
                                   kc == 0, kc == 1, reads=[kg] + rk(R_PTT + (kc * S + c * CH) * 2, 1024), writes=[BK(bp_)])
                            pb = (gx * NCH + c) % 2
                            sg_ = a32(A_SCR + S_2, 0, 128, pb * 1024, [1, CH])
                            sgk = sk(S_2 + pb * 4096, 2048)
                            P.op("act", lambda e, b=bg, sg_=sg_: e.activation(out=sg_, in_=banks[b][:, :], func=AF.Sigmoid),
                                 reads=[BK(bg)], writes=sgk)
                            P.op("dve", lambda e, b=bp_, sg_=sg_: e.tensor_tensor(out=sg_, in0=banks[b][:, :], in1=sg_, op=ALU.mult),
                                 reads=[BK(bp_)] + sgk, writes=sgk)
                            xa = X_ap(gx, c * CH, CH)
                            P.op("dve", lambda e, xa=xa, sg_=sg_: e.tensor_tensor(out=xa, in0=xa, in1=sg_, op=ALU.add),
                                 reads=sgk + [XK(gx, c)], writes=[XK(gx, c)])


            jobs = [("A", g_, c_) for g_ in range(2) for c_ in range(NCH)] + [("B", 0, c_) for c_ in range(NCH)] + \
                   [("Y", 0, c_) for c_ in range(NCH)] + [("G", 0, c_) for c_ in range(NCH)]

            def job_norm(k):
                kind, g_, c_ = jobs[k]
                norm_to_HT(gple0 if kind == "G" else gmix0, c_, buf=k % 2)

            if not have_first:
                job_norm(0)
            for k, (kind, g_, c_) in enumerate(jobs):
                if k + 1 < len(jobs):
                    job_norm(k + 1)
                elif l + 1 < nlayers:
                    norm_to_HT((l + 1) * 8, 0, buf=0)
                HTSEL[0] = k % 2
                if kind == "A":
                    proj_A(g_, c_)
                    if c_ == NCH - 1:
                        HTSEL[0] = 0
                        attention_A(g_)
                elif kind == "B":
                    proj_B(c_)
                    if c_ == NCH - 1:
                        HTSEL[0] = 0
                        mla_pairs()
                elif kind == "Y":
                    if c_ == 0:
                        prep_ptt()
                    proj_YO(c_)
                else:
                    proj_G(c_)
            HTSEL[0] = 0

        def final(s):
            for c in range(NCH):
                rmsnorm_chunk(c, 32,
                              lambda kc: a32(A_RR + R_XN, 0, 128, kc * CH, [1, CH]),
                              lambda kc: rk(R_XN + kc * 2048, 2048))
                for bl in range(4):
                    blk = c * 4 + bl
                    buf = blk % 2
                    xk = rk(R_XS + buf * 4096, 4096)
                    for half in range(2):
                        b = nbank()
                        for j in range(4):
                            kc = half * 4 + j
                            P.op("pe", lambda e, b=b, j=j, kc=kc, bl=bl: e.transpose(
                                banks[b][:, j * 128:(j + 1) * 128], a32(A_RR + R_XN, 0, 128, kc * CH + bl * 128, [1, 128]),
                                identf[:, :]), reads=rk(R_XN + kc * 2048, 2048) + ["identf"], writes=[BK(b)])
                        evac_copy(a32(A_RR + R_XS, 0, 128, buf * 1024 + half * 512, [1, 512]), banks[b][:, :],
                                  reads=[BK(b)], writes=xk)
                    o = P.op("sp", lambda e, blk=blk, buf=buf: e.dma_start(
                        out=out_d.ap()[s, blk * 128:(blk + 1) * 128, :], in_=a32(A_RR + R_XS, 0, 128, buf * 1024, [1, 1024])),
                        reads=xk, dma_key=("out", buf))
                    out_ops.append(o)

        ep = 1
        try:
            cut(0)
            for s in range(nseq):
                P.epoch = ep
                ep += 1
                seq_setup(s)
                if s == 0:
                    dbg("X0", a32(A_X, 0, 128, 0, [S, KC], [1, S]), [128, KC, S], F32, [XK(k_, c_) for k_ in range(KC) for c_ in range(NCH)])
                    dbg("CS", a16(A_CS, 0, 128, 0, [1, S]), [128, S], BF16, ["TAB"])
                    dbg("SN", a16(A_SN, 0, 128, 0, [1, S]), [128, S], BF16, ["TAB"])
                    dbg("DM", a16(A_DM, 0, 128, 0, [1, 4096]), [128, 4096], BF16, [("DM", n_) for n_ in range(NB)])
                cut(1)
                for l in range(nlayers):
                    P.epoch = ep
                    ep += 1
                    layer(s, l, l > 0)
                final(s)
        except _Cut:
            pass
        print("ops:", len(P.ops), {e: sum(1 for o in P.ops if o.eng == e) for e in ENGS})
        lastdma = {}
        for o in P.ops:
            if o.is_dma:
                lastdma[o.dma_key] = o
        P.emit(final_wait_ops=list(lastdma.values()))
    return nc


_CACHE = {}


def kernel(**inputs):
    x = np.ascontiguousarray(np.asarray(inputs["x"], np.float32))
    p = np.ascontiguousarray(np.asarray(inputs["p"], np.float32))
    pos = np.ascontiguousarray(np.asarray(inputs["positions"], np.int32))
    wsrc0 = prep_weights(inputs)
    wsrcs = []
    for c in range(NCORES):
        w_ = np.zeros((DEPTH, 128, WTOTAL + 16), np.float32)
        w_[:, :, :WTOTAL] = wsrc0
        w_[:, :, WTOTAL:] = float(c)
        wsrcs.append(w_)
    consts = host_consts()
    gcol = np.zeros((128, 64), np.float32)
    g_mix = np.asarray(inputs["g_mix"], np.float32)
    g_ple = np.asarray(inputs["g_ple"], np.float32)
    g_fin = np.asarray(inputs["g_final"], np.float32)
    g_q = np.asarray(inputs["g_q"], np.float32)
    g_kv = np.asarray(inputs["g_kv"], np.float32)
    for l in range(DEPTH):
        gcol[:, l * 8:(l + 1) * 8] = g_mix[l].reshape(8, 128).T
        gcol[:, 16 + l * 8:16 + (l + 1) * 8] = g_ple[l].reshape(8, 128).T
        gcol[:, 40 + l * 2:40 + l * 2 + 2] = g_q[l].reshape(2, 128).T
        gcol[:, 44 + l] = g_kv[l]
    gcol[:, 32:40] = g_fin.reshape(8, 128).T
    sink = np.asarray(inputs["sink"], np.float32).reshape(1, 16)
    if "nc" not in _CACHE:
        _CACHE["nc"] = build_program()
    nc = _CACHE["nc"]
    in_maps = []
    for c in range(NCORES):
        m = {
            "x": x[c * SPC:(c + 1) * SPC],
            "p": np.ascontiguousarray(p[:, c * SPC:(c + 1) * SPC]),
            "pos": pos[c * SPC:(c + 1) * SPC],
            "wsrc": wsrcs[c],
            "gcol": gcol,
            "sink": sink,
            "identf": consts["identf"],
            "maskneg": consts["maskneg"],
            "maskbig": consts["maskbig"],
            "sid": consts["sid"],
            "ropecol": consts["ropecol"],
        }
        in_maps.append(m)
    res = run_bass_kernel_spmd(nc, in_maps, core_ids=list(range(NCORES)))
    out = np.concatenate([np.asarray(r["out"], np.float32) for r in res.results], axis=0)
    return out
```

```python
import contextlib
import math
import numpy as np
import concourse.bass as bass
import concourse.mybir as mybir
from concourse.bass_utils import run_bass_kernel_spmd

F32 = mybir.dt.float32
BF16 = mybir.dt.bfloat16
I32 = mybir.dt.int32
AF = mybir.ActivationFunctionType
ALU = mybir.AluOpType

NCORES = 8
SPC = 4
S = 2048
D = 1024
NB = 16
NCH = 4
CH = 512
KC = 8
DEPTH = 2
EPS = 1e-6
BIGM = float(2 ** 20)
NEGM = -30000.0
SLOPES = [2.0 ** (-(i + 1)) for i in range(8)]
MLA_SCALE = 96.0 ** -0.5

ENGS = ("pe", "act", "dve", "pool", "sp")


class Op:
    __slots__ = ("eng", "fn", "deps", "sig", "epoch", "is_dma", "dma_key", "idx", "dma_cnt")

    def __init__(self, eng, fn, epoch, is_dma, dma_key):
        self.eng = eng
        self.fn = fn
        self.deps = []
        self.sig = None
        self.epoch = epoch
        self.is_dma = is_dma
        self.dma_key = dma_key
        self.dma_cnt = None


class Prog:
    def __init__(self, nc):
        self.nc = nc
        self.ops = []
        self.last_w = {}
        self.readers = {}
        self.epoch = 0
        self.dma_counts = {}

    def op(self, eng, fn, reads=(), writes=(), dma_key=None):
        is_dma = dma_key is not None
        o = Op(eng, fn, self.epoch, is_dma, dma_key)
        o.idx = len(self.ops)
        deps = {}
        for r in reads:
            w = self.last_w.get(r)
            if w is not None:
                deps[w.idx] = (w, "raw")
        for r in writes:
            w = self.last_w.get(r)
            if w is not None:
                deps.setdefault(w.idx, (w, "waw"))
            rd = self.readers.get(r)
            if rd:
                for x in rd[0].values():
                    deps.setdefault(x.idx, (x, "war"))
                for x in rd[1]:
                    deps.setdefault(x.idx, (x, "war"))
        for w, kind in deps.values():
            if w.eng == eng and not w.is_dma and not is_dma:
                if eng == "pe":
                    continue
                if kind == "war":
                    continue
            o.deps.append(w)
        for r in writes:
            self.last_w[r] = o
            self.readers[r] = ({}, [])
        for r in reads:
            rd = self.readers.get(r)
            if rd is None:
                rd = ({}, [])
                self.readers[r] = rd
            if is_dma:
                rd[1].append(o)
            else:
                rd[0][eng] = o
        if is_dma:
            c = self.dma_counts.get(dma_key, 0) + 1
            self.dma_counts[dma_key] = c
            o.dma_cnt = c
        self.ops.append(o)
        return o

    def emit(self, final_wait_ops=()):
        nc = self.nc
        need = set()
        for o in self.ops:
            for d in o.deps:
                need.add(d.idx)
        counters = {}
        semkeys = set()
        for o in self.ops:
            if o.is_dma:
                semkeys.add(("dma", o.dma_key))
                continue
            if o.idx in need:
                k = (o.eng, o.epoch)
                counters[k] = counters.get(k, 0) + 1
                o.sig = counters[k]
                semkeys.add(k)
        with contextlib.ExitStack() as st:
            sems = {}
            for i, k in enumerate(sorted(semkeys, key=str)):
                sems[k] = st.enter_context(nc.semaphore("s%d" % i))
            block = st.enter_context(nc.Block())
            per_eng = {e: [o for o in self.ops if o.eng == e] for e in ENGS}

            def target(d):
                if d.is_dma:
                    return ("dma", d.dma_key), 16 * d.dma_cnt
                return (d.eng, d.epoch), d.sig

            def run(engobj, ename):
                waited = {}
                for o in per_eng[ename]:
                    req = {}
                    for d in o.deps:
                        k, v = target(d)
                        if req.get(k, 0) < v:
                            req[k] = v
                    for k, v in req.items():
                        if waited.get(k, 0) >= v:
                            continue
                        engobj.wait_ge(sems[k], v)
                        waited[k] = v
                    ins = o.fn(engobj)
                    if o.is_dma:
                        ins.then_inc(sems[("dma", o.dma_key)], 16)
                    elif o.sig is not None:
                        ins.then_inc(sems[(o.eng, o.epoch)], 1)
                if ename == "sp":
                    for o in final_wait_ops:
                        k, v = target(o)
                        engobj.wait_ge(sems[k], v)

            @block.tensor
            def _(e):
                run(e, "pe")

            @block.scalar
            def _(e):
                run(e, "act")

            @block.vector
            def _(e):
                run(e, "dve")

            @block.gpsimd
            def _(e):
                run(e, "pool")

            @block.sync
            def _(e):
                run(e, "sp")


def unit_table():
    units = []

    def add(name, size, subs):
        units.append((name, size, subs))

    for g in range(2):
        add("A%da" % g, 3072, {"qa": 0, "ka": 2048})
        add("A%db" % g, 2560, {"ga": 0, "va": 2048})
    add("B1a", 2048, {"qd": 0})
    add("B1b", 1536, {"kvd": 0, "kr": 1024})
    add("B2a", 2048, {"gb": 0})
    add("B2b", 2048, {"gb": 0})
    for pr in range(4):
        add("MP%d" % pr, 1024, {"uq": 0, "uk": 768, "uv": 896})
    for gy in range(8):
        add("Y%d" % gy, 3072, {"ma": 0, "mb": 1024, "wa": 2048, "wb": 2560})
    for u in range(4):
        add("O%d" % u, 2048, {"wo": 0})
    for u in range(4):
        add("G%d" % u, 2560, {"pg": 0})
    offs = {}
    o = 0
    for name, size, subs in units:
        offs[name] = (o, size, subs)
        o += size
    return units, offs, o


UNITS, UOFF, WTOTAL = unit_table()
_YO = ["Y%d" % i for i in range(8)] + ["O%d" % i for i in range(4)]
_G = ["G%d" % i for i in range(4)]
MAIN_ORDER = ["A0a", "A0b"] * 4 + ["A1a", "A1b"] * 4 + ["B1a", "B1b", "B2a", "B2b"] * 4 + _YO * 4 + _G * 4
MP_ORDER = ["MP%d" % i for i in range(4)]
CONV_GROUPS = [("A0a", "A1b"), ("B1a", "MP3"), ("Y0", "Y3"), ("Y4", "Y7"), ("O0", "O3"), ("G0", "G3")]
WSLOT = 3072
NWSLOT = 3
MPSLOT = 1024


def prep_weights(inp):
    out = np.zeros((DEPTH, 128, WTOTAL), np.float32)

    def put(arr, off, W, cols, kcn):
        M = len(cols)
        for kc in range(kcn):
            arr[:, off + kc * M: off + (kc + 1) * M] = W[kc * 128:(kc + 1) * 128][:, cols]

    r = np.arange
    for l in range(DEPTH):
        a = out[l]
        win = np.asarray(inp["w_in"][l], np.float32)
        wuq = np.asarray(inp["w_uq"][l], np.float32)
        wukv = np.asarray(inp["w_ukv"][l], np.float32)
        wa = np.asarray(inp["w_br_a"][l], np.float32)
        wb = np.asarray(inp["w_br_b"][l], np.float32)
        wo = np.asarray(inp["w_out"][l], np.float32)
        wpg = np.asarray(inp["w_ple_gate"][l], np.float32)
        wpp = np.asarray(inp["w_ple_proj"][l], np.float32)
        for g in range(2):
            o, _, sub = UOFF["A%da" % g]
            for pr in range(2):
                put(a, o + sub["qa"] + pr * 1024, win, r(0, 128) + (4 * g + 2 * pr) * 64, 8)
            kcols = np.concatenate([r(0, 64), r(0, 64)]) + 512 + g * 64
            put(a, o + sub["ka"], win, kcols, 8)
            o, _, sub = UOFF["A%db" % g]
            for pr in range(2):
                put(a, o + sub["ga"] + pr * 1024, win, r(0, 128) + 768 + (4 * g + 2 * pr) * 64, 8)
            put(a, o + sub["va"], win, r(0, 64) + 640 + g * 64, 8)
        o, _, sub = UOFF["B1a"]
        for gq in range(2):
            put(a, o + gq * 1024, win, r(0, 128) + 1280 + gq * 128, 8)
        o, _, sub = UOFF["B1b"]
        put(a, o + sub["kvd"], win, r(0, 128) + 1536, 8)
        krc = np.concatenate([r(0, 32), r(16, 32), r(0, 16)]) + 1664
        put(a, o + sub["kr"], win, krc, 8)
        for i, nm in enumerate(["B2a", "B2b"]):
            o, _, sub = UOFF[nm]
            for t in range(2):
                put(a, o + t * 1024, win, r(0, 128) + 1696 + (2 * i + t) * 128, 8)
        for pr in range(4):
            o, _, sub = UOFF["MP%d" % pr]
            for kc in range(2):
                for hh in range(2):
                    h = 2 * pr + hh
                    base = o + sub["uq"] + ((kc * 2 + hh) * 2) * 96
                    a[:, base: base + 96] = wuq[kc * 128:(kc + 1) * 128, h * 96: h * 96 + 96]
                    base2 = base + 96
                    a[:, base2 + 64: base2 + 80] = wuq[kc * 128:(kc + 1) * 128, h * 96 + 80: h * 96 + 96]
                    a[:, base2 + 80: base2 + 96] = wuq[kc * 128:(kc + 1) * 128, h * 96 + 64: h * 96 + 80]
            for hh in range(2):
                h = 2 * pr + hh
                a[:, o + sub["uk"] + hh * 64: o + sub["uk"] + hh * 64 + 64] = wukv[:, h * 128: h * 128 + 64]
                a[:, o + sub["uv"] + hh * 64: o + sub["uv"] + hh * 64 + 64] = wukv[:, h * 128 + 64: h * 128 + 128]
        for gy in range(8):
            o, _, sub = UOFF["Y%d" % gy]
            put(a, o + sub["ma"], win, r(0, 128) + 2208 + gy * 128, 8)
            put(a, o + sub["mb"], win, r(0, 128) + 3232 + gy * 128, 8)
            put(a, o + sub["wa"], wa, r(0, 128) + gy * 128, 4)
            put(a, o + sub["wb"], wb, r(0, 128) + gy * 128, 4)
        for u in range(4):
            o, _, sub = UOFF["O%d" % u]
            for gl in range(2):
                put(a, o + gl * 1024, wo, r(0, 128) + (2 * u + gl) * 128, 8)
            o, _, sub = UOFF["G%d" % u]
            for gl in range(2):
                put(a, o + gl * 1280, wpg, r(0, 128) + (2 * u + gl) * 128, 8)
                put(a, o + gl * 1280 + 1024, wpp, r(0, 128) + (2 * u + gl) * 128, 2)
    return out


def host_consts():
    c = {}
    c["identf"] = np.eye(128, dtype=np.float32)
    s = np.arange(128)[:, None]
    q = np.arange(128)[None, :]
    c["maskneg"] = np.where(s <= q, 0.0, NEGM).astype(np.float32)
    mb = np.zeros((128, 2, 128), np.float32)
    mb[:, 0, :] = np.where(s > q, 0.0, BIGM)
    mb[:, 1, :] = np.where(s <= q, 0.0, BIGM)
    c["maskbig"] = mb
    sid = np.zeros((128, 8, 128), np.float32)
    for h in range(8):
        sid[:, h, :] = -8.0 * SLOPES[h] * np.eye(128, dtype=np.float32)
    c["sid"] = sid
    p = np.arange(128)
    inv = (10000.0 ** (-(np.arange(0, 32, 2, dtype=np.float32)) / 32.0)).astype(np.float32)
    sgn = np.where((p % 32) < 16, -1.0, 1.0).astype(np.float32)
    col = np.zeros((128, 8), np.float32)
    col[:, 0] = inv[p % 16]
    col[:, 1] = 2.0 * math.pi * sgn
    col[:, 4] = 2.0 * math.pi
    col[:, 5] = (inv[p % 16].astype(np.float64) / (2.0 * math.pi)).astype(np.float32)
    c["ropecol"] = col
    return c


A_X = 0
A_CS = 65536
A_SN = 69632
A_DM = 73728
A_HT = 81920
A_OA = 98304
A_OB = 114688
A_RR = 131072
RR_BYTES = 26624
A_KRF = A_RR + RR_BYTES
A_PTB = A_KRF + 4096
A_SCR = A_PTB + 6144
A_WSL = A_SCR + 16384
A_MPS = A_WSL + NWSLOT * WSLOT * 2
A_END = A_MPS + 2 * MPSLOT * 2
R_QTA = 0
R_KTA = 8192
R_VA = 12288
R_PTA = 18432
R_BIAS = 22528
R_KTB = 0
R_VB = 8192
R_QDN = 14336
R_KVDN = 22528
R_YT = 0
R_PTT = 16384
R_PSTG = 24576
R_XS = 0
R_POSI = 8192
R_POSF = 16384
R_XN = 8192
S_RS = 0
S_DN = 4096
S_2 = 8192


class _Cut(Exception):
    pass


def build_program(nseq=SPC, nlayers=DEPTH, cut_at=None, debug=False):
    nc = bass.Bass("TRN2", target_bir_lowering=False)

    def cut(n):
        if cut_at is not None and cut_at == n:
            raise _Cut()
    P = Prog(nc)

    def din(name, shape, dt=F32):
        return nc.dram_tensor(name, list(shape), dt, kind="ExternalInput")

    x_d = din("x", [SPC, S, D])
    p_d = din("p", [DEPTH, SPC, S, 256])
    pos_d = din("pos", [SPC, S], I32)
    wsrc_d = din("wsrc", [DEPTH, 128, WTOTAL + 16])
    gcol_d = din("gcol", [128, 64])
    sink_d = din("sink", [1, 16])
    identf_d = din("identf", [128, 128])
    maskneg_d = din("maskneg", [128, 128])
    maskbig_d = din("maskbig", [128, 2, 128])
    sid_d = din("sid", [128, 8, 128])
    ropecol_d = din("ropecol", [128, 8])
    out_d = nc.dram_tensor("out", [SPC, S, D], F32, kind="ExternalOutput")
    wscr_d = nc.dram_tensor("wscr", [DEPTH, 128, WTOTAL], BF16, kind="Internal")

    with contextlib.ExitStack() as st:
        def sb(name, shape, dt):
            return st.enter_context(nc.sbuf_tensor("sb_" + name, list(shape), dt))

        identf = sb("identf", [128, 128], F32)
        identb = sb("identb", [128, 128], BF16)
        onesb = sb("onesb", [128, 128], BF16)
        maskneg = sb("maskneg", [128, 128], BF16)
        maskbig = sb("maskbig", [128, 2, 128], F32)
        sidb = sb("sidb", [128, 8, 128], BF16)
        ropecol = sb("ropecol", [128, 8], F32)
        gcol = sb("gcol", [128, 64], F32)
        es = sb("es", [128, 16], F32)
        dummy = sb("dummy", [128, 8], F32)
        pki = sb("pki", [128, 16], I32)
        pkf = sb("pkf", [128, 16], F32)
        ARENA = sb("arena", [128, A_END // 2], BF16)
        A16 = ARENA
        A32 = ARENA[:, :].bitcast(F32).tensor
        AI32 = ARENA[:, :].bitcast(I32).tensor
        F16n = A_END // 2
        F32n = A_END // 4
        PS = st.enter_context(nc.psum_tensor("psall", [128, 4096], F32))

        class _Bank:
            def __init__(self, i):
                self.i = i

            def __getitem__(self, key):
                ps_, cs_ = key
                assert ps_ == slice(None)
                c0 = 0 if cs_.start is None else cs_.start
                c1 = 512 if cs_.stop is None else cs_.stop
                return bass.AP(PS, self.i * 512 + c0, [[4096, 128], [1, c1 - c0]])

        banks = [_Bank(i) for i in range(8)]

        def a16(byte_base, p0, pn, el_off, *dims):
            assert byte_base % 2 == 0
            return bass.AP(A16, p0 * F16n + byte_base // 2 + el_off, [[F16n, pn]] + [list(d_) for d_ in dims])

        def a32(byte_base, p0, pn, el_off, *dims):
            assert byte_base % 4 == 0
            return bass.AP(A32, p0 * F32n + byte_base // 4 + el_off, [[F32n, pn]] + [list(d_) for d_ in dims])

        def ai32(byte_base, p0, pn, el_off, *dims):
            return bass.AP(AI32, p0 * F32n + byte_base // 4 + el_off, [[F32n, pn]] + [list(d_) for d_ in dims])

        def tap(t, p0, pn, off, *dims):
            n = 1
            for d_ in list(t.shape)[1:]:
                n *= int(d_)
            return bass.AP(t, p0 * n + off, [[n, pn]] + [list(d_) for d_ in dims])

        def bkp(b, p0, pn, off, *dims):
            return bass.AP(PS, p0 * 4096 + b * 512 + off, [[4096, pn]] + [list(d_) for d_ in dims])

        def rk(byte_off, nbytes):
            lo = (A_RR + byte_off) // 1024
            hi = (A_RR + byte_off + nbytes - 1) // 1024
            return [("ar", i) for i in range(lo, hi + 1)]

        def sk(byte_off, nbytes):
            lo = (A_SCR + byte_off) // 1024
            hi = (A_SCR + byte_off + nbytes - 1) // 1024
            return [("ar", i) for i in range(lo, hi + 1)]

        def BK(i):
            return ("bank", i)

        def XK(kc, c):
            return ("X", kc, c)

        HTSEL = [0]

        def HK(kc, c):
            return ("HT", HTSEL[0], kc)

        def X_ap(kc, tok0, n):
            return a32(A_X, 0, 128, kc * S + tok0, [1, n])

        def HT_ap(kc, tok0, n):
            return a16(A_HT, 0, 128, HTSEL[0] * 4096 + kc * 512 + (tok0 % 512), [1, n])

        def OA_ap(t, tok0, n):
            return a16(A_OA, 0, 128, t * S + tok0, [1, n])

        def OB_ap(t, tok0, n):
            return a16(A_OB, 0, 128, t * S + tok0, [1, n])

        bank_rr = [0]

        def nbank(pool=None):
            pool = list(range(8)) if pool is None else list(pool)
            b = pool[bank_rr[0] % len(pool)]
            bank_rr[0] += 1
            return b

        evac_rr = [0]

        def evac_copy(out_ap, in_ap, reads, writes, eng=None):
            if eng is None:
                eng = "act" if (evac_rr[0] % 2 == 0) else "dve"
                evac_rr[0] += 1
            if eng == "act":
                P.op("act", lambda e: e.copy(out_ap, in_ap), reads=reads, writes=writes)
            else:
                P.op("dve", lambda e: e.tensor_copy(out_ap, in_ap), reads=reads, writes=writes)

        dbg_count = [0]

        def dbg(name, src_ap, shape, dt, reads):
            if not debug:
                return
            t = nc.dram_tensor("dbg_" + name, list(shape), dt, kind="ExternalOutput")
            P.op("sp", lambda e: e.dma_start(out=t.ap(), in_=src_ap), reads=reads, dma_key=("dbg", name))

        def mm(out_ap, lhsT, rhs, start, stop, reads, writes):
            P.op("pe", lambda e: e.matmul(out_ap, lhsT, rhs, start=start, stop=stop), reads=reads, writes=writes)

        class Stream:
            def __init__(self, base, nslot, slotsz, order, tag):
                self.base = base
                self.nslot = nslot
                self.slotsz = slotsz
                self.tag = tag
                self.seq = []
                self.pos = 0
                self.loc = {}
                self.order = order
                self.cnt = 0

            def plan(self, nseq_, nl):
                for s_ in range(nseq_):
                    for l in range(nl):
                        for nm in self.order:
                            self.seq.append((l, nm))

            def _emit_load(self, idx):
                l, nm = self.seq[idx]
                slot = idx % self.nslot
                off, size, _ = UOFF[nm]
                dst = a16(self.base, 0, 128, slot * self.slotsz, [1, size])
                src = bass.AP(wscr_d, l * 128 * WTOTAL + off, [[WTOTAL, 128], [1, size]])
                P.op("sp", lambda e: e.dma_start(out=dst, in_=src), reads=[("wscr", l, nm)],
                     writes=[(self.tag, slot)], dma_key=(self.tag, slot))
                self.loc[idx] = slot

            def get(self, l, nm, lookahead=2):
                idx = self.cnt
                assert self.seq[idx] == (l, nm), (self.seq[idx], l, nm)
                self.cnt += 1
                upto = min(len(self.seq), idx + 1 + lookahead, idx + self.nslot)
                while self.pos < upto:
                    self._emit_load(self.pos)
                    self.pos += 1
                slot = self.loc[idx]
                return slot, (self.tag, slot)

        WS = Stream(A_WSL, NWSLOT, WSLOT, MAIN_ORDER, "ws")
        MS = Stream(A_MPS, 2, MPSLOT, MP_ORDER, "mp")
        WS.plan(nseq, nlayers)
        MS.plan(nseq, nlayers)

        def wsl(slot, off, n):
            return a16(A_WSL, 0, 128, slot * WSLOT + off, [1, n])

        def mps(slot, off, n):
            return a16(A_MPS, 0, 128, slot * MPSLOT + off, [1, n])

        P.epoch = 0
        P.op("sp", lambda e: e.dma_start(out=identf[:], in_=identf_d.ap()), writes=["identf"], dma_key="c0")
        P.op("pool", lambda e: e.dma_start(out=identb[:], in_=identf_d.ap()), writes=["identb"], dma_key="c1")
        P.op("pool", lambda e: e.dma_start(out=maskneg[:], in_=maskneg_d.ap()), writes=["maskneg"], dma_key="c2")
        P.op("sp", lambda e: e.dma_start(out=maskbig[:], in_=maskbig_d.ap()), writes=["maskbig"], dma_key="c3")
        P.op("pool", lambda e: e.dma_start(out=sidb[:], in_=sid_d.ap()), writes=["sidb"], dma_key="c4")
        P.op("sp", lambda e: e.dma_start(out=ropecol[:], in_=ropecol_d.ap()), writes=["ropecol"], dma_key="c5")
        P.op("sp", lambda e: e.dma_start(out=gcol[:], in_=gcol_d.ap()), writes=["gcol"], dma_key="c6")
        P.op("sp", lambda e: e.dma_start(out=es[:], in_=sink_d.ap().partition_broadcast(128)), writes=["es"], dma_key="c8")
        P.op("pool", lambda e: e.memset(onesb[:], 1.0), writes=["onesb"])
        P.op("act", lambda e: e.activation(out=es[:], in_=es[:], func=AF.Exp), reads=["es"], writes=["es"])
        def emit_conv(l):
            for gi, (ua, ub) in enumerate(CONV_GROUPS):
                o0 = UOFF[ua][0]
                o1 = UOFF[ub][0] + UOFF[ub][1]
                names = [nm for nm, _, _ in UNITS if o0 <= UOFF[nm][0] < o1]
                pos_ = o0
                while pos_ < o1:
                    n = min(4096, o1 - pos_)
                    src = bass.AP(wsrc_d, l * 128 * (WTOTAL + 16) + pos_, [[WTOTAL + 16, 128], [1, n]])
                    dst = bass.AP(wscr_d, l * 128 * WTOTAL + pos_, [[WTOTAL, 128], [1, n]])
                    last = (pos_ + n >= o1)
                    P.op("pool", lambda e, src=src, dst=dst: e.dma_start(out=dst, in_=src),
                         writes=[("wscr", l, nm) for nm in names] if last else [],
                         dma_key=("cv", l, gi))
                    pos_ += n

        emit_conv(0)
        if nlayers > 1 and nseq == 0:
            emit_conv(1)
        out_ops = []
        if debug:
            tw = nc.dram_tensor("dbg_wscr", [128, WTOTAL], BF16, kind="ExternalOutput")
            P.op("sp", lambda e: e.dma_start(out=tw.ap(), in_=wscr_d.ap()[0]),
                 reads=[("wscr", 0, nm) for nm, _, _ in UNITS], dma_key=("dbg", "wscr"))

        def rmsnorm_chunk(c, gbase, dst_fn, dst_keys_fn):
            b = nbank()
            sqk = sk(S_2, 8192)
            sq_all = a16(A_SCR + S_2, 0, 128, 0, [CH, KC], [1, CH])
            P.op("act", lambda e: e.activation(out=sq_all, in_=a32(A_X, 0, 128, c * CH, [S, KC], [1, CH]), func=AF.Square),
                 reads=[XK(k, c) for k in range(KC)], writes=sqk)
            for kc in range(KC):
                mm(banks[b][:, :], onesb[:, :], a16(A_SCR + S_2, 0, 128, kc * CH, [1, CH]), kc == 0, kc == KC - 1,
                   reads=sqk + ["onesb"], writes=[BK(b)])
            rs = a32(A_SCR + S_RS, 0, 128, (c % 2) * CH, [1, CH])
            rkey = sk(S_RS + (c % 2) * 2048, 2048)
            P.op("act", lambda e: e.activation(out=rs, in_=banks[b][:, :], func=AF.Ln, bias=EPS, scale=1.0 / D),
                 reads=[BK(b)], writes=rkey)
            P.op("act", lambda e: e.activation(out=rs, in_=rs, func=AF.Exp, scale=-0.5), reads=rkey, writes=rkey)
            for kc in range(KC):
                dst = dst_fn(kc)
                P.op("dve", lambda e, kc=kc, dst=dst: e.scalar_tensor_tensor(
                    out=dst, in0=X_ap(kc, c * CH, CH), scalar=gcol[:, gbase + kc: gbase + kc + 1],
                    in1=rs, op0=ALU.mult, op1=ALU.mult),
                    reads=[XK(kc, c), "gcol"] + rkey, writes=dst_keys_fn(kc))

        def norm_A(c):
            sqk = sk(S_2, 8192)
            sq_all = a16(A_SCR + S_2, 0, 128, 0, [CH, KC], [1, CH])
            P.op("act", lambda e: e.activation(out=sq_all, in_=a32(A_X, 0, 128, c * CH, [S, KC], [1, CH]), func=AF.Square),
                 reads=[XK(k, c) for k in range(KC)], writes=sqk)

        def norm_B(c, gbase, buf):
            prev = HTSEL[0]
            HTSEL[0] = buf
            sqk = sk(S_2, 8192)
            b = nbank()
            for kc in range(KC):
                mm(banks[b][:, :], onesb[:, :], a16(A_SCR + S_2, 0, 128, kc * CH, [1, CH]), kc == 0, kc == KC - 1,
                   reads=sqk + ["onesb"], writes=[BK(b)])
            rs = a32(A_SCR + S_RS, 0, 128, (buf % 2) * CH, [1, CH])
            rkey = sk(S_RS + (buf % 2) * 2048, 2048)
            P.op("act", lambda e: e.activation(out=rs, in_=banks[b][:, :], func=AF.Ln, bias=EPS, scale=1.0 / D),
                 reads=[BK(b)], writes=rkey)
            P.op("act", lambda e: e.activation(out=rs, in_=rs, func=AF.Exp, scale=-0.5), reads=rkey, writes=rkey)
            for kc in range(KC):
                dst = HT_ap(kc, c * CH, CH)
                P.op("dve", lambda e, kc=kc, dst=dst: e.scalar_tensor_tensor(
                    out=dst, in0=X_ap(kc, c * CH, CH), scalar=gcol[:, gbase + kc: gbase + kc + 1],
                    in1=rs, op0=ALU.mult, op1=ALU.mult),
                    reads=[XK(kc, c), "gcol"] + rkey, writes=[HK(kc, c)])
            HTSEL[0] = prev

        def norm_to_HT(gbase, cq, buf=0):
            prev = HTSEL[0]
            HTSEL[0] = buf
            for c in (cq,):
                rmsnorm_chunk(c, gbase, lambda kc, c=c, d_=None: None, None) if False else None
                dsts = {kc: HT_ap(kc, c * CH, CH) for kc in range(KC)}
                keys = {kc: [HK(kc, c)] for kc in range(KC)}
                rmsnorm_chunk(c, gbase, lambda kc, dsts=dsts: dsts[kc], lambda kc, keys=keys: keys[kc])
            HTSEL[0] = prev

        def seq_setup(s):
            posi_k = rk(R_POSI, 8192)
            posf_k = rk(R_POSF, 8192)
            POSI = ai32(A_RR + R_POSI, 0, 128, 0, [1, S])
            POSF = a32(A_RR + R_POSF, 0, 128, 0, [1, S])
            P.op("sp", lambda e: e.dma_start(out=POSI, in_=pos_d.ap()[s:s + 1, :].partition_broadcast(128)),
                 writes=posi_k, dma_key="pos")
            for n in range(NB):
                src = bass.AP(pos_d, s * S + n * 128, [[1, 128], [1, 1]])
                P.op("sp", lambda e, n=n, src=src: e.dma_start(out=pki[:, n:n + 1], in_=src),
                     writes=["pki"], dma_key="pk")
            P.op("dve", lambda e: e.tensor_copy(POSF, POSI), reads=posi_k, writes=posf_k)
            P.op("dve", lambda e: e.tensor_copy(pkf[:], pki[:]), reads=["pki"], writes=["pkf"])
            r_k = [("OB", t_, c_) for t_ in range(2) for c_ in range(NCH)]
            ri_k = [("OB", t_, c_) for t_ in (2, 3) for c_ in range(NCH)]
            rf_k = sk(S_2, 8192)
            RV = a32(A_OB, 0, 128, 0, [1, S])
            RI = ai32(A_OB + 8192, 0, 128, 0, [1, S])
            RF = a32(A_SCR + S_2, 0, 128, 0, [1, S])
            for shift, dst_base, scol in ((0.0, A_SN, 1), (0.25, A_CS, 4)):
                P.op("dve", lambda e, shift=shift: e.tensor_scalar(out=RV, in0=POSF, scalar1=ropecol[:, 5:6], scalar2=shift,
                                                                   op0=ALU.mult, op1=ALU.add), reads=posf_k + ["ropecol"], writes=r_k)
                P.op("dve", lambda e: e.tensor_copy(RI, RV), reads=r_k, writes=ri_k)
                P.op("dve", lambda e: e.tensor_copy(RF, RI), reads=ri_k, writes=rf_k)
                P.op("dve", lambda e: e.tensor_tensor(out=RV, in0=RV, in1=RF, op=ALU.subtract), reads=r_k + rf_k, writes=r_k)
                P.op("dve", lambda e: e.tensor_scalar(out=RF, in0=RV, scalar1=0.5, scalar2=None, op0=ALU.is_ge), reads=r_k, writes=rf_k)
                P.op("dve", lambda e: e.tensor_tensor(out=RV, in0=RV, in1=RF, op=ALU.subtract), reads=r_k + rf_k, writes=r_k)
                P.op("dve", lambda e: e.tensor_scalar(out=RF, in0=RV, scalar1=-0.5, scalar2=None, op0=ALU.is_lt), reads=r_k, writes=rf_k)
                P.op("dve", lambda e: e.tensor_tensor(out=RV, in0=RV, in1=RF, op=ALU.add), reads=r_k + rf_k, writes=r_k)
                P.op("act", lambda e, dst_base=dst_base, scol=scol: e.activation(
                    out=a16(dst_base, 0, 128, 0, [1, S]), in_=RV, func=AF.Sin, scale=ropecol[:, scol:scol + 1]),
                    reads=r_k + ["ropecol"], writes=["TAB"])
            for blk in range(NB):
                buf = blk % 2
                xk = rk(R_XS + buf * 4096, 4096)
                P.op("sp", lambda e, blk=blk, buf=buf: e.dma_start(
                    out=a32(A_RR + R_XS, 0, 128, buf * 1024, [1, 1024]), in_=x_d.ap()[s, blk * 128:(blk + 1) * 128, :]),
                    writes=xk, dma_key=("xs", buf))
                for half in range(2):
                    b = nbank()
                    for j in range(4):
                        kc = half * 4 + j
                        P.op("pe", lambda e, b=b, j=j, kc=kc, buf=buf: e.transpose(
                            banks[b][:, j * 128:(j + 1) * 128], a32(A_RR + R_XS, 0, 128, buf * 1024 + kc * 128, [1, 128]),
                            identf[:, :]), reads=xk + ["identf"], writes=[BK(b)])
                    dst = a32(A_X, 0, 128, half * 4 * S + blk * 128, [S, 4], [1, 128])
                    src = bkp(b, 0, 128, 0, [128, 4], [1, 128])
                    evac_copy(dst, src, reads=[BK(b)], writes=[XK(half * 4 + j, blk // 4) for j in range(4)])
            for n in range(NB):
                for w in range(2):
                    j = n - 1 + w
                    if j < 0:
                        continue
                    tb = (n * 2 + w) % 2
                    tmp = a32(A_SCR + S_DN, 0, 128, tb * CH, [1, 128])
                    tk = sk(S_DN + tb * 2048, 512)
                    P.op("pool", lambda e, n=n, j=j, tmp=tmp: e.tensor_scalar(
                        out=tmp, in0=a32(A_RR + R_POSF, 0, 128, n * 128, [1, 128]), scalar1=pkf[:, j:j + 1], scalar2=None,
                        op0=ALU.subtract), reads=posf_k + ["pkf"], writes=tk)
                    P.op("pool", lambda e, n=n, w=w, tmp=tmp: e.tensor_tensor(
                        out=a16(A_DM, 0, 128, (n * 2 + w) * 128, [1, 128]), in0=tmp, in1=maskbig[:, w, :], op=ALU.add),
                        reads=tk + ["maskbig"], writes=[("DM", n)])

        dn_rr = [0]

        def normalise(be, bo, ncols, gate_fn, out_fn, gate_keys, out_keys, sink_cols=None):
            buf = dn_rr[0] % 2
            dn_rr[0] += 1
            dnk = sk(S_DN + buf * 2048, 2048)

            def dn(p0, pn):
                return a32(A_SCR + S_DN, p0, pn, buf * CH, [1, ncols])
            if sink_cols is None:
                P.op("act", lambda e: e.activation(out=dn(0, 64), in_=bkp(be, 64, 64, 0, [1, ncols]), func=AF.Ln),
                     reads=[BK(be)], writes=dnk)
                P.op("act", lambda e: e.activation(out=dn(64, 64), in_=bkp(bo, 0, 64, 0, [1, ncols]), func=AF.Ln),
                     reads=[BK(bo)], writes=dnk)
            else:
                for k in range(ncols // 128):
                    he, ho = sink_cols[0][k], sink_cols[1][k]
                    P.op("act", lambda e, k=k, he=he: e.activation(
                        out=a32(A_SCR + S_DN, 0, 64, buf * CH + k * 128, [1, 128]), in_=bkp(be, 64, 64, k * 128, [1, 128]),
                        func=AF.Ln, bias=tap(es, 64, 64, he, [1, 1])), reads=[BK(be), "es"], writes=dnk)
                    P.op("act", lambda e, k=k, ho=ho: e.activation(
                        out=a32(A_SCR + S_DN, 64, 64, buf * CH + k * 128, [1, 128]), in_=bkp(bo, 0, 64, k * 128, [1, 128]),
                        func=AF.Ln, bias=tap(es, 0, 64, ho, [1, 1])), reads=[BK(bo), "es"], writes=dnk)
            P.op("act", lambda e: e.activation(out=dn(0, 128), in_=dn(0, 128), func=AF.Exp, scale=-1.0), reads=dnk, writes=dnk)
            P.op("pool", lambda e: e.tensor_tensor(out=dn(0, 128), in0=dn(0, 128), in1=gate_fn(0, 128), op=ALU.mult),
                 reads=dnk + gate_keys, writes=dnk)
            P.op("dve", lambda e: e.tensor_tensor(out=out_fn(0, 64), in0=bkp(be, 0, 64, 0, [1, ncols]), in1=dn(0, 64), op=ALU.mult),
                 reads=[BK(be)] + dnk, writes=out_keys)
            P.op("dve", lambda e: e.tensor_tensor(out=out_fn(64, 64), in0=bkp(bo, 64, 64, 0, [1, ncols]), in1=dn(64, 64), op=ALU.mult),
                 reads=[BK(bo)] + dnk, writes=out_keys)

        def layer(s, l):
            gmix0 = l * 8
            gple0 = 16 + l * 8
            cut(2)
            def proj_A(g, cq):
                hcs = (cq,)
                sa, ka = WS.get(l, "A%da" % g)
                for pr in range(2):
                    for c in hcs:
                        b = nbank()
                        for kc in range(KC):
                            mm(banks[b][:, :], wsl(sa, (pr * 8 + kc) * 128, 128), HT_ap(kc, c * CH, CH),
                               kc == 0, kc == KC - 1, reads=[ka, HK(kc, c)], writes=[BK(b)])
                        evac_copy(a16(A_RR + R_QTA, 0, 128, pr * S + c * CH, [1, CH]), banks[b][:, :],
                                  reads=[BK(b)], writes=rk(R_QTA + (pr * S + c * CH) * 2, 1024))
                for c in hcs:
                    b = nbank()
                    for kc in range(KC):
                        mm(banks[b][:, :], wsl(sa, 2048 + kc * 128, 128), HT_ap(kc, c * CH, CH),
                           kc == 0, kc == KC - 1, reads=[ka, HK(kc, c)], writes=[BK(b)])
                    evac_copy(a16(A_RR + R_KTA, 0, 128, c * CH, [1, CH]), banks[b][:, :],
                              reads=[BK(b)], writes=rk(R_KTA + c * CH * 2, 1024))
                sbb, kb = WS.get(l, "A%db" % g)
                for pr in range(2):
                    for c in hcs:
                        b = nbank()
                        for kc in range(KC):
                            mm(banks[b][:, :], wsl(sbb, (pr * 8 + kc) * 128, 128), HT_ap(kc, c * CH, CH),
                               kc == 0, kc == KC - 1, reads=[kb, HK(kc, c)], writes=[BK(b)])
                        t = 2 * g + pr
                        P.op("act", lambda e, b=b, t=t, c=c: e.activation(out=OA_ap(t, c * CH, CH), in_=banks[b][:, :], func=AF.Silu),
                             reads=[BK(b)], writes=[("OA", t, c)])
                if cq == 0:
                    vak = rk(R_VA, NB * 192 * 2)
                    P.op("pool", lambda e: e.memset(a16(A_RR + R_VA, 0, 128, 0, [192, NB], [1, 64]), 1.0), writes=vak)
                    P.op("pool", lambda e: e.memset(a16(A_RR + R_VA, 0, 128, 128, [192, NB], [1, 64]), 1.0), writes=vak)
                for q4 in hcs:
                    b = nbank()
                    for j in range(4):
                        blk = q4 * 4 + j
                        for kc in range(KC):
                            mm(bkp(b, 0, 128, j * 64, [1, 64]), HT_ap(kc, blk * 128, 128),
                               wsl(sbb, 2048 + kc * 64, 64), kc == 0, kc == KC - 1,
                               reads=[kb, HK(kc, q4)], writes=[BK(b)])
                    evac_copy(a16(A_RR + R_VA, 0, 128, q4 * 4 * 192 + 64, [192, 4], [1, 64]),
                              bkp(b, 0, 128, 0, [64, 4], [1, 64]), reads=[BK(b)],
                              writes=rk(R_VA + q4 * 4 * 192 * 2, 4 * 192 * 2))

            def attention_A(g):
                cut(3)

                def swa_scores(n, g=g):
                    buf = n % 2
                    sb0 = 0 if n % 2 == 0 else 2
                    ws_valid = [w for w in range(2) if n - 1 + w >= 0]
                    first = [True, True]
                    for w in ws_valid:
                        j = n - 1 + w
                        for hl in range(4):
                            par = hl % 2
                            jj = hl // 2
                            r0 = par * 64
                            mm(bkp(sb0 + par, 0, 128, (w * 2 + jj) * 128, [1, 128]),
                               a16(A_RR + R_KTA, r0, 64, j * 128, [1, 128]),
                               a16(A_RR + R_QTA, r0, 64, jj * S + n * 128, [1, 128]),
                               first[par], False,
                               reads=rk(R_KTA + j * 256, 256) + rk(R_QTA + (jj * S + n * 128) * 2, 256), writes=[BK(sb0 + par)])
                            first[par] = False
                    for w in ws_valid:
                        for hl in range(4):
                            par = hl % 2
                            jj = hl // 2
                            mm(bkp(sb0 + par, 0, 128, (w * 2 + jj) * 128, [1, 128]), tap(sidb, 0, 128, (4 * g + hl) * 128, [1, 128]),
                               a16(A_DM, 0, 128, (n * 2 + w) * 128, [1, 128]),
                               False, (w == ws_valid[-1] and jj == 1), reads=["sidb", ("DM", n)], writes=[BK(sb0 + par)])
                    w0 = ws_valid[0]
                    nw = len(ws_valid)
                    P.op("act", lambda e: e.activation(
                        out=a16(A_RR + R_PTA + buf * 2048, 0, 128, w0 * 256, [512, 2], [1, nw * 256]),
                        in_=bkp(sb0, 0, 128, w0 * 256, [512, 2], [1, nw * 256]), func=AF.Exp, scale=0.125),
                        reads=[BK(sb0), BK(sb0 + 1)], writes=rk(R_PTA + buf * 2048, 2048))

                def swa_pv(n, g=g):
                    buf = n % 2
                    ws_valid = [w for w in range(2) if n - 1 + w >= 0]
                    be = 4 + (n % 2) * 2
                    bo = be + 1
                    for par, bacc in ((0, be), (1, bo)):
                        for w in ws_valid:
                            j = n - 1 + w
                            lhs = a16(A_RR + R_VA, 0, 128, j * 192 + (64 if par == 0 else 0), [1, 128])
                            rhs = a16(A_RR + R_PTA + buf * 2048, 0, 128, par * 512 + w * 256, [1, 256])
                            mm(bkp(bacc, 0, 128, 0, [1, 256]), lhs, rhs, w == ws_valid[0], w == ws_valid[-1],
                               reads=rk(R_VA + j * 384, 384) + rk(R_PTA + buf * 2048, 2048), writes=[BK(bacc)])
                    c = n // 4
                    oak = [("OA", 2 * g, c), ("OA", 2 * g + 1, c)]
                    normalise(be, bo, 256,
                              lambda p0, pn, n=n, g=g: a16(A_OA, p0, pn, 2 * g * S + n * 128, [S, 2], [1, 128]),
                              lambda p0, pn, n=n, g=g: a16(A_OA, p0, pn, 2 * g * S + n * 128, [S, 2], [1, 128]),
                              oak, oak,
                              sink_cols=([l * 8 + 4 * g + 0, l * 8 + 4 * g + 2], [l * 8 + 4 * g + 1, l * 8 + 4 * g + 3]))

                swa_scores(0)
                for n in range(NB):
                    if n + 1 < NB:
                        swa_scores(n + 1)
                    swa_pv(n)

            def proj_B(cq):
                hcs = (cq,)
                s1a, k1a = WS.get(l, "B1a")
                s1b, k1b = WS.get(l, "B1b", lookahead=1)
                gq0 = 40 + l * 2
                gkv0 = 44 + l
                for c in hcs:
                    qb = []
                    for gq in range(2):
                        b = nbank()
                        qb.append(b)
                        for kc in range(KC):
                            mm(banks[b][:, :], wsl(s1a, (gq * 8 + kc) * 128, 128), HT_ap(kc, c * CH, CH),
                               kc == 0, kc == KC - 1, reads=[k1a, HK(kc, c)], writes=[BK(b)])
                        P.op("act", lambda e, b=b, gq=gq: e.activation(out=a16(A_RR + 0, 0, 128, gq * CH, [1, CH]),
                                                                       in_=banks[b][:, :], func=AF.Square),
                             reads=[BK(b)], writes=rk(0 + gq * 1024, 1024))
                    bkv = nbank()
                    for kc in range(KC):
                        mm(banks[bkv][:, :], wsl(s1b, kc * 128, 128), HT_ap(kc, c * CH, CH),
                           kc == 0, kc == KC - 1, reads=[k1b, HK(kc, c)], writes=[BK(bkv)])
                    P.op("act", lambda e, b=bkv: e.activation(out=a16(A_RR + 0, 0, 128, 2 * CH, [1, CH]),
                                                              in_=banks[b][:, :], func=AF.Square),
                         reads=[BK(bkv)], writes=rk(0 + 2048, 1024))
                    bs1 = nbank()
                    for gq in range(2):
                        mm(banks[bs1][:, :], onesb[:, :], a16(A_RR + 0, 0, 128, gq * CH, [1, CH]), gq == 0, gq == 1,
                           reads=rk(0 + gq * 1024, 1024) + ["onesb"], writes=[BK(bs1)])
                    bs2 = nbank()
                    mm(banks[bs2][:, :], onesb[:, :], a16(A_RR + 0, 0, 128, 2 * CH, [1, CH]), True, True,
                       reads=rk(0 + 2048, 1024) + ["onesb"], writes=[BK(bs2)])
                    rsq = a32(A_RR + 0 + 4096, 0, 128, 0, [1, CH])
                    rsk = a32(A_RR + 0 + 6144, 0, 128, 0, [1, CH])
                    rsqk = rk(0 + 4096, 2048)
                    rskk = rk(0 + 6144, 2048)
                    P.op("act", lambda e, b=bs1: e.activation(out=rsq, in_=banks[b][:, :], func=AF.Ln, bias=EPS, scale=1.0 / 256),
                         reads=[BK(bs1)], writes=rsqk)
                    P.op("act", lambda e: e.activation(out=rsq, in_=rsq, func=AF.Exp, scale=-0.5), reads=rsqk, writes=rsqk)
                    P.op("act", lambda e, b=bs2: e.activation(out=rsk, in_=banks[b][:, :], func=AF.Ln, bias=EPS, scale=1.0 / 128),
                         reads=[BK(bs2)], writes=rskk)
                    P.op("act", lambda e: e.activation(out=rsk, in_=rsk, func=AF.Exp, scale=-0.5), reads=rskk, writes=rskk)
                    for gq in range(2):
                        P.op("dve", lambda e, gq=gq, b=qb[gq], c=c: e.scalar_tensor_tensor(
                            out=a16(A_RR + R_QDN, 0, 128, gq * S + c * CH, [1, CH]), in0=banks[b][:, :],
                            scalar=gcol[:, gq0 + gq: gq0 + gq + 1], in1=rsq, op0=ALU.mult, op1=ALU.mult),
                            reads=[BK(qb[gq]), "gcol"] + rsqk, writes=rk(R_QDN + (gq * S + c * CH) * 2, 1024))
                    P.op("dve", lambda e, b=bkv, c=c: e.scalar_tensor_tensor(
                        out=a16(A_RR + R_KVDN, 0, 128, c * CH, [1, CH]), in0=banks[b][:, :],
                        scalar=gcol[:, gkv0: gkv0 + 1], in1=rsk, op0=ALU.mult, op1=ALU.mult),
                        reads=[BK(bkv), "gcol"] + rskk, writes=rk(R_KVDN + c * CH * 2, 1024))
                    bka = nbank()
                    bkb = nbank()
                    for var, b in ((0, bka), (1, bkb)):
                        for kc in range(KC):
                            mm(bkp(b, 0, 32, 0, [1, CH]), wsl(s1b, 1024 + kc * 64 + var * 32, 32),
                               HT_ap(kc, c * CH, CH), kc == 0, kc == KC - 1,
                               reads=[k1b, HK(kc, c)], writes=[BK(b)])
                    t1 = a32(A_SCR + S_DN, 0, 32, 0, [1, CH])
                    t2 = a32(A_SCR + S_DN, 0, 32, CH, [1, CH])
                    t1k = sk(S_DN, 2048)
                    t2k = sk(S_DN + 2048, 2048)
                    P.op("dve", lambda e, b=bka, c=c: e.tensor_tensor(out=t1, in0=bkp(b, 0, 32, 0, [1, CH]),
                                                                     in1=a16(A_CS, 0, 32, c * CH, [1, CH]), op=ALU.mult),
                         reads=[BK(bka), "TAB"], writes=t1k)
                    P.op("dve", lambda e, b=bkb, c=c: e.tensor_tensor(out=t2, in0=bkp(b, 0, 32, 0, [1, CH]),
                                                                     in1=a16(A_SN, 0, 32, c * CH, [1, CH]), op=ALU.mult),
                         reads=[BK(bkb), "TAB"], writes=t2k)
                    P.op("pool", lambda e, c=c: e.tensor_tensor(out=a16(A_KRF, 0, 32, c * CH, [1, CH]), in0=t1, in1=t2, op=ALU.add),
                         reads=t1k + t2k, writes=[("KRF", c)])
                for i, nm in enumerate(["B2a", "B2b"]):
                    s2, k2 = WS.get(l, nm)
                    for tt in range(2):
                        t = 2 * i + tt
                        for c in hcs:
                            b = nbank()
                            for kc in range(KC):
                                mm(banks[b][:, :], wsl(s2, (tt * 8 + kc) * 128, 128), HT_ap(kc, c * CH, CH),
                                   kc == 0, kc == KC - 1, reads=[k2, HK(kc, c)], writes=[BK(b)])
                            P.op("act", lambda e, b=b, t=t, c=c: e.activation(out=OB_ap(t, c * CH, CH), in_=banks[b][:, :], func=AF.Silu),
                                 reads=[BK(b)], writes=[("OB", t, c)])

            def mla_pairs():
                if s == 0 and l == 0 and nlayers > 1:
                    emit_conv(1)
                P.op("pool", lambda e: e.memset(a16(A_RR + R_VB, 0, 128, 64, [192, NB], [1, 64]), 1.0),
                     writes=rk(R_VB, NB * 192 * 2))
                if s == 0 and l == 0:
                    dbg("QDN", a16(A_RR + R_QDN, 0, 128, 0, [S, 2], [1, S]), [128, 2, S], BF16, rk(R_QDN, 8192))
                    dbg("KVDN", a16(A_RR + R_KVDN, 0, 128, 0, [1, S]), [128, S], BF16, rk(R_KVDN, 4096))
                    dbg("KRF", a16(A_KRF, 0, 128, 0, [1, S]), [128, S], BF16, [("KRF", c_) for c_ in range(NCH)])
                    dbg("GB", a16(A_OB, 0, 128, 0, [S, 4], [1, S]), [128, 4, S], BF16, [("OB", t_, c_) for t_ in range(4) for c_ in range(NCH)])
                cut(5)
                for pair in range(4):
                    sm, km = MS.get(l, "MP%d" % pair, lookahead=1)
                    for c in range(NCH):
                        for hh in range(2):
                            b = nbank(range(4))
                            mm(bkp(b, 0, 64, 0, [1, CH]), mps(sm, 768 + hh * 64, 64), a16(A_RR + R_KVDN, 0, 128, c * CH, [1, CH]),
                               True, True, reads=[km] + rk(R_KVDN + c * CH * 2, 1024), writes=[BK(b)])
                            evac_copy(a16(A_RR + R_KTB, 0, 64, hh * S + c * CH, [1, CH]), bkp(b, 0, 64, 0, [1, CH]),
                                      reads=[BK(b)], writes=rk(R_KTB + (hh * S + c * CH) * 2, 1024))
                    P.op("sp", lambda e: e.dma_start(out=a16(A_RR + R_KTB, 64, 32, 0, [S, 2], [1, S]),
                                                     in_=a16(A_KRF, 0, 32, 0, [0, 2], [1, S])),
                         reads=[("KRF", c) for c in range(NCH)], writes=rk(R_KTB, 8192), dma_key="krep")
                    for q4 in range(4):
                        b = nbank(range(4))
                        for j in range(4):
                            blk = q4 * 4 + j
                            mm(bkp(b, 0, 128, j * 128, [1, 128]), a16(A_RR + R_KVDN, 0, 128, blk * 128, [1, 128]),
                               mps(sm, 896, 128), True, True, reads=[km] + rk(R_KVDN + blk * 256, 256), writes=[BK(b)])
                        evac_copy(a16(A_RR + R_VB, 0, 128, q4 * 4 * 192, [192, 4], [128, 2], [1, 64]),
                                  bkp(b, 0, 128, 0, [128, 4], [64, 2], [1, 64]),
                                  reads=[BK(b)], writes=rk(R_VB + q4 * 4 * 384, 4 * 384))

                    def emit_q(c, pair=pair, sm=sm, km=km):
                        qbuf = c % 2
                        for hh in range(2):
                            bp = nbank(range(4))
                            bs_ = nbank(range(4))
                            for var, b in ((0, bp), (1, bs_)):
                                for kc in range(2):
                                    mm(bkp(b, 0, 96, 0, [1, CH]), mps(sm, ((kc * 2 + hh) * 2 + var) * 96, 96),
                                       a16(A_RR + R_QDN, 0, 128, kc * S + c * CH, [1, CH]), kc == 0, kc == 1,
                                       reads=[km] + rk(R_QDN + (kc * S + c * CH) * 2, 1024), writes=[BK(b)])
                            qslot = qbuf * 2 + hh
                            qk = [("HT", 1, qslot)]
                            P.op("act", lambda e, bp=bp, qslot=qslot: e.copy(a16(A_HT, 0, 64, 4096 + qslot * CH, [1, CH]), bkp(bp, 0, 64, 0, [1, CH])),
                                 reads=[BK(bp)], writes=qk)
                            t1 = a32(A_KRF, 64, 32, 0, [1, CH])
                            t2 = a32(A_KRF, 64, 32, CH, [1, CH])
                            t1k = [("T1q",)]
                            t2k = [("T2q",)]
                            P.op("dve", lambda e, bp=bp: e.tensor_tensor(out=t1, in0=bkp(bp, 64, 32, 0, [1, CH]),
                                                                        in1=a16(A_CS, 64, 32, c * CH, [1, CH]), op=ALU.mult),
                                 reads=[BK(bp), "TAB"], writes=t1k)
                            P.op("dve", lambda e, bs_=bs_: e.tensor_tensor(out=t2, in0=bkp(bs_, 64, 32, 0, [1, CH]),
                                                                          in1=a16(A_SN, 64, 32, c * CH, [1, CH]), op=ALU.mult),
                                 reads=[BK(bs_), "TAB"], writes=t2k)
                            P.op("pool", lambda e, qslot=qslot: e.tensor_tensor(out=a16(A_HT, 64, 32, 4096 + qslot * CH, [1, CH]), in0=t1, in1=t2, op=ALU.add),
                                 reads=t1k + t2k, writes=qk)

                    emit_q(0)
                    pt_rr = [0]
                    for c in range(NCH):
                        qbuf = c % 2
                        te = 4 + 2 * (c % 2)
                        to = 5 + 2 * (c % 2)
                        steps = [(j, hh) for j in range(4 * c + 4) for hh in range(2)]
                        state = {}

                        def emit_s(idx, c=c, qbuf=qbuf, steps=steps, state=state):
                            j, hh = steps[idx]
                            i = j - 4 * c
                            col0 = 128 * i if i > 0 else 0
                            diag = i >= 0
                            ncol = CH - col0
                            b = nbank(range(4))
                            pb = pt_rr[0] % 6
                            pt_rr[0] += 1
                            state[idx] = (b, pb, col0, ncol)
                            qslot = qbuf * 2 + hh
                            mm(bkp(b, 0, 128, col0, [1, ncol]), a16(A_RR + R_KTB, 0, 96, hh * S + j * 128, [1, 128]),
                               a16(A_HT, 0, 96, 4096 + qslot * CH + col0, [1, ncol]), True, not diag,
                               reads=rk(R_KTB + (hh * S + j * 128) * 2, 256) + [("HT", 1, qslot)], writes=[BK(b)])
                            if diag:
                                mm(bkp(b, 0, 128, col0, [1, 128]), identb[:, :], maskneg[:, :], False, True,
                                   reads=["identb", "maskneg"], writes=[BK(b)])
                            P.op("act", lambda e: e.activation(out=a16(A_PTB, 0, 128, pb * CH + col0, [1, ncol]),
                                                               in_=bkp(b, 0, 128, col0, [1, ncol]), func=AF.Exp, scale=MLA_SCALE),
                                 reads=[BK(b)], writes=[("PTB", pb)])

                        def emit_pv(idx, c=c, te=te, to=to, steps=steps, state=state):
                            j, hh = steps[idx]
                            b, pb, col0, ncol = state[idx]
                            acc = te if hh == 0 else to
                            lhs = a16(A_RR + R_VB, 0, 128, j * 192 + (0 if hh == 0 else 64), [1, 128])
                            mm(bkp(acc, 0, 128, col0, [1, ncol]), lhs, a16(A_PTB, 0, 128, pb * CH + col0, [1, ncol]),
                               j == 0, j == 4 * c + 3, reads=rk(R_VB + j * 384, 384) + [("PTB", pb)], writes=[BK(acc)])

                        LA = 2
                        for k in range(min(LA, len(steps))):
                            emit_s(k)
                        for idx in range(len(steps)):
                            if idx + LA < len(steps):
                                emit_s(idx + LA)
                            emit_pv(idx)
                            if idx == len(steps) // 2 and c + 1 < NCH:
                                emit_q(c + 1)
                        obk = [("OB", pair, c)]
                        normalise(te, to, CH,
                                  lambda p0, pn, c=c, pair=pair: a16(A_OB, p0, pn, pair * S + c * CH, [1, CH]),
                                  lambda p0, pn, c=c, pair=pair: a16(A_OB, p0, pn, pair * S + c * CH, [1, CH]),
                                  obk, obk)
                if s == 0 and l == 0:
                    dbg("OB", a16(A_OB, 0, 128, 0, [S, 4], [1, S]), [128, 4, S], BF16, [("OB", t_, c_) for t_ in range(4) for c_ in range(NCH)])

            def prep_ptt():
                for blk in range(NB):
                    buf = blk % 2
                    pk_ = rk(R_PSTG + buf * 1024, 1024)
                    P.op("sp", lambda e, blk=blk, buf=buf: e.dma_start(
                        out=a32(A_RR + R_PSTG, 0, 128, buf * 256, [1, 256]), in_=p_d.ap()[l, s, blk * 128:(blk + 1) * 128, :]),
                        writes=pk_, dma_key=("ps", buf))
                    b = nbank()
                    for kc in range(2):
                        P.op("pe", lambda e, b=b, kc=kc, buf=buf: e.transpose(
                            banks[b][:, kc * 128:(kc + 1) * 128], a32(A_RR + R_PSTG, 0, 128, buf * 256 + kc * 128, [1, 128]),
                            identf[:, :]), reads=pk_ + ["identf"], writes=[BK(b)])
                    evac_copy(a16(A_RR + R_PTT, 0, 128, blk * 128, [S, 2], [1, 128]),
                              bkp(b, 0, 128, 0, [128, 2], [1, 128]), reads=[BK(b)],
                              writes=rk(R_PTT + blk * 256, 256) + rk(R_PTT + (S + blk * 128) * 2, 256))

            def proj_YO(cq):
                hcs = (cq,)
                for gy in range(8):
                    sy, ky = WS.get(l, "Y%d" % gy)
                    for c in hcs:
                        bma, bmb, bya, byb = nbank(), nbank(), nbank(), nbank()
                        for kc in range(KC):
                            mm(banks[bma][:, :], wsl(sy, kc * 128, 128), HT_ap(kc, c * CH, CH),
                               kc == 0, kc == KC - 1, reads=[ky, HK(kc, c)], writes=[BK(bma)])
                        for kc in range(KC):
                            mm(banks[bmb][:, :], wsl(sy, 1024 + kc * 128, 128), HT_ap(kc, c * CH, CH),
                               kc == 0, kc == KC - 1, reads=[ky, HK(kc, c)], writes=[BK(bmb)])
                        for kc in range(4):
                            mm(banks[bya][:, :], wsl(sy, 2048 + kc * 128, 128), OA_ap(kc, c * CH, CH),
                               kc == 0, kc == 3, reads=[ky, ("OA", kc, c)], writes=[BK(bya)])
                        for kc in range(4):
                            mm(banks[byb][:, :], wsl(sy, 2560 + kc * 128, 128), OB_ap(kc, c * CH, CH),
                               kc == 0, kc == 3, reads=[ky, ("OB", kc, c)], writes=[BK(byb)])
                        pb = (gy * NCH + c) % 2
                        sa_ = a32(A_RR + 8192, 0, 128, pb * 1024, [1, CH])
                        sb_ = a32(A_RR + 8192, 0, 128, pb * 1024 + CH, [1, CH])
                        sak = rk(8192 + pb * 4096, 2048)
                        sbk = rk(8192 + pb * 4096 + 2048, 2048)
                        P.op("act", lambda e, b=bma, sa_=sa_: e.activation(out=sa_, in_=banks[b][:, :], func=AF.Sigmoid),
                             reads=[BK(bma)], writes=sak)
                        P.op("act", lambda e, b=bmb, sb_=sb_: e.activation(out=sb_, in_=banks[b][:, :], func=AF.Sigmoid),
                             reads=[BK(bmb)], writes=sbk)
                        P.op("dve", lambda e, b=bya, sa_=sa_: e.tensor_tensor(out=sa_, in0=banks[b][:, :], in1=sa_, op=ALU.mult),
                             reads=[BK(bya)] + sak, writes=sak)
                        P.op("dve", lambda e, b=byb, sb_=sb_: e.tensor_tensor(out=sb_, in0=banks[b][:, :], in1=sb_, op=ALU.mult),
                             reads=[BK(byb)] + sbk, writes=sbk)
                        P.op("dve", lambda e, gy=gy, c=c, sa_=sa_, sb_=sb_: e.tensor_tensor(
                            out=a16(A_RR + R_YT, 0, 128, gy * 512, [1, CH]), in0=sa_, in1=sb_, op=ALU.add),
                            reads=sak + sbk, writes=rk(R_YT + (gy * 512) * 2, 1024))
                for u in range(4):
                    so, ko = WS.get(l, "O%d" % u)
                    for gl in range(2):
                        gx = 2 * u + gl
                        for c in hcs:
                            b = nbank()
                            for kc in range(KC):
                                mm(banks[b][:, :], wsl(so, (gl * 8 + kc) * 128, 128), a16(A_RR + R_YT, 0, 128, kc * 512, [1, CH]),
                                   kc == 0, kc == KC - 1, reads=[ko] + rk(R_YT + (kc * 512) * 2, 1024), writes=[BK(b)])
                            xa = X_ap(gx, c * CH, CH)
                            P.op("dve", lambda e, b=b, xa=xa: e.tensor_tensor(out=xa, in0=xa, in1=banks[b][:, :], op=ALU.add),
                                 reads=[BK(b), XK(gx, c)], writes=[XK(gx, c)])

            def proj_G(cq):
                hcs = (cq,)
                for u in range(4):
                    sg, kg = WS.get(l, "G%d" % u)
                    for gl in range(2):
                        gx = 2 * u + gl
                        for c in hcs:
                            bg, bp_ = nbank(), nbank()
                            for kc in range(KC):
                                mm(banks[bg][:, :], wsl(sg, gl * 1280 + kc * 128, 128), HT_ap(kc, c * CH, CH),
                                   kc == 0, kc == KC - 1, reads=[kg, HK(kc, c)], writes=[BK(bg)])
                            for kc in range(2):
                                mm(banks[bp_][:, :], wsl(sg, gl * 1280 + 1024 + kc * 128, 128),
                                   a16(A_RR + R_PTT, 0, 128, kc * S + c * CH, [1, CH]),
                                   kc == 0, kc == 1, reads=[kg] + rk(R_PTT + (kc * S + c * CH) * 2, 1024), writes=[BK(bp_)])
                            pb = (gx * NCH + c) % 2
                            sg_ = a32(A_RR + 8192, 0, 128, pb * 1024, [1, CH])
                            sgk = rk(8192 + pb * 4096, 2048)
                            P.op("act", lambda e, b=bg, sg_=sg_: e.activation(out=sg_, in_=banks[b][:, :], func=AF.Sigmoid),
                                 reads=[BK(bg)], writes=sgk)
                            P.op("dve", lambda e, b=bp_, sg_=sg_: e.tensor_tensor(out=sg_, in0=banks[b][:, :], in1=sg_, op=ALU.mult),
                                 reads=[BK(bp_)] + sgk, writes=sgk)
                            xa = X_ap(gx, c * CH, CH)
                            P.op("dve", lambda e, xa=xa, sg_=sg_: e.tensor_tensor(out=xa, in0=xa, in1=sg_, op=ALU.add),
                                 reads=sgk + [XK(gx, c)], writes=[XK(gx, c)])


            return dict(proj_A=proj_A, attention_A=attention_A, proj_B=proj_B, mla_pairs=mla_pairs,
                        prep_ptt=prep_ptt, proj_YO=proj_YO, proj_G=proj_G)

        def final(s):
            for c in range(NCH):
                rmsnorm_chunk(c, 32,
                              lambda kc: a32(A_RR + R_XN, 0, 128, kc * CH, [1, CH]),
                              lambda kc: rk(R_XN + kc * 2048, 2048))
                for bl in range(4):
                    blk = c * 4 + bl
                    buf = blk % 2
                    xk = rk(R_XS + buf * 4096, 4096)
                    for half in range(2):
                        b = nbank()
                        for j in range(4):
                            kc = half * 4 + j
                            P.op("pe", lambda e, b=b, j=j, kc=kc, bl=bl: e.transpose(
                                banks[b][:, j * 128:(j + 1) * 128], a32(A_RR + R_XN, 0, 128, kc * CH + bl * 128, [1, 128]),
                                identf[:, :]), reads=rk(R_XN + kc * 2048, 2048) + ["identf"], writes=[BK(b)])
                        evac_copy(a32(A_RR + R_XS, 0, 128, buf * 1024 + half * 512, [1, 512]), banks[b][:, :],
                                  reads=[BK(b)], writes=xk)
                    o = P.op("sp", lambda e, blk=blk, buf=buf: e.dma_start(
                        out=out_d.ap()[s, blk * 128:(blk + 1) * 128, :], in_=a32(A_RR + R_XS, 0, 128, buf * 1024, [1, 1024])),
                        reads=xk, dma_key=("out", buf))
                    out_ops.append(o)

        ep = 1
        try:
            cut(0)
            for s in range(nseq):
                P.epoch = ep
                ep += 1
                seq_setup(s)
                if s == 0:
                    dbg("X0", a32(A_X, 0, 128, 0, [S, KC], [1, S]), [128, KC, S], F32, [XK(k_, c_) for k_ in range(KC) for c_ in range(NCH)])
                    dbg("CS", a16(A_CS, 0, 128, 0, [1, S]), [128, S], BF16, ["TAB"])
                    dbg("SN", a16(A_SN, 0, 128, 0, [1, S]), [128, S], BF16, ["TAB"])
                    dbg("DM", a16(A_DM, 0, 128, 0, [1, 4096]), [128, 4096], BF16, [("DM", n_) for n_ in range(NB)])
                cut(1)
                LJ = [("A", g_, c_) for g_ in range(2) for c_ in range(NCH)] + [("B", 0, c_) for c_ in range(NCH)] + \
                     [("Y", 0, c_) for c_ in range(NCH)] + [("G", 0, c_) for c_ in range(NCH)]
                alljobs = [(l_,) + j_ for l_ in range(nlayers) for j_ in LJ]
                fns = {}

                def gbase_of(k):
                    l_, kind, g_, c_ = alljobs[k]
                    return (16 + l_ * 8) if kind == "G" else l_ * 8

                nj = len(alljobs)
                norm_A(alljobs[0][3])
                norm_B(alljobs[0][3], gbase_of(0), 0)
                if nj > 1:
                    norm_A(alljobs[1][3])
                cur_l = -1
                for k, (l_, kind, g_, c_) in enumerate(alljobs):
                    if l_ != cur_l:
                        cur_l = l_
                        P.epoch = ep
                        ep += 1
                        fns = layer(s, l_)
                    if k + 1 < nj:
                        norm_B(alljobs[k + 1][3], gbase_of(k + 1), (k + 1) % 2)
                    if k + 2 < nj:
                        norm_A(alljobs[k + 2][3])
                    HTSEL[0] = k % 2
                    if kind == "A":
                        fns["proj_A"](g_, c_)
                        if c_ == NCH - 1:
                            HTSEL[0] = 0
                            fns["attention_A"](g_)
                    elif kind == "B":
                        fns["proj_B"](c_)
                        if c_ == NCH - 1:
                            HTSEL[0] = 0
                            fns["mla_pairs"]()
                    elif kind == "Y":
                        if c_ == 0:
                            fns["prep_ptt"]()
                        fns["proj_YO"](c_)
                    else:
                        fns["proj_G"](c_)
                HTSEL[0] = 0
                final(s)
        except _Cut:
            pass
        print("ops:", len(P.ops), {e: sum(1 for o in P.ops if o.eng == e) for e in ENGS})
        lastdma = {}
        for o in P.ops:
            if o.is_dma:
                lastdma[o.dma_key] = o
        P.emit(final_wait_ops=list(lastdma.values()))
    return nc


_CACHE = {}


def kernel(**inputs):
    x = np.ascontiguousarray(np.asarray(inputs["x"], np.float32))
    p = np.ascontiguousarray(np.asarray(inputs["p"], np.float32))
    pos = np.ascontiguousarray(np.asarray(inputs["positions"], np.int32))
    wsrc0 = prep_weights(inputs)
    wsrcs = []
    for c in range(NCORES):
        w_ = np.zeros((DEPTH, 128, WTOTAL + 16), np.float32)
        w_[:, :, :WTOTAL] = wsrc0
        w_[:, :, WTOTAL:] = float(c)
        wsrcs.append(w_)
    consts = host_consts()
    gcol = np.zeros((128, 64), np.float32)
    g_mix = np.asarray(inputs["g_mix"], np.float32)
    g_ple = np.asarray(inputs["g_ple"], np.float32)
    g_fin = np.asarray(inputs["g_final"], np.float32)
    g_q = np.asarray(inputs["g_q"], np.float32)
    g_kv = np.asarray(inputs["g_kv"], np.float32)
    for l in range(DEPTH):
        gcol[:, l * 8:(l + 1) * 8] = g_mix[l].reshape(8, 128).T
        gcol[:, 16 + l * 8:16 + (l + 1) * 8] = g_ple[l].reshape(8, 128).T
        gcol[:, 40 + l * 2:40 + l * 2 + 2] = g_q[l].reshape(2, 128).T
        gcol[:, 44 + l] = g_kv[l]
    gcol[:, 32:40] = g_fin.reshape(8, 128).T
    sink = np.asarray(inputs["sink"], np.float32).reshape(1, 16)
    if "nc" not in _CACHE:
        _CACHE["nc"] = build_program()
    nc = _CACHE["nc"]
    in_maps = []
    for c in range(NCORES):
        m = {
            "x": x[c * SPC:(c + 1) * SPC],
            "p": np.ascontiguousarray(p[:, c * SPC:(c + 1) * SPC]),
            "pos": pos[c * SPC:(c + 1) * SPC],
            "wsrc": wsrcs[c],
            "gcol": gcol,
            "sink": sink,
            "identf": consts["identf"],
            "maskneg": consts["maskneg"],
            "maskbig": consts["maskbig"],
            "sid": consts["sid"],
            "ropecol": consts["ropecol"],
        }
        in_maps.append(m)
    res = run_bass_kernel_spmd(nc, in_maps, core_ids=list(range(NCORES)))
    out = np.concatenate([np.asarray(r["out"], np.float32) for r in res.results], axis=0)
    return out
```

```python
import contextlib
import math
import numpy as np
import concourse.bass as bass
import concourse.mybir as mybir
from concourse.bass_utils import run_bass_kernel_spmd

F32 = mybir.dt.float32
BF16 = mybir.dt.bfloat16
I32 = mybir.dt.int32
AF = mybir.ActivationFunctionType
ALU = mybir.AluOpType

NCORES = 8
SPC = 4
S = 2048
D = 1024
NB = 16
NCH = 4
CH = 512
KC = 8
DEPTH = 2
EPS = 1e-6
BIGM = float(2 ** 20)
NEGM = -30000.0
SLOPES = [2.0 ** (-(i + 1)) for i in range(8)]
MLA_SCALE = 96.0 ** -0.5

ENGS = ("pe", "act", "dve", "pool", "sp")


class Op:
    __slots__ = ("eng", "fn", "deps", "sig", "epoch", "is_dma", "dma_key", "idx", "dma_cnt")

    def __init__(self, eng, fn, epoch, is_dma, dma_key):
        self.eng = eng
        self.fn = fn
        self.deps = []
        self.sig = None
        self.epoch = epoch
        self.is_dma = is_dma
        self.dma_key = dma_key
        self.dma_cnt = None


class Prog:
    def __init__(self, nc):
        self.nc = nc
        self.ops = []
        self.last_w = {}
        self.readers = {}
        self.epoch = 0
        self.dma_counts = {}

    def op(self, eng, fn, reads=(), writes=(), dma_key=None):
        is_dma = dma_key is not None
        o = Op(eng, fn, self.epoch, is_dma, dma_key)
        o.idx = len(self.ops)
        deps = {}
        for r in reads:
            w = self.last_w.get(r)
            if w is not None:
                deps[w.idx] = (w, "raw")
        for r in writes:
            w = self.last_w.get(r)
            if w is not None:
                deps.setdefault(w.idx, (w, "waw"))
            rd = self.readers.get(r)
            if rd:
                for x in rd[0].values():
                    deps.setdefault(x.idx, (x, "war"))
                for x in rd[1]:
                    deps.setdefault(x.idx, (x, "war"))
        for w, kind in deps.values():
            if w.eng == eng and not w.is_dma and not is_dma:
                if eng == "pe":
                    continue
                if kind == "war":
                    continue
            o.deps.append(w)
        for r in writes:
            self.last_w[r] = o
            self.readers[r] = ({}, [])
        for r in reads:
            rd = self.readers.get(r)
            if rd is None:
                rd = ({}, [])
                self.readers[r] = rd
            if is_dma:
                rd[1].append(o)
            else:
                rd[0][eng] = o
        if is_dma:
            c = self.dma_counts.get(dma_key, 0) + 1
            self.dma_counts[dma_key] = c
            o.dma_cnt = c
        self.ops.append(o)
        return o

    def emit(self, final_wait_ops=()):
        nc = self.nc
        need = set()
        for o in self.ops:
            for d in o.deps:
                need.add(d.idx)
        counters = {}
        semkeys = set()
        for o in self.ops:
            if o.is_dma:
                semkeys.add(("dma", o.dma_key))
                continue
            if o.idx in need:
                k = (o.eng, o.epoch)
                counters[k] = counters.get(k, 0) + 1
                o.sig = counters[k]
                semkeys.add(k)
        with contextlib.ExitStack() as st:
            sems = {}
            for i, k in enumerate(sorted(semkeys, key=str)):
                sems[k] = st.enter_context(nc.semaphore("s%d" % i))
            block = st.enter_context(nc.Block())
            per_eng = {e: [o for o in self.ops if o.eng == e] for e in ENGS}

            def target(d):
                if d.is_dma:
                    return ("dma", d.dma_key), 16 * d.dma_cnt
                return (d.eng, d.epoch), d.sig

            def run(engobj, ename):
                waited = {}
                for o in per_eng[ename]:
                    req = {}
                    for d in o.deps:
                        k, v = target(d)
                        if req.get(k, 0) < v:
                            req[k] = v
                    for k, v in req.items():
                        if waited.get(k, 0) >= v:
                            continue
                        engobj.wait_ge(sems[k], v)
                        waited[k] = v
                    ins = o.fn(engobj)
                    if o.is_dma:
                        ins.then_inc(sems[("dma", o.dma_key)], 16)
                    elif o.sig is not None:
                        ins.then_inc(sems[(o.eng, o.epoch)], 1)
                if ename == "sp":
                    for o in final_wait_ops:
                        k, v = target(o)
                        engobj.wait_ge(sems[k], v)

            @block.tensor
            def _(e):
                run(e, "pe")

            @block.scalar
            def _(e):
                run(e, "act")

            @block.vector
            def _(e):
                run(e, "dve")

            @block.gpsimd
            def _(e):
                run(e, "pool")

            @block.sync
            def _(e):
                run(e, "sp")


def unit_table():
    units = []

    def add(name, size, subs):
        units.append((name, size, subs))

    for g in range(2):
        add("A%da" % g, 3072, {"qa": 0, "ka": 2048})
        add("A%db" % g, 2560, {"ga": 0, "va": 2048})
    add("B1a", 2048, {"qd": 0})
    add("B1b", 1536, {"kvd": 0, "kr": 1024})
    add("B2a", 2048, {"gb": 0})
    add("B2b", 2048, {"gb": 0})
    for pr in range(4):
        add("MP%d" % pr, 1024, {"uq": 0, "uk": 768, "uv": 896})
    for gy in range(8):
        add("Y%d" % gy, 3072, {"ma": 0, "mb": 1024, "wa": 2048, "wb": 2560})
    for u in range(4):
        add("O%d" % u, 2048, {"wo": 0})
    for u in range(4):
        add("G%d" % u, 2560, {"pg": 0})
    offs = {}
    o = 0
    for name, size, subs in units:
        offs[name] = (o, size, subs)
        o += size
    return units, offs, o


UNITS, UOFF, WTOTAL = unit_table()
_YO = ["Y%d" % i for i in range(8)] + ["O%d" % i for i in range(4)]
_G = ["G%d" % i for i in range(4)]
MAIN_ORDER = ["A0a", "A0b"] * 4 + ["A1a", "A1b"] * 4 + ["B1a", "B1b", "B2a", "B2b"] * 4 + _YO * 4 + _G * 4
MP_ORDER = ["MP%d" % i for i in range(4)]
CONV_GROUPS = [("A0a", "A1b"), ("B1a", "MP3"), ("Y0", "Y3"), ("Y4", "Y7"), ("O0", "O3"), ("G0", "G3")]
WSLOT = 3072
NWSLOT = 3
MPSLOT = 1024


def prep_weights(inp):
    out = np.zeros((DEPTH, 128, WTOTAL), np.float32)

    def put(arr, off, W, cols, kcn):
        M = len(cols)
        for kc in range(kcn):
            arr[:, off + kc * M: off + (kc + 1) * M] = W[kc * 128:(kc + 1) * 128][:, cols]

    r = np.arange
    for l in range(DEPTH):
        a = out[l]
        win = np.asarray(inp["w_in"][l], np.float32)
        wuq = np.asarray(inp["w_uq"][l], np.float32)
        wukv = np.asarray(inp["w_ukv"][l], np.float32)
        wa = np.asarray(inp["w_br_a"][l], np.float32)
        wb = np.asarray(inp["w_br_b"][l], np.float32)
        wo = np.asarray(inp["w_out"][l], np.float32)
        wpg = np.asarray(inp["w_ple_gate"][l], np.float32)
        wpp = np.asarray(inp["w_ple_proj"][l], np.float32)
        for g in range(2):
            o, _, sub = UOFF["A%da" % g]
            for pr in range(2):
                put(a, o + sub["qa"] + pr * 1024, win, r(0, 128) + (4 * g + 2 * pr) * 64, 8)
            kcols = np.concatenate([r(0, 64), r(0, 64)]) + 512 + g * 64
            put(a, o + sub["ka"], win, kcols, 8)
            o, _, sub = UOFF["A%db" % g]
            for pr in range(2):
                put(a, o + sub["ga"] + pr * 1024, win, r(0, 128) + 768 + (4 * g + 2 * pr) * 64, 8)
            put(a, o + sub["va"], win, r(0, 64) + 640 + g * 64, 8)
        o, _, sub = UOFF["B1a"]
        for gq in range(2):
            put(a, o + gq * 1024, win, r(0, 128) + 1280 + gq * 128, 8)
        o, _, sub = UOFF["B1b"]
        put(a, o + sub["kvd"], win, r(0, 128) + 1536, 8)
        krc = np.concatenate([r(0, 32), r(16, 32), r(0, 16)]) + 1664
        put(a, o + sub["kr"], win, krc, 8)
        for i, nm in enumerate(["B2a", "B2b"]):
            o, _, sub = UOFF[nm]
            for t in range(2):
                put(a, o + t * 1024, win, r(0, 128) + 1696 + (2 * i + t) * 128, 8)
        for pr in range(4):
            o, _, sub = UOFF["MP%d" % pr]
            for kc in range(2):
                for hh in range(2):
                    h = 2 * pr + hh
                    base = o + sub["uq"] + ((kc * 2 + hh) * 2) * 96
                    a[:, base: base + 96] = wuq[kc * 128:(kc + 1) * 128, h * 96: h * 96 + 96]
                    base2 = base + 96
                    a[:, base2 + 64: base2 + 80] = wuq[kc * 128:(kc + 1) * 128, h * 96 + 80: h * 96 + 96]
                    a[:, base2 + 80: base2 + 96] = wuq[kc * 128:(kc + 1) * 128, h * 96 + 64: h * 96 + 80]
            for hh in range(2):
                h = 2 * pr + hh
                a[:, o + sub["uk"] + hh * 64: o + sub["uk"] + hh * 64 + 64] = wukv[:, h * 128: h * 128 + 64]
                a[:, o + sub["uv"] + hh * 64: o + sub["uv"] + hh * 64 + 64] = wukv[:, h * 128 + 64: h * 128 + 128]
        for gy in range(8):
            o, _, sub = UOFF["Y%d" % gy]
            put(a, o + sub["ma"], win, r(0, 128) + 2208 + gy * 128, 8)
            put(a, o + sub["mb"], win, r(0, 128) + 3232 + gy * 128, 8)
            put(a, o + sub["wa"], wa, r(0, 128) + gy * 128, 4)
            put(a, o + sub["wb"], wb, r(0, 128) + gy * 128, 4)
        for u in range(4):
            o, _, sub = UOFF["O%d" % u]
            for gl in range(2):
                put(a, o + gl * 1024, wo, r(0, 128) + (2 * u + gl) * 128, 8)
            o, _, sub = UOFF["G%d" % u]
            for gl in range(2):
                put(a, o + gl * 1280, wpg, r(0, 128) + (2 * u + gl) * 128, 8)
                put(a, o + gl * 1280 + 1024, wpp, r(0, 128) + (2 * u + gl) * 128, 2)
    return out


def host_consts():
    c = {}
    c["identf"] = np.eye(128, dtype=np.float32)
    s = np.arange(128)[:, None]
    q = np.arange(128)[None, :]
    c["maskneg"] = np.where(s <= q, 0.0, NEGM).astype(np.float32)
    mb = np.zeros((128, 2, 128), np.float32)
    mb[:, 0, :] = np.where(s > q, 0.0, BIGM)
    mb[:, 1, :] = np.where(s <= q, 0.0, BIGM)
    c["maskbig"] = mb
    sid = np.zeros((128, 8, 128), np.float32)
    for h in range(8):
        sid[:, h, :] = -8.0 * SLOPES[h] * np.eye(128, dtype=np.float32)
    c["sid"] = sid
    p = np.arange(128)
    inv = (10000.0 ** (-(np.arange(0, 32, 2, dtype=np.float32)) / 32.0)).astype(np.float32)
    sgn = np.where((p % 32) < 16, -1.0, 1.0).astype(np.float32)
    col = np.zeros((128, 8), np.float32)
    col[:, 0] = inv[p % 16]
    col[:, 1] = 2.0 * math.pi * sgn
    col[:, 4] = 2.0 * math.pi
    col[:, 5] = (inv[p % 16].astype(np.float64) / (2.0 * math.pi)).astype(np.float32)
    c["ropecol"] = col
    return c


A_X = 0
A_CS = 65536
A_SN = 69632
A_DM = 73728
A_HT = 81920
A_OA = 98304
A_OB = 114688
A_RR = 131072
RR_BYTES = 26624
A_KRF = A_RR + RR_BYTES
A_PTB = A_KRF + 4096
A_SCR = A_PTB + 6144
A_WSL = A_SCR + 16384
A_MPS = A_WSL + NWSLOT * WSLOT * 2
A_END = A_MPS + 2 * MPSLOT * 2
R_QTA = 0
R_KTA = 8192
R_VA = 12288
R_PTA = 18432
R_BIAS = 22528
R_KTB = 0
R_VB = 8192
R_QDN = 14336
R_KVDN = 22528
R_YT = 0
R_PTT = 16384
R_PSTG = 24576
R_XS = 0
R_POSI = 8192
R_POSF = 16384
R_XN = 8192
S_RS = 0
S_DN = 4096
S_2 = 8192


class _Cut(Exception):
    pass


def build_program(nseq=SPC, nlayers=DEPTH, cut_at=None, debug=False):
    nc = bass.Bass("TRN2", target_bir_lowering=False)

    def cut(n):
        if cut_at is not None and cut_at == n:
            raise _Cut()
    P = Prog(nc)

    def din(name, shape, dt=F32):
        return nc.dram_tensor(name, list(shape), dt, kind="ExternalInput")

    x_d = din("x", [SPC, S, D])
    p_d = din("p", [DEPTH, SPC, S, 256])
    pos_d = din("pos", [SPC, S], I32)
    wsrc_d = din("wsrc", [DEPTH, 128, WTOTAL + 16])
    gcol_d = din("gcol", [128, 64])
    sink_d = din("sink", [1, 16])
    identf_d = din("identf", [128, 128])
    maskneg_d = din("maskneg", [128, 128])
    maskbig_d = din("maskbig", [128, 2, 128])
    sid_d = din("sid", [128, 8, 128])
    ropecol_d = din("ropecol", [128, 8])
    out_d = nc.dram_tensor("out", [SPC, S, D], F32, kind="ExternalOutput")
    wscr_d = nc.dram_tensor("wscr", [DEPTH, 128, WTOTAL], BF16, kind="Internal")

    with contextlib.ExitStack() as st:
        def sb(name, shape, dt):
            return st.enter_context(nc.sbuf_tensor("sb_" + name, list(shape), dt))

        identf = sb("identf", [128, 128], F32)
        identb = sb("identb", [128, 128], BF16)
        onesb = sb("onesb", [128, 128], BF16)
        maskneg = sb("maskneg", [128, 128], BF16)
        maskbig = sb("maskbig", [128, 2, 128], F32)
        sidb = sb("sidb", [128, 8, 128], BF16)
        ropecol = sb("ropecol", [128, 8], F32)
        gcol = sb("gcol", [128, 64], F32)
        es = sb("es", [128, 16], F32)
        dummy = sb("dummy", [128, 8], F32)
        pki = sb("pki", [128, 16], I32)
        pkf = sb("pkf", [128, 16], F32)
        ARENA = sb("arena", [128, A_END // 2], BF16)
        A16 = ARENA
        A32 = ARENA[:, :].bitcast(F32).tensor
        AI32 = ARENA[:, :].bitcast(I32).tensor
        F16n = A_END // 2
        F32n = A_END // 4
        PS = st.enter_context(nc.psum_tensor("psall", [128, 4096], F32))

        class _Bank:
            def __init__(self, i):
                self.i = i

            def __getitem__(self, key):
                ps_, cs_ = key
                assert ps_ == slice(None)
                c0 = 0 if cs_.start is None else cs_.start
                c1 = 512 if cs_.stop is None else cs_.stop
                return bass.AP(PS, self.i * 512 + c0, [[4096, 128], [1, c1 - c0]])

        banks = [_Bank(i) for i in range(8)]

        def a16(byte_base, p0, pn, el_off, *dims):
            assert byte_base % 2 == 0
            return bass.AP(A16, p0 * F16n + byte_base // 2 + el_off, [[F16n, pn]] + [list(d_) for d_ in dims])

        def a32(byte_base, p0, pn, el_off, *dims):
            assert byte_base % 4 == 0
            return bass.AP(A32, p0 * F32n + byte_base // 4 + el_off, [[F32n, pn]] + [list(d_) for d_ in dims])

        def ai32(byte_base, p0, pn, el_off, *dims):
            return bass.AP(AI32, p0 * F32n + byte_base // 4 + el_off, [[F32n, pn]] + [list(d_) for d_ in dims])

        def tap(t, p0, pn, off, *dims):
            n = 1
            for d_ in list(t.shape)[1:]:
                n *= int(d_)
            return bass.AP(t, p0 * n + off, [[n, pn]] + [list(d_) for d_ in dims])

        def bkp(b, p0, pn, off, *dims):
            return bass.AP(PS, p0 * 4096 + b * 512 + off, [[4096, pn]] + [list(d_) for d_ in dims])

        def rk(byte_off, nbytes):
            lo = (A_RR + byte_off) // 1024
            hi = (A_RR + byte_off + nbytes - 1) // 1024
            return [("ar", i) for i in range(lo, hi + 1)]

        def sk(byte_off, nbytes):
            lo = (A_SCR + byte_off) // 1024
            hi = (A_SCR + byte_off + nbytes - 1) // 1024
            return [("ar", i) for i in range(lo, hi + 1)]

        def BK(i):
            return ("bank", i)

        def XK(kc, c):
            return ("X", kc, c)

        HTSEL = [0]

        def HK(kc, c):
            return ("HT", HTSEL[0], kc)

        def X_ap(kc, tok0, n):
            return a32(A_X, 0, 128, kc * S + tok0, [1, n])

        def HT_ap(kc, tok0, n):
            return a16(A_HT, 0, 128, HTSEL[0] * 4096 + kc * 512 + (tok0 % 512), [1, n])

        def OA_ap(t, tok0, n):
            return a16(A_OA, 0, 128, t * S + tok0, [1, n])

        def OB_ap(t, tok0, n):
            return a16(A_OB, 0, 128, t * S + tok0, [1, n])

        bank_rr = [0]

        def nbank(pool=None):
            pool = list(range(8)) if pool is None else list(pool)
            b = pool[bank_rr[0] % len(pool)]
            bank_rr[0] += 1
            return b

        evac_rr = [0]

        def evac_copy(out_ap, in_ap, reads, writes, eng=None):
            if eng is None:
                eng = "act" if (evac_rr[0] % 2 == 0) else "dve"
                evac_rr[0] += 1
            if eng == "act":
                P.op("act", lambda e: e.copy(out_ap, in_ap), reads=reads, writes=writes)
            else:
                P.op("dve", lambda e: e.tensor_copy(out_ap, in_ap), reads=reads, writes=writes)

        dbg_count = [0]

        def dbg(name, src_ap, shape, dt, reads):
            if not debug:
                return
            t = nc.dram_tensor("dbg_" + name, list(shape), dt, kind="ExternalOutput")
            P.op("sp", lambda e: e.dma_start(out=t.ap(), in_=src_ap), reads=reads, dma_key=("dbg", name))

        def mm(out_ap, lhsT, rhs, start, stop, reads, writes):
            P.op("pe", lambda e: e.matmul(out_ap, lhsT, rhs, start=start, stop=stop), reads=reads, writes=writes)

        class Stream:
            def __init__(self, base, nslot, slotsz, order, tag):
                self.base = base
                self.nslot = nslot
                self.slotsz = slotsz
                self.tag = tag
                self.seq = []
                self.pos = 0
                self.loc = {}
                self.order = order
                self.cnt = 0

            def plan(self, nseq_, nl):
                for s_ in range(nseq_):
                    for l in range(nl):
                        for nm in self.order:
                            self.seq.append((l, nm))

            def _emit_load(self, idx):
                l, nm = self.seq[idx]
                slot = idx % self.nslot
                off, size, _ = UOFF[nm]
                dst = a16(self.base, 0, 128, slot * self.slotsz, [1, size])
                src = bass.AP(wscr_d, l * 128 * WTOTAL + off, [[WTOTAL, 128], [1, size]])
                P.op("sp", lambda e: e.dma_start(out=dst, in_=src), reads=[("wscr", l, nm)],
                     writes=[(self.tag, slot)], dma_key=(self.tag, slot))
                self.loc[idx] = slot

            def get(self, l, nm, lookahead=2):
                idx = self.cnt
                assert self.seq[idx] == (l, nm), (self.seq[idx], l, nm)
                self.cnt += 1
                upto = min(len(self.seq), idx + 1 + lookahead, idx + self.nslot)
                while self.pos < upto:
                    self._emit_load(self.pos)
                    self.pos += 1
                slot = self.loc[idx]
                return slot, (self.tag, slot)

        WS = Stream(A_WSL, NWSLOT, WSLOT, MAIN_ORDER, "ws")
        MS = Stream(A_MPS, 2, MPSLOT, MP_ORDER, "mp")
        WS.plan(nseq, nlayers)
        MS.plan(nseq, nlayers)

        def wsl(slot, off, n):
            return a16(A_WSL, 0, 128, slot * WSLOT + off, [1, n])

        def mps(slot, off, n):
            return a16(A_MPS, 0, 128, slot * MPSLOT + off, [1, n])

        P.epoch = 0
        P.op("sp", lambda e: e.dma_start(out=identf[:], in_=identf_d.ap()), writes=["identf"], dma_key="c0")
        P.op("pool", lambda e: e.dma_start(out=identb[:], in_=identf_d.ap()), writes=["identb"], dma_key="c1")
        P.op("pool", lambda e: e.dma_start(out=maskneg[:], in_=maskneg_d.ap()), writes=["maskneg"], dma_key="c2")
        P.op("sp", lambda e: e.dma_start(out=maskbig[:], in_=maskbig_d.ap()), writes=["maskbig"], dma_key="c3")
        P.op("pool", lambda e: e.dma_start(out=sidb[:], in_=sid_d.ap()), writes=["sidb"], dma_key="c4")
        P.op("sp", lambda e: e.dma_start(out=ropecol[:], in_=ropecol_d.ap()), writes=["ropecol"], dma_key="c5")
        P.op("sp", lambda e: e.dma_start(out=gcol[:], in_=gcol_d.ap()), writes=["gcol"], dma_key="c6")
        P.op("sp", lambda e: e.dma_start(out=es[:], in_=sink_d.ap().partition_broadcast(128)), writes=["es"], dma_key="c8")
        P.op("pool", lambda e: e.memset(onesb[:], 1.0), writes=["onesb"])
        P.op("act", lambda e: e.activation(out=es[:], in_=es[:], func=AF.Exp), reads=["es"], writes=["es"])
        def emit_conv(l, groups=None, after=()):
            for gi, (ua, ub) in enumerate(CONV_GROUPS):
                if groups is not None and gi not in groups:
                    continue
                o0 = UOFF[ua][0]
                o1 = UOFF[ub][0] + UOFF[ub][1]
                names = [nm for nm, _, _ in UNITS if o0 <= UOFF[nm][0] < o1]
                pos_ = o0
                while pos_ < o1:
                    n = min(4096, o1 - pos_)
                    src = bass.AP(wsrc_d, l * 128 * (WTOTAL + 16) + pos_, [[WTOTAL + 16, 128], [1, n]])
                    dst = bass.AP(wscr_d, l * 128 * WTOTAL + pos_, [[WTOTAL, 128], [1, n]])
                    last = (pos_ + n >= o1)
                    P.op("pool", lambda e, src=src, dst=dst: e.dma_start(out=dst, in_=src),
                         writes=[("wscr", l, nm) for nm in names] if last else [], reads=list(after),
                         dma_key=("cv", l, gi))
                    pos_ += n

        emit_conv(0, groups=[0])
        out_ops = []
        if debug:
            tw = nc.dram_tensor("dbg_wscr", [128, WTOTAL], BF16, kind="ExternalOutput")
            P.op("sp", lambda e: e.dma_start(out=tw.ap(), in_=wscr_d.ap()[0]),
                 reads=[("wscr", 0, nm) for nm, _, _ in UNITS], dma_key=("dbg", "wscr"))

        def rmsnorm_chunk(c, gbase, dst_fn, dst_keys_fn):
            b = nbank()
            sqk = sk(S_2, 8192)
            sq_all = a16(A_SCR + S_2, 0, 128, 0, [CH, KC], [1, CH])
            P.op("act", lambda e: e.activation(out=sq_all, in_=a32(A_X, 0, 128, c * CH, [S, KC], [1, CH]), func=AF.Square),
                 reads=[XK(k, c) for k in range(KC)], writes=sqk)
            for kc in range(KC):
                mm(banks[b][:, :], onesb[:, :], a16(A_SCR + S_2, 0, 128, kc * CH, [1, CH]), kc == 0, kc == KC - 1,
                   reads=sqk + ["onesb"], writes=[BK(b)])
            rs = a32(A_SCR + S_RS, 0, 128, (c % 2) * CH, [1, CH])
            rkey = sk(S_RS + (c % 2) * 2048, 2048)
            P.op("act", lambda e: e.activation(out=rs, in_=banks[b][:, :], func=AF.Ln, bias=EPS, scale=1.0 / D),
                 reads=[BK(b)], writes=rkey)
            P.op("act", lambda e: e.activation(out=rs, in_=rs, func=AF.Exp, scale=-0.5), reads=rkey, writes=rkey)
            for kc in range(KC):
                dst = dst_fn(kc)
                P.op("dve", lambda e, kc=kc, dst=dst: e.scalar_tensor_tensor(
                    out=dst, in0=X_ap(kc, c * CH, CH), scalar=gcol[:, gbase + kc: gbase + kc + 1],
                    in1=rs, op0=ALU.mult, op1=ALU.mult),
                    reads=[XK(kc, c), "gcol"] + rkey, writes=dst_keys_fn(kc))

        def norm_A(c):
            sqk = sk(S_2, 8192)
            sq_all = a16(A_SCR + S_2, 0, 128, 0, [CH, KC], [1, CH])
            P.op("act", lambda e: e.activation(out=sq_all, in_=a32(A_X, 0, 128, c * CH, [S, KC], [1, CH]), func=AF.Square),
                 reads=[XK(k, c) for k in range(KC)], writes=sqk)

        def norm_B(c, gbase, buf):
            prev = HTSEL[0]
            HTSEL[0] = buf
            sqk = sk(S_2, 8192)
            b = nbank()
            for kc in range(KC):
                mm(banks[b][:, :], onesb[:, :], a16(A_SCR + S_2, 0, 128, kc * CH, [1, CH]), kc == 0, kc == KC - 1,
                   reads=sqk + ["onesb"], writes=[BK(b)])
            rs = a32(A_SCR + S_RS, 0, 128, (buf % 2) * CH, [1, CH])
            rkey = sk(S_RS + (buf % 2) * 2048, 2048)
            P.op("act", lambda e: e.activation(out=rs, in_=banks[b][:, :], func=AF.Ln, bias=EPS, scale=1.0 / D),
                 reads=[BK(b)], writes=rkey)
            P.op("act", lambda e: e.activation(out=rs, in_=rs, func=AF.Exp, scale=-0.5), reads=rkey, writes=rkey)
            for kc in range(KC):
                dst = HT_ap(kc, c * CH, CH)
                P.op("dve", lambda e, kc=kc, dst=dst: e.scalar_tensor_tensor(
                    out=dst, in0=X_ap(kc, c * CH, CH), scalar=gcol[:, gbase + kc: gbase + kc + 1],
                    in1=rs, op0=ALU.mult, op1=ALU.mult),
                    reads=[XK(kc, c), "gcol"] + rkey, writes=[HK(kc, c)])
            HTSEL[0] = prev

        def norm_to_HT(gbase, cq, buf=0):
            prev = HTSEL[0]
            HTSEL[0] = buf
            for c in (cq,):
                rmsnorm_chunk(c, gbase, lambda kc, c=c, d_=None: None, None) if False else None
                dsts = {kc: HT_ap(kc, c * CH, CH) for kc in range(KC)}
                keys = {kc: [HK(kc, c)] for kc in range(KC)}
                rmsnorm_chunk(c, gbase, lambda kc, dsts=dsts: dsts[kc], lambda kc, keys=keys: keys[kc])
            HTSEL[0] = prev

        def seq_setup(s):
            posi_k = rk(R_POSI, 8192)
            posf_k = rk(R_POSF, 8192)
            POSI = ai32(A_RR + R_POSI, 0, 128, 0, [1, S])
            POSF = a32(A_RR + R_POSF, 0, 128, 0, [1, S])
            P.op("sp", lambda e: e.dma_start(out=POSI, in_=pos_d.ap()[s:s + 1, :].partition_broadcast(128)),
                 writes=posi_k, dma_key="pos")
            for n in range(NB):
                src = bass.AP(pos_d, s * S + n * 128, [[1, 128], [1, 1]])
                P.op("sp", lambda e, n=n, src=src: e.dma_start(out=pki[:, n:n + 1], in_=src),
                     writes=["pki"], dma_key="pk")
            P.op("dve", lambda e: e.tensor_copy(POSF, POSI), reads=posi_k, writes=posf_k)
            P.op("dve", lambda e: e.tensor_copy(pkf[:], pki[:]), reads=["pki"], writes=["pkf"])
            r_k = [("OB", t_, c_) for t_ in range(2) for c_ in range(NCH)]
            ri_k = [("OB", t_, c_) for t_ in (2, 3) for c_ in range(NCH)]
            rf_k = sk(S_2, 8192)
            RV = a32(A_OB, 0, 128, 0, [1, S])
            RI = ai32(A_OB + 8192, 0, 128, 0, [1, S])
            RF = a32(A_SCR + S_2, 0, 128, 0, [1, S])
            for shift, dst_base, scol in ((0.0, A_SN, 1), (0.25, A_CS, 4)):
                P.op("dve", lambda e, shift=shift: e.tensor_scalar(out=RV, in0=POSF, scalar1=ropecol[:, 5:6], scalar2=shift,
                                                                   op0=ALU.mult, op1=ALU.add), reads=posf_k + ["ropecol"], writes=r_k)
                P.op("dve", lambda e: e.tensor_copy(RI, RV), reads=r_k, writes=ri_k)
                P.op("dve", lambda e: e.tensor_copy(RF, RI), reads=ri_k, writes=rf_k)
                P.op("dve", lambda e: e.tensor_tensor(out=RV, in0=RV, in1=RF, op=ALU.subtract), reads=r_k + rf_k, writes=r_k)
                P.op("dve", lambda e: e.tensor_scalar(out=RF, in0=RV, scalar1=0.5, scalar2=None, op0=ALU.is_ge), reads=r_k, writes=rf_k)
                P.op("dve", lambda e: e.tensor_tensor(out=RV, in0=RV, in1=RF, op=ALU.subtract), reads=r_k + rf_k, writes=r_k)
                P.op("dve", lambda e: e.tensor_scalar(out=RF, in0=RV, scalar1=-0.5, scalar2=None, op0=ALU.is_lt), reads=r_k, writes=rf_k)
                P.op("dve", lambda e: e.tensor_tensor(out=RV, in0=RV, in1=RF, op=ALU.add), reads=r_k + rf_k, writes=r_k)
                P.op("act", lambda e, dst_base=dst_base, scol=scol: e.activation(
                    out=a16(dst_base, 0, 128, 0, [1, S]), in_=RV, func=AF.Sin, scale=ropecol[:, scol:scol + 1]),
                    reads=r_k + ["ropecol"], writes=["TAB"])
            for blk in range(NB):
                buf = blk % 2
                xk = rk(R_XS + buf * 4096, 4096)
                P.op("sp", lambda e, blk=blk, buf=buf: e.dma_start(
                    out=a32(A_RR + R_XS, 0, 128, buf * 1024, [1, 1024]), in_=x_d.ap()[s, blk * 128:(blk + 1) * 128, :]),
                    writes=xk, dma_key=("xs", buf))
                for half in range(2):
                    b = nbank()
                    for j in range(4):
                        kc = half * 4 + j
                        P.op("pe", lambda e, b=b, j=j, kc=kc, buf=buf: e.transpose(
                            banks[b][:, j * 128:(j + 1) * 128], a32(A_RR + R_XS, 0, 128, buf * 1024 + kc * 128, [1, 128]),
                            identf[:, :]), reads=xk + ["identf"], writes=[BK(b)])
                    dst = a32(A_X, 0, 128, half * 4 * S + blk * 128, [S, 4], [1, 128])
                    src = bkp(b, 0, 128, 0, [128, 4], [1, 128])
                    evac_copy(dst, src, reads=[BK(b)], writes=[XK(half * 4 + j, blk // 4) for j in range(4)])
            for n in range(NB):
                for w in range(2):
                    j = n - 1 + w
                    if j < 0:
                        continue
                    tb = (n * 2 + w) % 2
                    tmp = a32(A_SCR + S_DN, 0, 128, tb * CH, [1, 128])
                    tk = sk(S_DN + tb * 2048, 512)
                    P.op("pool", lambda e, n=n, j=j, tmp=tmp: e.tensor_scalar(
                        out=tmp, in0=a32(A_RR + R_POSF, 0, 128, n * 128, [1, 128]), scalar1=pkf[:, j:j + 1], scalar2=None,
                        op0=ALU.subtract), reads=posf_k + ["pkf"], writes=tk)
                    P.op("pool", lambda e, n=n, w=w, tmp=tmp: e.tensor_tensor(
                        out=a16(A_DM, 0, 128, (n * 2 + w) * 128, [1, 128]), in0=tmp, in1=maskbig[:, w, :], op=ALU.add),
                        reads=tk + ["maskbig"], writes=[("DM", n)])

        dn_rr = [0]

        def normalise(be, bo, ncols, gate_fn, out_fn, gate_keys, out_keys, sink_cols=None):
            buf = dn_rr[0] % 2
            dn_rr[0] += 1
            dnk = sk(S_DN + buf * 2048, 2048)

            def dn(p0, pn):
                return a32(A_SCR + S_DN, p0, pn, buf * CH, [1, ncols])
            if sink_cols is None:
                P.op("act", lambda e: e.activation(out=dn(0, 64), in_=bkp(be, 64, 64, 0, [1, ncols]), func=AF.Ln),
                     reads=[BK(be)], writes=dnk)
                P.op("act", lambda e: e.activation(out=dn(64, 64), in_=bkp(bo, 0, 64, 0, [1, ncols]), func=AF.Ln),
                     reads=[BK(bo)], writes=dnk)
            else:
                for k in range(ncols // 128):
                    he, ho = sink_cols[0][k], sink_cols[1][k]
                    P.op("act", lambda e, k=k, he=he: e.activation(
                        out=a32(A_SCR + S_DN, 0, 64, buf * CH + k * 128, [1, 128]), in_=bkp(be, 64, 64, k * 128, [1, 128]),
                        func=AF.Ln, bias=tap(es, 64, 64, he, [1, 1])), reads=[BK(be), "es"], writes=dnk)
                    P.op("act", lambda e, k=k, ho=ho: e.activation(
                        out=a32(A_SCR + S_DN, 64, 64, buf * CH + k * 128, [1, 128]), in_=bkp(bo, 0, 64, k * 128, [1, 128]),
                        func=AF.Ln, bias=tap(es, 0, 64, ho, [1, 1])), reads=[BK(bo), "es"], writes=dnk)
            P.op("act", lambda e: e.activation(out=dn(0, 128), in_=dn(0, 128), func=AF.Exp, scale=-1.0), reads=dnk, writes=dnk)
            P.op("pool", lambda e: e.tensor_tensor(out=dn(0, 128), in0=dn(0, 128), in1=gate_fn(0, 128), op=ALU.mult),
                 reads=dnk + gate_keys, writes=dnk)
            P.op("dve", lambda e: e.tensor_tensor(out=out_fn(0, 64), in0=bkp(be, 0, 64, 0, [1, ncols]), in1=dn(0, 64), op=ALU.mult),
                 reads=[BK(be)] + dnk, writes=out_keys)
            P.op("dve", lambda e: e.tensor_tensor(out=out_fn(64, 64), in0=bkp(bo, 64, 64, 0, [1, ncols]), in1=dn(64, 64), op=ALU.mult),
                 reads=[BK(bo)] + dnk, writes=out_keys)

        def layer(s, l):
            gmix0 = l * 8
            gple0 = 16 + l * 8
            cut(2)
            def proj_A(g, cq):
                hcs = (cq,)
                sa, ka = WS.get(l, "A%da" % g)
                for pr in range(2):
                    for c in hcs:
                        b = nbank()
                        for kc in range(KC):
                            mm(banks[b][:, :], wsl(sa, (pr * 8 + kc) * 128, 128), HT_ap(kc, c * CH, CH),
                               kc == 0, kc == KC - 1, reads=[ka, HK(kc, c)], writes=[BK(b)])
                        evac_copy(a16(A_RR + R_QTA, 0, 128, pr * S + c * CH, [1, CH]), banks[b][:, :],
                                  reads=[BK(b)], writes=rk(R_QTA + (pr * S + c * CH) * 2, 1024))
                for c in hcs:
                    b = nbank()
                    for kc in range(KC):
                        mm(banks[b][:, :], wsl(sa, 2048 + kc * 128, 128), HT_ap(kc, c * CH, CH),
                           kc == 0, kc == KC - 1, reads=[ka, HK(kc, c)], writes=[BK(b)])
                    evac_copy(a16(A_RR + R_KTA, 0, 128, c * CH, [1, CH]), banks[b][:, :],
                              reads=[BK(b)], writes=rk(R_KTA + c * CH * 2, 1024))
                sbb, kb = WS.get(l, "A%db" % g)
                for pr in range(2):
                    for c in hcs:
                        b = nbank()
                        for kc in range(KC):
                            mm(banks[b][:, :], wsl(sbb, (pr * 8 + kc) * 128, 128), HT_ap(kc, c * CH, CH),
                               kc == 0, kc == KC - 1, reads=[kb, HK(kc, c)], writes=[BK(b)])
                        t = 2 * g + pr
                        P.op("act", lambda e, b=b, t=t, c=c: e.activation(out=OA_ap(t, c * CH, CH), in_=banks[b][:, :], func=AF.Silu),
                             reads=[BK(b)], writes=[("OA", t, c)])
                if cq == 0:
                    vak = rk(R_VA, NB * 192 * 2)
                    P.op("pool", lambda e: e.memset(a16(A_RR + R_VA, 0, 128, 0, [192, NB], [1, 64]), 1.0), writes=vak)
                    P.op("pool", lambda e: e.memset(a16(A_RR + R_VA, 0, 128, 128, [192, NB], [1, 64]), 1.0), writes=vak)
                for q4 in hcs:
                    b = nbank()
                    for j in range(4):
                        blk = q4 * 4 + j
                        for kc in range(KC):
                            mm(bkp(b, 0, 128, j * 64, [1, 64]), HT_ap(kc, blk * 128, 128),
                               wsl(sbb, 2048 + kc * 64, 64), kc == 0, kc == KC - 1,
                               reads=[kb, HK(kc, q4)], writes=[BK(b)])
                    evac_copy(a16(A_RR + R_VA, 0, 128, q4 * 4 * 192 + 64, [192, 4], [1, 64]),
                              bkp(b, 0, 128, 0, [64, 4], [1, 64]), reads=[BK(b)],
                              writes=rk(R_VA + q4 * 4 * 192 * 2, 4 * 192 * 2))

            def attention_A(g):
                cut(3)

                def swa_scores(n, g=g):
                    buf = n % 2
                    sb0 = 0 if n % 2 == 0 else 2
                    ws_valid = [w for w in range(2) if n - 1 + w >= 0]
                    first = [True, True]
                    for w in ws_valid:
                        j = n - 1 + w
                        for hl in range(4):
                            par = hl % 2
                            jj = hl // 2
                            r0 = par * 64
                            mm(bkp(sb0 + par, 0, 128, (w * 2 + jj) * 128, [1, 128]),
                               a16(A_RR + R_KTA, r0, 64, j * 128, [1, 128]),
                               a16(A_RR + R_QTA, r0, 64, jj * S + n * 128, [1, 128]),
                               first[par], False,
                               reads=rk(R_KTA + j * 256, 256) + rk(R_QTA + (jj * S + n * 128) * 2, 256), writes=[BK(sb0 + par)])
                            first[par] = False
                    for w in ws_valid:
                        for hl in range(4):
                            par = hl % 2
                            jj = hl // 2
                            mm(bkp(sb0 + par, 0, 128, (w * 2 + jj) * 128, [1, 128]), tap(sidb, 0, 128, (4 * g + hl) * 128, [1, 128]),
                               a16(A_DM, 0, 128, (n * 2 + w) * 128, [1, 128]),
                               False, (w == ws_valid[-1] and jj == 1), reads=["sidb", ("DM", n)], writes=[BK(sb0 + par)])
                    w0 = ws_valid[0]
                    nw = len(ws_valid)
                    P.op("act", lambda e: e.activation(
                        out=a16(A_RR + R_PTA + buf * 2048, 0, 128, w0 * 256, [512, 2], [1, nw * 256]),
                        in_=bkp(sb0, 0, 128, w0 * 256, [512, 2], [1, nw * 256]), func=AF.Exp, scale=0.125),
                        reads=[BK(sb0), BK(sb0 + 1)], writes=rk(R_PTA + buf * 2048, 2048))

                def swa_pv(n, g=g):
                    buf = n % 2
                    ws_valid = [w for w in range(2) if n - 1 + w >= 0]
                    be = 4 + (n % 2) * 2
                    bo = be + 1
                    for par, bacc in ((0, be), (1, bo)):
                        for w in ws_valid:
                            j = n - 1 + w
                            lhs = a16(A_RR + R_VA, 0, 128, j * 192 + (64 if par == 0 else 0), [1, 128])
                            rhs = a16(A_RR + R_PTA + buf * 2048, 0, 128, par * 512 + w * 256, [1, 256])
                            mm(bkp(bacc, 0, 128, 0, [1, 256]), lhs, rhs, w == ws_valid[0], w == ws_valid[-1],
                               reads=rk(R_VA + j * 384, 384) + rk(R_PTA + buf * 2048, 2048), writes=[BK(bacc)])
                    c = n // 4
                    oak = [("OA", 2 * g, c), ("OA", 2 * g + 1, c)]
                    normalise(be, bo, 256,
                              lambda p0, pn, n=n, g=g: a16(A_OA, p0, pn, 2 * g * S + n * 128, [S, 2], [1, 128]),
                              lambda p0, pn, n=n, g=g: a16(A_OA, p0, pn, 2 * g * S + n * 128, [S, 2], [1, 128]),
                              oak, oak,
                              sink_cols=([l * 8 + 4 * g + 0, l * 8 + 4 * g + 2], [l * 8 + 4 * g + 1, l * 8 + 4 * g + 3]))

                swa_scores(0)
                for n in range(NB):
                    if n + 1 < NB:
                        swa_scores(n + 1)
                    swa_pv(n)

            def proj_B(cq):
                hcs = (cq,)
                s1a, k1a = WS.get(l, "B1a")
                s1b, k1b = WS.get(l, "B1b", lookahead=1)
                gq0 = 40 + l * 2
                gkv0 = 44 + l
                for c in hcs:
                    qb = []
                    for gq in range(2):
                        b = nbank()
                        qb.append(b)
                        for kc in range(KC):
                            mm(banks[b][:, :], wsl(s1a, (gq * 8 + kc) * 128, 128), HT_ap(kc, c * CH, CH),
                               kc == 0, kc == KC - 1, reads=[k1a, HK(kc, c)], writes=[BK(b)])
                        P.op("act", lambda e, b=b, gq=gq: e.activation(out=a16(A_RR + 0, 0, 128, gq * CH, [1, CH]),
                                                                       in_=banks[b][:, :], func=AF.Square),
                             reads=[BK(b)], writes=rk(0 + gq * 1024, 1024))
                    bkv = nbank()
                    for kc in range(KC):
                        mm(banks[bkv][:, :], wsl(s1b, kc * 128, 128), HT_ap(kc, c * CH, CH),
                           kc == 0, kc == KC - 1, reads=[k1b, HK(kc, c)], writes=[BK(bkv)])
                    P.op("act", lambda e, b=bkv: e.activation(out=a16(A_RR + 0, 0, 128, 2 * CH, [1, CH]),
                                                              in_=banks[b][:, :], func=AF.Square),
                         reads=[BK(bkv)], writes=rk(0 + 2048, 1024))
                    bs1 = nbank()
                    for gq in range(2):
                        mm(banks[bs1][:, :], onesb[:, :], a16(A_RR + 0, 0, 128, gq * CH, [1, CH]), gq == 0, gq == 1,
                           reads=rk(0 + gq * 1024, 1024) + ["onesb"], writes=[BK(bs1)])
                    bs2 = nbank()
                    mm(banks[bs2][:, :], onesb[:, :], a16(A_RR + 0, 0, 128, 2 * CH, [1, CH]), True, True,
                       reads=rk(0 + 2048, 1024) + ["onesb"], writes=[BK(bs2)])
                    rsq = a32(A_RR + 0 + 4096, 0, 128, 0, [1, CH])
                    rsk = a32(A_RR + 0 + 6144, 0, 128, 0, [1, CH])
                    rsqk = rk(0 + 4096, 2048)
                    rskk = rk(0 + 6144, 2048)
                    P.op("act", lambda e, b=bs1: e.activation(out=rsq, in_=banks[b][:, :], func=AF.Ln, bias=EPS, scale=1.0 / 256),
                         reads=[BK(bs1)], writes=rsqk)
                    P.op("act", lambda e: e.activation(out=rsq, in_=rsq, func=AF.Exp, scale=-0.5), reads=rsqk, writes=rsqk)
                    P.op("act", lambda e, b=bs2: e.activation(out=rsk, in_=banks[b][:, :], func=AF.Ln, bias=EPS, scale=1.0 / 128),
                         reads=[BK(bs2)], writes=rskk)
                    P.op("act", lambda e: e.activation(out=rsk, in_=rsk, func=AF.Exp, scale=-0.5), reads=rskk, writes=rskk)
                    for gq in range(2):
                        P.op("dve", lambda e, gq=gq, b=qb[gq], c=c: e.scalar_tensor_tensor(
                            out=a16(A_RR + R_QDN, 0, 128, gq * S + c * CH, [1, CH]), in0=banks[b][:, :],
                            scalar=gcol[:, gq0 + gq: gq0 + gq + 1], in1=rsq, op0=ALU.mult, op1=ALU.mult),
                            reads=[BK(qb[gq]), "gcol"] + rsqk, writes=rk(R_QDN + (gq * S + c * CH) * 2, 1024))
                    P.op("dve", lambda e, b=bkv, c=c: e.scalar_tensor_tensor(
                        out=a16(A_RR + R_KVDN, 0, 128, c * CH, [1, CH]), in0=banks[b][:, :],
                        scalar=gcol[:, gkv0: gkv0 + 1], in1=rsk, op0=ALU.mult, op1=ALU.mult),
                        reads=[BK(bkv), "gcol"] + rskk, writes=rk(R_KVDN + c * CH * 2, 1024))
                    bka = nbank()
                    bkb = nbank()
                    for var, b in ((0, bka), (1, bkb)):
                        for kc in range(KC):
                            mm(bkp(b, 0, 32, 0, [1, CH]), wsl(s1b, 1024 + kc * 64 + var * 32, 32),
                               HT_ap(kc, c * CH, CH), kc == 0, kc == KC - 1,
                               reads=[k1b, HK(kc, c)], writes=[BK(b)])
                    t1 = a32(A_SCR + S_DN, 0, 32, 0, [1, CH])
                    t2 = a32(A_SCR + S_DN, 0, 32, CH, [1, CH])
                    t1k = sk(S_DN, 2048)
                    t2k = sk(S_DN + 2048, 2048)
                    P.op("dve", lambda e, b=bka, c=c: e.tensor_tensor(out=t1, in0=bkp(b, 0, 32, 0, [1, CH]),
                                                                     in1=a16(A_CS, 0, 32, c * CH, [1, CH]), op=ALU.mult),
                         reads=[BK(bka), "TAB"], writes=t1k)
                    P.op("dve", lambda e, b=bkb, c=c: e.tensor_tensor(out=t2, in0=bkp(b, 0, 32, 0, [1, CH]),
                                                                     in1=a16(A_SN, 0, 32, c * CH, [1, CH]), op=ALU.mult),
                         reads=[BK(bkb), "TAB"], writes=t2k)
                    P.op("pool", lambda e, c=c: e.tensor_tensor(out=a16(A_KRF, 0, 32, c * CH, [1, CH]), in0=t1, in1=t2, op=ALU.add),
                         reads=t1k + t2k, writes=[("KRF", c)])
                for i, nm in enumerate(["B2a", "B2b"]):
                    s2, k2 = WS.get(l, nm)
                    for tt in range(2):
                        t = 2 * i + tt
                        for c in hcs:
                            b = nbank()
                            for kc in range(KC):
                                mm(banks[b][:, :], wsl(s2, (tt * 8 + kc) * 128, 128), HT_ap(kc, c * CH, CH),
                                   kc == 0, kc == KC - 1, reads=[k2, HK(kc, c)], writes=[BK(b)])
                            P.op("act", lambda e, b=b, t=t, c=c: e.activation(out=OB_ap(t, c * CH, CH), in_=banks[b][:, :], func=AF.Silu),
                                 reads=[BK(b)], writes=[("OB", t, c)])

            def mla_pairs():
                if s == 0 and l == 0 and nlayers > 1:
                    emit_conv(1)
                P.op("pool", lambda e: e.memset(a16(A_RR + R_VB, 0, 128, 64, [192, NB], [1, 64]), 1.0),
                     writes=rk(R_VB, NB * 192 * 2))
                if s == 0 and l == 0:
                    dbg("QDN", a16(A_RR + R_QDN, 0, 128, 0, [S, 2], [1, S]), [128, 2, S], BF16, rk(R_QDN, 8192))
                    dbg("KVDN", a16(A_RR + R_KVDN, 0, 128, 0, [1, S]), [128, S], BF16, rk(R_KVDN, 4096))
                    dbg("KRF", a16(A_KRF, 0, 128, 0, [1, S]), [128, S], BF16, [("KRF", c_) for c_ in range(NCH)])
                    dbg("GB", a16(A_OB, 0, 128, 0, [S, 4], [1, S]), [128, 4, S], BF16, [("OB", t_, c_) for t_ in range(4) for c_ in range(NCH)])
                cut(5)
                for pair in range(4):
                    sm, km = MS.get(l, "MP%d" % pair, lookahead=1)
                    for c in range(NCH):
                        for hh in range(2):
                            b = nbank(range(4))
                            mm(bkp(b, 0, 64, 0, [1, CH]), mps(sm, 768 + hh * 64, 64), a16(A_RR + R_KVDN, 0, 128, c * CH, [1, CH]),
                               True, True, reads=[km] + rk(R_KVDN + c * CH * 2, 1024), writes=[BK(b)])
                            evac_copy(a16(A_RR + R_KTB, 0, 64, hh * S + c * CH, [1, CH]), bkp(b, 0, 64, 0, [1, CH]),
                                      reads=[BK(b)], writes=rk(R_KTB + (hh * S + c * CH) * 2, 1024))
                    P.op("sp", lambda e: e.dma_start(out=a16(A_RR + R_KTB, 64, 32, 0, [S, 2], [1, S]),
                                                     in_=a16(A_KRF, 0, 32, 0, [0, 2], [1, S])),
                         reads=[("KRF", c) for c in range(NCH)], writes=rk(R_KTB, 8192), dma_key="krep")
                    for q4 in range(4):
                        b = nbank(range(4))
                        for j in range(4):
                            blk = q4 * 4 + j
                            mm(bkp(b, 0, 128, j * 128, [1, 128]), a16(A_RR + R_KVDN, 0, 128, blk * 128, [1, 128]),
                               mps(sm, 896, 128), True, True, reads=[km] + rk(R_KVDN + blk * 256, 256), writes=[BK(b)])
                        evac_copy(a16(A_RR + R_VB, 0, 128, q4 * 4 * 192, [192, 4], [128, 2], [1, 64]),
                                  bkp(b, 0, 128, 0, [128, 4], [64, 2], [1, 64]),
                                  reads=[BK(b)], writes=rk(R_VB + q4 * 4 * 384, 4 * 384))

                    def emit_q(c, pair=pair, sm=sm, km=km):
                        qbuf = c % 2
                        for hh in range(2):
                            bp = nbank(range(4))
                            bs_ = nbank(range(4))
                            for var, b in ((0, bp), (1, bs_)):
                                for kc in range(2):
                                    mm(bkp(b, 0, 96, 0, [1, CH]), mps(sm, ((kc * 2 + hh) * 2 + var) * 96, 96),
                                       a16(A_RR + R_QDN, 0, 128, kc * S + c * CH, [1, CH]), kc == 0, kc == 1,
                                       reads=[km] + rk(R_QDN + (kc * S + c * CH) * 2, 1024), writes=[BK(b)])
                            qslot = qbuf * 2 + hh
                            qk = [("HT", 1, qslot)]
                            P.op("act", lambda e, bp=bp, qslot=qslot: e.copy(a16(A_HT, 0, 64, 4096 + qslot * CH, [1, CH]), bkp(bp, 0, 64, 0, [1, CH])),
                                 reads=[BK(bp)], writes=qk)
                            t1 = a32(A_KRF, 64, 32, 0, [1, CH])
                            t2 = a32(A_KRF, 64, 32, CH, [1, CH])
                            t1k = [("T1q",)]
                            t2k = [("T2q",)]
                            P.op("dve", lambda e, bp=bp: e.tensor_tensor(out=t1, in0=bkp(bp, 64, 32, 0, [1, CH]),
                                                                        in1=a16(A_CS, 64, 32, c * CH, [1, CH]), op=ALU.mult),
                                 reads=[BK(bp), "TAB"], writes=t1k)
                            P.op("dve", lambda e, bs_=bs_: e.tensor_tensor(out=t2, in0=bkp(bs_, 64, 32, 0, [1, CH]),
                                                                          in1=a16(A_SN, 64, 32, c * CH, [1, CH]), op=ALU.mult),
                                 reads=[BK(bs_), "TAB"], writes=t2k)
                            P.op("pool", lambda e, qslot=qslot: e.tensor_tensor(out=a16(A_HT, 64, 32, 4096 + qslot * CH, [1, CH]), in0=t1, in1=t2, op=ALU.add),
                                 reads=t1k + t2k, writes=qk)

                    emit_q(0)
                    pt_rr = [0]
                    for c in range(NCH):
                        qbuf = c % 2
                        te = 4 + 2 * (c % 2)
                        to = 5 + 2 * (c % 2)
                        steps = [(j, hh) for j in range(4 * c + 4) for hh in range(2)]
                        state = {}

                        def emit_s(idx, c=c, qbuf=qbuf, steps=steps, state=state):
                            j, hh = steps[idx]
                            i = j - 4 * c
                            col0 = 128 * i if i > 0 else 0
                            diag = i >= 0
                            ncol = CH - col0
                            b = nbank(range(4))
                            pb = pt_rr[0] % 6
                            pt_rr[0] += 1
                            state[idx] = (b, pb, col0, ncol)
                            qslot = qbuf * 2 + hh
                            mm(bkp(b, 0, 128, col0, [1, ncol]), a16(A_RR + R_KTB, 0, 96, hh * S + j * 128, [1, 128]),
                               a16(A_HT, 0, 96, 4096 + qslot * CH + col0, [1, ncol]), True, not diag,
                               reads=rk(R_KTB + (hh * S + j * 128) * 2, 256) + [("HT", 1, qslot)], writes=[BK(b)])
                            if diag:
                                mm(bkp(b, 0, 128, col0, [1, 128]), identb[:, :], maskneg[:, :], False, True,
                                   reads=["identb", "maskneg"], writes=[BK(b)])
                            P.op("act", lambda e: e.activation(out=a16(A_PTB, 0, 128, pb * CH + col0, [1, ncol]),
                                                               in_=bkp(b, 0, 128, col0, [1, ncol]), func=AF.Exp, scale=MLA_SCALE),
                                 reads=[BK(b)], writes=[("PTB", pb)])

                        def emit_pv(idx, c=c, te=te, to=to, steps=steps, state=state):
                            j, hh = steps[idx]
                            b, pb, col0, ncol = state[idx]
                            acc = te if hh == 0 else to
                            lhs = a16(A_RR + R_VB, 0, 128, j * 192 + (0 if hh == 0 else 64), [1, 128])
                            mm(bkp(acc, 0, 128, col0, [1, ncol]), lhs, a16(A_PTB, 0, 128, pb * CH + col0, [1, ncol]),
                               j == 0, j == 4 * c + 3, reads=rk(R_VB + j * 384, 384) + [("PTB", pb)], writes=[BK(acc)])

                        LA = 2
                        for k in range(min(LA, len(steps))):
                            emit_s(k)
                        for idx in range(len(steps)):
                            if idx + LA < len(steps):
                                emit_s(idx + LA)
                            emit_pv(idx)
                            if idx == len(steps) // 2 and c + 1 < NCH:
                                emit_q(c + 1)
                        obk = [("OB", pair, c)]
                        normalise(te, to, CH,
                                  lambda p0, pn, c=c, pair=pair: a16(A_OB, p0, pn, pair * S + c * CH, [1, CH]),
                                  lambda p0, pn, c=c, pair=pair: a16(A_OB, p0, pn, pair * S + c * CH, [1, CH]),
                                  obk, obk)
                if s == 0 and l == 0:
                    dbg("OB", a16(A_OB, 0, 128, 0, [S, 4], [1, S]), [128, 4, S], BF16, [("OB", t_, c_) for t_ in range(4) for c_ in range(NCH)])

            def prep_ptt():
                for blk in range(NB):
                    buf = blk % 2
                    pk_ = rk(R_PSTG + buf * 1024, 1024)
                    P.op("sp", lambda e, blk=blk, buf=buf: e.dma_start(
                        out=a32(A_RR + R_PSTG, 0, 128, buf * 256, [1, 256]), in_=p_d.ap()[l, s, blk * 128:(blk + 1) * 128, :]),
                        writes=pk_, dma_key=("ps", buf))
                    b = nbank()
                    for kc in range(2):
                        P.op("pe", lambda e, b=b, kc=kc, buf=buf: e.transpose(
                            banks[b][:, kc * 128:(kc + 1) * 128], a32(A_RR + R_PSTG, 0, 128, buf * 256 + kc * 128, [1, 128]),
                            identf[:, :]), reads=pk_ + ["identf"], writes=[BK(b)])
                    evac_copy(a16(A_RR + R_PTT, 0, 128, blk * 128, [S, 2], [1, 128]),
                              bkp(b, 0, 128, 0, [128, 2], [1, 128]), reads=[BK(b)],
                              writes=rk(R_PTT + blk * 256, 256) + rk(R_PTT + (S + blk * 128) * 2, 256))

            def proj_YO(cq):
                hcs = (cq,)
                for gy in range(8):
                    sy, ky = WS.get(l, "Y%d" % gy)
                    for c in hcs:
                        bma, bmb, bya, byb = nbank(), nbank(), nbank(), nbank()
                        for kc in range(KC):
                            mm(banks[bma][:, :], wsl(sy, kc * 128, 128), HT_ap(kc, c * CH, CH),
                               kc == 0, kc == KC - 1, reads=[ky, HK(kc, c)], writes=[BK(bma)])
                        for kc in range(KC):
                            mm(banks[bmb][:, :], wsl(sy, 1024 + kc * 128, 128), HT_ap(kc, c * CH, CH),
                               kc == 0, kc == KC - 1, reads=[ky, HK(kc, c)], writes=[BK(bmb)])
                        for kc in range(4):
                            mm(banks[bya][:, :], wsl(sy, 2048 + kc * 128, 128), OA_ap(kc, c * CH, CH),
                               kc == 0, kc == 3, reads=[ky, ("OA", kc, c)], writes=[BK(bya)])
                        for kc in range(4):
                            mm(banks[byb][:, :], wsl(sy, 2560 + kc * 128, 128), OB_ap(kc, c * CH, CH),
                               kc == 0, kc == 3, reads=[ky, ("OB", kc, c)], writes=[BK(byb)])
                        pb = (gy * NCH + c) % 2
                        sa_ = a32(A_RR + 8192, 0, 128, pb * 1024, [1, CH])
                        sb_ = a32(A_RR + 8192, 0, 128, pb * 1024 + CH, [1, CH])
                        sak = rk(8192 + pb * 4096, 2048)
                        sbk = rk(8192 + pb * 4096 + 2048, 2048)
                        P.op("act", lambda e, b=bma, sa_=sa_: e.activation(out=sa_, in_=banks[b][:, :], func=AF.Sigmoid),
                             reads=[BK(bma)], writes=sak)
                        P.op("act", lambda e, b=bmb, sb_=sb_: e.activation(out=sb_, in_=banks[b][:, :], func=AF.Sigmoid),
                             reads=[BK(bmb)], writes=sbk)
                        P.op("dve", lambda e, b=bya, sa_=sa_: e.tensor_tensor(out=sa_, in0=banks[b][:, :], in1=sa_, op=ALU.mult),
                             reads=[BK(bya)] + sak, writes=sak)
                        P.op("dve", lambda e, b=byb, sb_=sb_: e.tensor_tensor(out=sb_, in0=banks[b][:, :], in1=sb_, op=ALU.mult),
                             reads=[BK(byb)] + sbk, writes=sbk)
                        P.op("dve", lambda e, gy=gy, c=c, sa_=sa_, sb_=sb_: e.tensor_tensor(
                            out=a16(A_RR + R_YT, 0, 128, gy * 512, [1, CH]), in0=sa_, in1=sb_, op=ALU.add),
                            reads=sak + sbk, writes=rk(R_YT + (gy * 512) * 2, 1024))
                for u in range(4):
                    so, ko = WS.get(l, "O%d" % u)
                    for gl in range(2):
                        gx = 2 * u + gl
                        for c in hcs:
                            b = nbank()
                            for kc in range(KC):
                                mm(banks[b][:, :], wsl(so, (gl * 8 + kc) * 128, 128), a16(A_RR + R_YT, 0, 128, kc * 512, [1, CH]),
                                   kc == 0, kc == KC - 1, reads=[ko] + rk(R_YT + (kc * 512) * 2, 1024), writes=[BK(b)])
                            xa = X_ap(gx, c * CH, CH)
                            P.op("dve", lambda e, b=b, xa=xa: e.tensor_tensor(out=xa, in0=xa, in1=banks[b][:, :], op=ALU.add),
                                 reads=[BK(b), XK(gx, c)], writes=[XK(gx, c)])

            def proj_G(cq):
                hcs = (cq,)
                for u in range(4):
                    sg, kg = WS.get(l, "G%d" % u)
                    for gl in range(2):
                        gx = 2 * u + gl
                        for c in hcs:
                            bg, bp_ = nbank(), nbank()
                            for kc in range(KC):
                                mm(banks[bg][:, :], wsl(sg, gl * 1280 + kc * 128, 128), HT_ap(kc, c * CH, CH),
                                   kc == 0, kc == KC - 1, reads=[kg, HK(kc, c)], writes=[BK(bg)])
                            for kc in range(2):
                                mm(banks[bp_][:, :], wsl(sg, gl * 1280 + 1024 + kc * 128, 128),
                                   a16(A_RR + R_PTT, 0, 128, kc * S + c * CH, [1, CH]),
                                   kc == 0, kc == 1, reads=[kg] + rk(R_PTT + (kc * S + c * CH) * 2, 1024), writes=[BK(bp_)])
                            pb = (gx * NCH + c) % 2
                            sg_ = a32(A_RR + 8192, 0, 128, pb * 1024, [1, CH])
                            sgk = rk(8192 + pb * 4096, 2048)
                            P.op("act", lambda e, b=bg, sg_=sg_: e.activation(out=sg_, in_=banks[b][:, :], func=AF.Sigmoid),
                                 reads=[BK(bg)], writes=sgk)
                            P.op("dve", lambda e, b=bp_, sg_=sg_: e.tensor_tensor(out=sg_, in0=banks[b][:, :], in1=sg_, op=ALU.mult),
                                 reads=[BK(bp_)] + sgk, writes=sgk)
                            xa = X_ap(gx, c * CH, CH)
                            P.op("dve", lambda e, xa=xa, sg_=sg_: e.tensor_tensor(out=xa, in0=xa, in1=sg_, op=ALU.add),
                                 reads=sgk + [XK(gx, c)], writes=[XK(gx, c)])


            return dict(proj_A=proj_A, attention_A=attention_A, proj_B=proj_B, mla_pairs=mla_pairs,
                        prep_ptt=prep_ptt, proj_YO=proj_YO, proj_G=proj_G)

        def final(s):
            for c in range(NCH):
                rmsnorm_chunk(c, 32,
                              lambda kc: a32(A_RR + R_XN, 0, 128, kc * CH, [1, CH]),
                              lambda kc: rk(R_XN + kc * 2048, 2048))
                for bl in range(4):
                    blk = c * 4 + bl
                    buf = blk % 2
                    xk = rk(R_XS + buf * 4096, 4096)
                    for half in range(2):
                        b = nbank()
                        for j in range(4):
                            kc = half * 4 + j
                            P.op("pe", lambda e, b=b, j=j, kc=kc, bl=bl: e.transpose(
                                banks[b][:, j * 128:(j + 1) * 128], a32(A_RR + R_XN, 0, 128, kc * CH + bl * 128, [1, 128]),
                                identf[:, :]), reads=rk(R_XN + kc * 2048, 2048) + ["identf"], writes=[BK(b)])
                        evac_copy(a32(A_RR + R_XS, 0, 128, buf * 1024 + half * 512, [1, 512]), banks[b][:, :],
                                  reads=[BK(b)], writes=xk)
                    o = P.op("sp", lambda e, blk=blk, buf=buf: e.dma_start(
                        out=out_d.ap()[s, blk * 128:(blk + 1) * 128, :], in_=a32(A_RR + R_XS, 0, 128, buf * 1024, [1, 1024])),
                        reads=xk, dma_key=("out", buf))
                    out_ops.append(o)

        ep = 1
        try:
            cut(0)
            for s in range(nseq):
                P.epoch = ep
                ep += 1
                seq_setup(s)
                if s == 0:
                    emit_conv(0, groups=[1, 2, 3, 4, 5], after=[XK(7, 3), XK(3, 3)])
                if s == 0:
                    dbg("X0", a32(A_X, 0, 128, 0, [S, KC], [1, S]), [128, KC, S], F32, [XK(k_, c_) for k_ in range(KC) for c_ in range(NCH)])
                    dbg("CS", a16(A_CS, 0, 128, 0, [1, S]), [128, S], BF16, ["TAB"])
                    dbg("SN", a16(A_SN, 0, 128, 0, [1, S]), [128, S], BF16, ["TAB"])
                    dbg("DM", a16(A_DM, 0, 128, 0, [1, 4096]), [128, 4096], BF16, [("DM", n_) for n_ in range(NB)])
                cut(1)
                LJ = [("A", g_, c_) for g_ in range(2) for c_ in range(NCH)] + [("B", 0, c_) for c_ in range(NCH)] + \
                     [("Y", 0, c_) for c_ in range(NCH)] + [("G", 0, c_) for c_ in range(NCH)]
                alljobs = [(l_,) + j_ for l_ in range(nlayers) for j_ in LJ]
                fns = {}

                def gbase_of(k):
                    l_, kind, g_, c_ = alljobs[k]
                    return (16 + l_ * 8) if kind == "G" else l_ * 8

                nj = len(alljobs)
                norm_A(alljobs[0][3])
                norm_B(alljobs[0][3], gbase_of(0), 0)
                if nj > 1:
                    norm_A(alljobs[1][3])
                cur_l = -1
                for k, (l_, kind, g_, c_) in enumerate(alljobs):
                    if l_ != cur_l:
                        cur_l = l_
                        P.epoch = ep
                        ep += 1
                        fns = layer(s, l_)
                    if k + 1 < nj:
                        norm_B(alljobs[k + 1][3], gbase_of(k + 1), (k + 1) % 2)
                    if k + 2 < nj:
                        norm_A(alljobs[k + 2][3])
                    HTSEL[0] = k % 2
                    if kind == "A":
                        fns["proj_A"](g_, c_)
                        if c_ == NCH - 1:
                            HTSEL[0] = 0
                            fns["attention_A"](g_)
                    elif kind == "B":
                        fns["proj_B"](c_)
                        if c_ == NCH - 1:
                            HTSEL[0] = 0
                            fns["mla_pairs"]()
                    elif kind == "Y":
                        if c_ == 0:
                            fns["prep_ptt"]()
                        fns["proj_YO"](c_)
                    else:
                        fns["proj_G"](c_)
                HTSEL[0] = 0
                final(s)
        except _Cut:
            pass
        print("ops:", len(P.ops), {e: sum(1 for o in P.ops if o.eng == e) for e in ENGS})
        lastdma = {}
        for o in P.ops:
            if o.is_dma:
                lastdma[o.dma_key] = o
        P.emit(final_wait_ops=list(lastdma.values()))
    return nc


_CACHE = {}


def kernel(**inputs):
    x = np.ascontiguousarray(np.asarray(inputs["x"], np.float32))
    p = np.ascontiguousarray(np.asarray(inputs["p"], np.float32))
    pos = np.ascontiguousarray(np.asarray(inputs["positions"], np.int32))
    wsrc0 = prep_weights(inputs)
    wsrcs = []
    for c in range(NCORES):
        w_ = np.zeros((DEPTH, 128, WTOTAL + 16), np.float32)
        w_[:, :, :WTOTAL] = wsrc0
        w_[:, :, WTOTAL:] = float(c)
        wsrcs.append(w_)
    consts = host_consts()
    gcol = np.zeros((128, 64), np.float32)
    g_mix = np.asarray(inputs["g_mix"], np.float32)
    g_ple = np.asarray(inputs["g_ple"], np.float32)
    g_fin = np.asarray(inputs["g_final"], np.float32)
    g_q = np.asarray(inputs["g_q"], np.float32)
    g_kv = np.asarray(inputs["g_kv"], np.float32)
    for l in range(DEPTH):
        gcol[:, l * 8:(l + 1) * 8] = g_mix[l].reshape(8, 128).T
        gcol[:, 16 + l * 8:16 + (l + 1) * 8] = g_ple[l].reshape(8, 128).T
        gcol[:, 40 + l * 2:40 + l * 2 + 2] = g_q[l].reshape(2, 128).T
        gcol[:, 44 + l] = g_kv[l]
    gcol[:, 32:40] = g_fin.reshape(8, 128).T
    sink = np.asarray(inputs["sink"], np.float32).reshape(1, 16)
    if "nc" not in _CACHE:
        _CACHE["nc"] = build_program()
    nc = _CACHE["nc"]
    in_maps = []
    for c in range(NCORES):
        m = {
            "x": x[c * SPC:(c + 1) * SPC],
            "p": np.ascontiguousarray(p[:, c * SPC:(c + 1) * SPC]),
            "pos": pos[c * SPC:(c + 1) * SPC],
            "wsrc": wsrcs[c],
            "gcol": gcol,
            "sink": sink,
            "identf": consts["identf"],
            "maskneg": consts["maskneg"],
            "maskbig": consts["maskbig"],
            "sid": consts["sid"],
            "ropecol": consts["ropecol"],
        }
        in_maps.append(m)
    res = run_bass_kernel_spmd(nc, in_maps, core_ids=list(range(NCORES)))
    out = np.concatenate([np.asarray(r["out"], np.float32) for r in res.results], axis=0)
    return out
```
